# Optimizing a Trainium2 kernel written in Bass

```python
import jax, jax.numpy as jnp
from jax import lax
import numpy as np

D_MODEL = 1024
BATCH = 4
SEQ = 8192
DEPTH = 1

CTX_LEN = 256
GRID_W = 64
D_CONV = 1024
CONV_W = 3
N_Q_HEADS = 8
N_KV_HEADS = 2
HEAD_DIM = 128
GROUP = N_Q_HEADS // N_KV_HEADS
ROPE_THETA = 10000.0
D_FF = 2816
Q_BLOCK = 128
EPS = 1e-6
N_MOD = 9

Q_W = N_Q_HEADS * HEAD_DIM
KV_W = N_KV_HEADS * HEAD_DIM
PIECE_WIDTHS = (D_CONV, D_CONV, D_CONV, Q_W, KV_W, KV_W, 2 * D_MODEL)
D_IN = int(sum(PIECE_WIDTHS))
SPLITS = tuple(int(s) for s in np.cumsum(PIECE_WIDTHS)[:-1])

kernel_name = "hybrid_shortconv_gqa_macaron_dit_layer"


def rms_norm(x, g):
    xf = x.astype(jnp.float32)
    y = xf * lax.rsqrt(jnp.mean(xf * xf, axis=-1, keepdims=True) + EPS)
    return (y * g.astype(jnp.float32)).astype(x.dtype)


def modulate(h, shift, scale):
    return h * (1.0 + scale) + shift


def swiglu(h, w_in, w_out):
    a, b = jnp.split(h @ w_in, 2, axis=-1)
    return (jax.nn.silu(a) * b) @ w_out


def axial_rope_tables(n_tokens):
    rows = n_tokens // GRID_W
    row = jnp.repeat(jnp.arange(rows), GRID_W).astype(jnp.float32)
    col = jnp.tile(jnp.arange(GRID_W), rows).astype(jnp.float32)
    n_freq = HEAD_DIM // 4
    inv = ROPE_THETA ** (-jnp.arange(n_freq, dtype=jnp.float32) / n_freq)
    ang = jnp.stack([row[:, None] * inv, col[:, None] * inv], axis=1)
    return jnp.cos(ang), jnp.sin(ang)


def apply_axial_rope(x, cos, sin):
    xs = x.reshape(x.shape[:3] + (2, 2, HEAD_DIM // 4))
    x1, x2 = xs[..., 0, :], xs[..., 1, :]
    c = cos[None, :, None].astype(x.dtype)
    s = sin[None, :, None].astype(x.dtype)
    out = jnp.stack([x1 * c - x2 * s, x2 * c + x1 * s], axis=-2)
    return out.reshape(x.shape)


def short_conv(u, w):
    L = u.shape[1]
    pad = CONV_W // 2
    up = jnp.pad(u, ((0, 0), (pad, pad), (0, 0)))
    y = up[:, 0:L] * w[0]
    for j in range(1, CONV_W):
        y = y + up[:, j:j + L] * w[j]
    return y


def conv_branch(bg, cg, vc, conv_w):
    return bg * short_conv(cg * vc, conv_w)


def gqa_attend(q, k, v):
    s = jnp.einsum('btkgd,blkd->bkgtl', q, k, preferred_element_type=jnp.float32) * (HEAD_DIM ** -0.5)
    p = jax.nn.softmax(s, axis=-1).astype(v.dtype)
    return jnp.einsum('bkgtl,blkd->btkgd', p, v)


def latent_attention(q_lat, k_all, v_all):
    B, S = q_lat.shape[:2]
    nblk = S // Q_BLOCK
    qb = q_lat.reshape(B, nblk, Q_BLOCK, N_KV_HEADS, GROUP, HEAD_DIM).swapaxes(0, 1)
    ob = lax.map(lambda qi: gqa_attend(qi, k_all, v_all), qb)
    return ob.swapaxes(0, 1).reshape(B, S, Q_W)


def split_heads(q, k, v, q_g, k_g):
    B, L = q.shape[:2]
    q = rms_norm(q.reshape(B, L, N_Q_HEADS, HEAD_DIM), q_g)
    k = rms_norm(k.reshape(B, L, N_KV_HEADS, HEAD_DIM), k_g)
    v = v.reshape(B, L, N_KV_HEADS, HEAD_DIM)
    return q, k, v


def merge_branches(y_conv, o_attn, gate_logits, w_bc, w_ba, w_out):
    g_conv, g_attn = jnp.split(jax.nn.sigmoid(gate_logits), 2, axis=-1)
    return (g_conv * (y_conv @ w_bc) + g_attn * (o_attn @ w_ba)) @ w_out


def setup_inputs(seed: int = 0) -> dict:
    key = jax.random.key(seed)
    ks = jax.random.split(key, 24)

    def nrm(k, shape, scale):
        return jax.random.normal(k, shape, jnp.float32) * scale

    def gain(k, shape):
        return 1.0 + 0.1 * jax.random.normal(k, shape, jnp.float32)

    L = DEPTH
    return {
        "x": nrm(ks[0], (BATCH, SEQ, D_MODEL), 1.0),
        "c": nrm(ks[1], (BATCH, D_MODEL), 1.0),
        "ctx": nrm(ks[2], (BATCH, CTX_LEN, D_MODEL), 1.0),
        "c_ctx": nrm(ks[3], (D_MODEL,), 1.0),
        "w_mod": nrm(ks[4], (L, D_MODEL, N_MOD * D_MODEL), 0.5 * D_MODEL ** -0.5),
        "b_mod": nrm(ks[5], (L, N_MOD * D_MODEL), 0.02),
        "norm1_g": gain(ks[6], (L, D_MODEL)),
        "norm2_g": gain(ks[7], (L, D_MODEL)),
        "norm3_g": gain(ks[8], (L, D_MODEL)),
        "ffn1_w_in": nrm(ks[9], (L, D_MODEL, 2 * D_FF), D_MODEL ** -0.5),
        "ffn1_w_out": nrm(ks[10], (L, D_FF, D_MODEL), D_FF ** -0.5),
        "w_in": nrm(ks[11], (L, D_MODEL, D_IN), D_MODEL ** -0.5),
        "conv_w": nrm(ks[12], (L, CONV_W, D_CONV), CONV_W ** -0.5),
        "q_norm_g": gain(ks[13], (L, HEAD_DIM)),
        "k_norm_g": gain(ks[14], (L, HEAD_DIM)),
        "w_branch_conv": nrm(ks[15], (L, D_CONV, D_MODEL), D_CONV ** -0.5),
        "w_branch_attn": nrm(ks[16], (L, Q_W, D_MODEL), Q_W ** -0.5),
        "w_out": nrm(ks[17], (L, D_MODEL, D_MODEL), D_MODEL ** -0.5),
        "ffn2_w_in": nrm(ks[18], (L, D_MODEL, 2 * D_FF), D_MODEL ** -0.5),
        "ffn2_w_out": nrm(ks[19], (L, D_FF, D_MODEL), D_FF ** -0.5),
        "final_g": gain(ks[20], (D_MODEL,)),
    }


def reference(x, c, ctx, c_ctx, w_mod, b_mod, norm1_g, norm2_g, norm3_g,
              ffn1_w_in, ffn1_w_out, w_in, conv_w, q_norm_g, k_norm_g,
              w_branch_conv, w_branch_attn, w_out, ffn2_w_in, ffn2_w_out, final_g):
    B, S, _ = x.shape
    cos, sin = axial_rope_tables(S)
    cx = ctx
    for layer in range(DEPTH):
        last = layer == DEPTH - 1
        mod_lat = (jax.nn.silu(c) @ w_mod[layer] + b_mod[layer])[:, None, :]
        mod_ctx = (jax.nn.silu(c_ctx) @ w_mod[layer] + b_mod[layer])[None, None, :]
        ml = jnp.split(mod_lat, N_MOD, axis=-1)
        mc = jnp.split(mod_ctx, N_MOD, axis=-1)

        x = x + 0.5 * ml[2] * swiglu(modulate(rms_norm(x, norm1_g[layer]), ml[0], ml[1]),
                                      ffn1_w_in[layer], ffn1_w_out[layer])
        cx = cx + 0.5 * mc[2] * swiglu(modulate(rms_norm(cx, norm1_g[layer]), mc[0], mc[1]),
                                        ffn1_w_in[layer], ffn1_w_out[layer])

        hx = modulate(rms_norm(x, norm2_g[layer]), ml[3], ml[4])
        hc = modulate(rms_norm(cx, norm2_g[layer]), mc[3], mc[4])
        bg_l, cg_l, vc_l, q_l, k_l, v_l, gt_l = jnp.split(hx @ w_in[layer], SPLITS, axis=-1)
        bg_c, cg_c, vc_c, q_c, k_c, v_c, gt_c = jnp.split(hc @ w_in[layer], SPLITS, axis=-1)

        y_conv_l = conv_branch(bg_l, cg_l, vc_l, conv_w[layer])

        q_l, k_l, v_l = split_heads(q_l, k_l, v_l, q_norm_g[layer], k_norm_g[layer])
        q_c, k_c, v_c = split_heads(q_c, k_c, v_c, q_norm_g[layer], k_norm_g[layer])
        q_l = apply_axial_rope(q_l, cos, sin)
        k_l = apply_axial_rope(k_l, cos, sin)
        k_all = jnp.concatenate([k_c, k_l], axis=1)
        v_all = jnp.concatenate([v_c, v_l], axis=1)
        o_l = latent_attention(q_l, k_all, v_all)

        x = x + ml[5] * merge_branches(y_conv_l, o_l, gt_l, w_branch_conv[layer],
                                       w_branch_attn[layer], w_out[layer])

        if not last:
            y_conv_c = conv_branch(bg_c, cg_c, vc_c, conv_w[layer])
            o_c = gqa_attend(q_c.reshape(B, cx.shape[1], N_KV_HEADS, GROUP, HEAD_DIM), k_c, v_c)
            o_c = o_c.reshape(B, cx.shape[1], Q_W)
            cx = cx + mc[5] * merge_branches(y_conv_c, o_c, gt_c, w_branch_conv[layer],
                                             w_branch_attn[layer], w_out[layer])
            cx = cx + 0.5 * mc[8] * swiglu(modulate(rms_norm(cx, norm3_g[layer]), mc[6], mc[7]),
                                            ffn2_w_in[layer], ffn2_w_out[layer])

        x = x + 0.5 * ml[8] * swiglu(modulate(rms_norm(x, norm3_g[layer]), ml[6], ml[7]),
                                      ffn2_w_in[layer], ffn2_w_out[layer])
    return rms_norm(x, final_g)
```

```python
import numpy as np
from contextlib import ExitStack
import concourse.bass as bass
import concourse.mybir as mybir
from concourse.bass_utils import run_bass_kernel_spmd

F32 = mybir.dt.float32
BF16 = mybir.dt.bfloat16
I32 = mybir.dt.int32
AF = mybir.ActivationFunctionType
ALU = mybir.AluOpType
AX = mybir.AxisListType

D = 1024
DFF = 2816
NJ = DFF // 128
EPS = 1e-6
THETA = 10000.0
COMPUTE = ("pe", "act", "dve", "pool")


class Op:
    __slots__ = ("id", "eng", "fn", "deps", "dma", "sem", "val", "needed", "ext", "prewait")


class Prog:
    NRING = {"sp": 8, "act": 2, "pool": 4}

    def __init__(self, nc, es, name):
        self.nc = nc
        self.name = name
        self.ops = []
        self.eng_ops = {e: [] for e in ("pe", "act", "dve", "pool", "sp")}
        self.res = {}
        self.sems = {e: es.enter_context(nc.semaphore(f"{name}_{e}")) for e in COMPUTE}
        self.dsems = {q: [es.enter_context(nc.semaphore(f"{name}_d{q}{i}")) for i in range(self.NRING[q])]
                      for q in ("sp", "act", "pool")}
        self.dcount = {q: 0 for q in ("sp", "act", "pool")}

    def add(self, eng, fn, reads=(), writes=(), dma=False, ext=()):
        op = Op()
        op.id = len(self.ops)
        op.eng = eng
        op.fn = fn
        op.dma = dma
        op.ext = list(ext)
        op.needed = False
        op.sem = None
        op.val = 0
        op.prewait = None
        deps = {}
        for k in reads:
            st = self.res.get(k)
            if st is not None and st[0] is not None:
                deps[st[0]] = "raw"
        for k in writes:
            st = self.res.get(k)
            if st is not None:
                if st[0] is not None and st[0] not in deps:
                    deps[st[0]] = "waw"
                for r in st[1]:
                    if r not in deps:
                        deps[r] = "war"
        deps.pop(op.id, None)
        for k in reads:
            st = self.res.setdefault(k, [None, []])
            st[1].append(op.id)
        for k in writes:
            self.res[k] = [op.id, []]
        keep = []
        for pid, kind in deps.items():
            p = self.ops[pid]
            if (not dma) and (not p.dma) and p.eng == eng:
                if eng == "pe":
                    continue
            keep.append(pid)
            p.needed = True
        op.deps = keep
        if dma:
            q = eng
            i = self.dcount[q]
            self.dcount[q] = i + 1
            nr = self.NRING[q]
            op.sem = self.dsems[q][i % nr]
            op.val = 16 * (i // nr + 1)
            if i >= nr:
                op.prewait = (op.sem, op.val - 16)
        self.ops.append(op)
        self.eng_ops[eng].append(op)
        return op

    def add_raw(self, eng, fn):
        op = Op()
        op.id = len(self.ops)
        op.eng, op.fn, op.dma, op.ext, op.needed, op.sem, op.val, op.prewait, op.deps = eng, fn, "raw", [], False, None, 0, None, []
        self.ops.append(op)
        self.eng_ops[eng].append(op)
        return op

    def emit(self, block):
        for e in COMPUTE:
            ops = [o for o in self.eng_ops[e] if not o.dma]
            if ops:
                ops[-1].needed = True
            c = 0
            for o in ops:
                if o.needed:
                    c += 1
                    o.sem = self.sems[e]
                    o.val = c
        finals = []
        for e in COMPUTE:
            ops = [o for o in self.eng_ops[e] if not o.dma]
            if ops:
                finals.append((self.sems[e], ops[-1].val))
        for q in ("sp", "act", "pool"):
            n = self.dcount[q]
            nr = self.NRING[q]
            for r in range(min(n, nr)):
                cnt = (n - r + nr - 1) // nr
                finals.append((self.dsems[q][r], 16 * cnt))

        def run(eng):
            def f(h):
                waited = {}

                def wait(sem, val):
                    key = id(sem)
                    if waited.get(key, 0) >= val:
                        return
                    h.wait_ge(sem, val)
                    waited[key] = val

                for o in self.eng_ops[eng]:
                    for pid in o.deps:
                        p = self.ops[pid]
                        wait(p.sem, p.val)
                    for (s, v) in o.ext:
                        wait(s, v)
                    if o.prewait is not None:
                        wait(*o.prewait)
                    ins = o.fn(h)
                    if o.dma == "raw":
                        continue
                    if o.dma:
                        ins.then_inc(o.sem, 16)
                    elif o.needed:
                        ins.then_inc(o.sem, 1)
                for (s, v) in finals:
                    wait(s, v)
            return f

        block.tensor(run("pe"))
        block.scalar(run("act"))
        block.vector(run("dve"))
        block.gpsimd(run("pool"))
        block.sync(run("sp"))


class Ring:
    def __init__(self, tile, nslot):
        self.tile = tile
        self.nslot = nslot
        self.i = 0

    def load(self, P, src_ap, ncols=4096, ext=()):
        s = self.i % self.nslot
        self.i += 1
        dst = self.tile[:, s, 0:ncols]
        P.add("sp", lambda h, d=dst, a=src_ap: h.dma_start(out=d, in_=a), reads=(), writes=[("ring", s)],
              dma=True, ext=ext)
        return self.tile[:, s, :], ("ring", s)


def build(cfg):
    NO, NH, NCX = cfg["NO"], cfg["NH"], cfg["NCX"]
    MT = 512
    assert NO % MT == 0 and NH % MT == 0 and NCX % 128 == 0 and NCX <= MT
    MO, MH = NO // MT, NH // MT
    NKEY = NCX + NO + NH
    NKT = NKEY // 128
    assert NKT % 2 == 0
    NSUB = (NO + NH) // 128
    NRING1 = cfg.get("NRING1", 6)
    NRING2 = cfg.get("NRING2", 3)
    NRING3 = cfg.get("NRING3", 6)

    nc = bass.Bass("TRN2", target_bir_lowering=False)

    def din(name, shape, dt=F32):
        return nc.dram_tensor(name, list(shape), dt, kind="ExternalInput").ap()

    def dscr(name, shape, dt):
        return nc.dram_tensor(name, list(shape), dt, kind="Internal").ap()

    x_own = din("x_own", [NO, D])
    x_oth = din("x_oth", [NH, D])
    ctx = din("ctx", [NCX, D])
    cT_in = din("cT", [128, 16])
    bmodT_in = din("bmodT", [128, 72])
    bmod_in = din("bmod", [9 * D])
    gT_in = din("gT", [128, 24])
    convT_in = din("convT", [128, 24])
    fg_in = din("final_g", [D])
    qg_in = din("q_norm_g", [128])
    kg_in = din("k_norm_g", [128])
    meta_in = din("meta", [128, 8])
    w_mod = din("w_mod", [D, 9 * D])
    w1i = din("ffn1_w_in", [D, 2 * DFF])
    w1o = din("ffn1_w_out", [DFF, D])
    w_in = din("w_in", [D, 6656])
    w_bc = din("w_branch_conv", [D, D])
    w_ba = din("w_branch_attn", [D, D])
    w_o = din("w_out", [D, D])
    w3i = din("ffn2_w_in", [D, 2 * DFF])
    w3o = din("ffn2_w_out", [DFF, D])
    y_out = nc.dram_tensor("y", [NO, D], F32, kind="ExternalOutput").ap()

    S_1I = dscr("s_1i", [11, 128, 4096], BF16)
    S_1O = dscr("s_1o", [6, 128, 4096], BF16)
    S_3I = dscr("s_3i", [11, 128, 4096], BF16)
    S_3O = dscr("s_3o", [6, 128, 4096], BF16)
    S_KV = dscr("s_kv", [1, 128, 4096], BF16)
    S_Q = dscr("s_q", [2, 128, 4096], BF16)
    S_C = dscr("s_c", [2, 128, 4096], BF16)
    S_V = dscr("s_v", [2, 128, 4096], BF16)
    S_B = dscr("s_b", [2, 128, 4096], BF16)
    S_G = dscr("s_g", [4, 128, 4096], BF16)
    S_R = dscr("s_r", [4, 128, 4096], BF16)
    S_O = dscr("s_o", [2, 128, 4096], BF16)
    x1_s = dscr("x1_s", [NO, D], F32)
    xm_s = dscr("xm_s", [NO, D], F32)
    h2T_s = dscr("h2T_s", [MO, 128, 4096], BF16)
    rope_s = dscr("rope_s", [NSUB, 128, 128], F32)
    gbc_s = dscr("gbc_s", [5, 128, D], F32)
    kt_s = dscr("kt_s", [128, 2, NKEY], BF16)
    vx_s = dscr("vx_s", [128, NKT, 2, 130], BF16)

    dbg_t = {}
    if cfg.get("dbg"):
        for nm, n in (("ycT", 4096), ("OT", 4096), ("qT", 4096), ("KT", 2 * NKEY), ("Vx", NKT * 2 * 130)):
            dbg_t[nm] = nc.dram_tensor("dbg_" + nm, [128, n], BF16, kind="ExternalOutput").ap()
    es = ExitStack()
    with es:
        def sb(name, shape, dt, stack=es):
            return stack.enter_context(nc.sbuf_tensor("t_" + name, list(shape), dt))

        ps = es.enter_context(nc.psum_tensor("ps", [128, 8, 512], F32))

        def psb(b):
            return ps[:, b, :]

        def psb16(b):
            return ps[:, b, :].bitcast(BF16)

        ident = sb("ident", [128, 128], BF16)
        identf = sb("identf", [128, 128], F32)
        onesf = sb("onesf", [128, 128], F32)
        mhalf = sb("mhalf", [128, 16], F32)
        modc = sb("modc", [128, 9, 8, 2], F32)
        gsc = sb("gsc", [128, 3, 8, 2], F32)
        meta = sb("meta", [128, 8], F32)
        convc = sb("convc", [128, 3, 8], F32)
        qgb = sb("qgb", [128, 128], F32)
        kgb = sb("kgb", [128, 128], F32)
        nbias = sb("nbias", [128, 1], F32)
        halo_all = sb("halo_all", [128, MO + MH, 8, 2], BF16)

        cast_groups = {}

        def cast_group(name):
            s = es.enter_context(nc.semaphore("cg_" + name))
            cast_groups[name] = [s, 0, []]
            return cast_groups[name]

        def castdma(g, out, in_):
            g[2].append((out, in_))
            g[1] += 16

        def in_view(w, c0, n):
            return w.rearrange("(k p) c -> p k c", p=128)[:, :, c0:c0 + n]

        def chunk_view(S, c, off, n):
            return S[c].rearrange("p (k c) -> p k c", k=8)[:, :, off:off + n]

        cg = {}
        for nm, S, w in (("1i", S_1I, w1i), ("3i", S_3I, w3i)):
            g = cast_group(nm)
            cg[nm] = g
            for jj in range(11):
                castdma(g, chunk_view(S, jj, 0, 256), in_view(w, jj * 256, 256))
                castdma(g, chunk_view(S, jj, 256, 256), in_view(w, DFF + jj * 256, 256))
        for nm, S, w in (("1o", S_1O, w1o), ("3o", S_3O, w3o)):
            g = cast_group(nm)
            cg[nm] = g
            wv = w.rearrange("(j p) d -> p j d", p=128)
            for c in range(6):
                nj = 4 if c < 5 else 2
                castdma(g, S[c].rearrange("p (j d) -> p j d", j=4)[:, 0:nj, :], wv[:, 4 * c:4 * c + nj, :])
        g = cast_group("kv")
        cg["kv"] = g
        castdma(g, chunk_view(S_KV, 0, 0, 512), in_view(w_in, 4096, 512))
        for nm, S, c0 in (("q", S_Q, 3072), ("c", S_C, 1024), ("v", S_V, 2048), ("b", S_B, 0)):
            g = cast_group(nm)
            cg[nm] = g
            for c in range(2):
                castdma(g, chunk_view(S, c, 0, 512), in_view(w_in, c0 + c * 512, 512))
        g = cast_group("g")
        cg["g"] = g
        for jj in range(4):
            for t in range(2):
                j = 2 * jj + t
                castdma(g, chunk_view(S_G, jj, t * 256, 128), in_view(w_in, 4608 + j * 128, 128))
                castdma(g, chunk_view(S_G, jj, t * 256 + 128, 128), in_view(w_in, 5632 + j * 128, 128))
        g = cast_group("r")
        cg["r"] = g
        for jj in range(4):
            for t in range(2):
                j = 2 * jj + t
                castdma(g, chunk_view(S_R, jj, t * 256, 128), in_view(w_bc, j * 128, 128))
                castdma(g, chunk_view(S_R, jj, t * 256 + 128, 128), in_view(w_ba, j * 128, 128))
        g = cast_group("o")
        cg["o"] = g
        for c in range(2):
            castdma(g, chunk_view(S_O, c, 0, 512), in_view(w_o, c * 512, 512))
        cast_order = ["1i", "1o", "kv", "q", "c", "v", "b", "g", "r", "o", "3i", "3o"]

        def cext(nm):
            return [(cg[nm][0], cg[nm][1])]

        with ExitStack() as e0:
            P = Prog(nc, es, "p0")
            A = P.add
            cT = sb("cT", [128, 8, 2], F32, e0)
            sc = sb("sc", [128, 8, 2], F32, e0)
            bmodT = sb("bmodT", [128, 72], F32, e0)
            gT = sb("gT", [128, 3, 8], F32, e0)
            wm = sb("wm", [128, 2, 8, 1024], F32, e0)
            screp = sb("screp", [128, 2, 8, 128], F32, e0)
            bbc = sb("bbc", [128, 2, 1024], F32, e0)
            gtmp = sb("gtmp", [128, 2, 1024], F32, e0)
            iot = sb("iot", [128, 128], I32, e0)
            iof = sb("iof", [128, 128], F32, e0)
            inv = sb("inv", [128, 32], F32, e0)
            rowp = sb("rowp", [128, NSUB], F32, e0)
            ang = sb("ang", [128, NSUB, 2, 32], F32, e0)
            ang2 = sb("ang2", [128, NSUB, 2, 2, 32], F32, e0)
            tab = sb("tab", [128, NSUB, 128], F32, e0)
            twopi = sb("twopi", [128, 1], F32, e0)
            negpi = sb("negpi", [128, 1], F32, e0)
            mx = sb("mx", [128, 2], F32, e0)

            def cast_fn(nm, i0, i1):
                def f(h):
                    for (o, i) in cg[nm][2][i0:i1]:
                        h.dma_start(out=o, in_=i).then_inc(cg[nm][0], 16)
                return f
            for nm in ("1i", "1o", "kv"):
                P.add_raw("pool", cast_fn(nm, 0, len(cg[nm][2])))
            late_casts = []
            for nm in cast_order[3:]:
                n = len(cg[nm][2])
                for i0 in range(0, n, 6):
                    late_casts.append(cast_fn(nm, i0, min(n, i0 + 6)))

            A("sp", lambda h: h.dma_start(out=cT[:].rearrange("p k n -> p (k n)"), in_=cT_in), (), ["cT"], dma=True)
            A("sp", lambda h: h.dma_start(out=bmodT[:], in_=bmodT_in), (), ["bmodT"], dma=True)
            A("sp", lambda h: h.dma_start(out=gT[:].rearrange("p a b -> p (a b)"), in_=gT_in), (), ["gT"], dma=True)
            A("sp", lambda h: h.dma_start(out=convc[:].rearrange("p a b -> p (a b)"), in_=convT_in), (), ["convc"], dma=True)
            A("sp", lambda h: h.dma_start(out=meta[:], in_=meta_in), (), ["meta"], dma=True)
            A("sp", lambda h: h.dma_start(out=qgb[:], in_=qg_in.partition_broadcast(128)), (), ["qgb"], dma=True)
            A("sp", lambda h: h.dma_start(out=kgb[:], in_=kg_in.partition_broadcast(128)), (), ["kgb"], dma=True)

            A("pool", lambda h: h.memset(onesf[:], 1.0), (), ["onesf"])
            A("pool", lambda h: h.memset(mhalf[:], -0.5), (), ["mhalf"])
            A("pool", lambda h: h.memset(twopi[:], 2.0 * np.pi), (), ["twopi"])
            A("pool", lambda h: h.memset(negpi[:], -np.pi * 0.999999), (), ["negpi"])
            A("pool", lambda h: h.iota(iot[:], [[1, 128]], channel_multiplier=-1), (), ["iot"])
            A("dve", lambda h: h.tensor_copy(out=iof[:], in_=iot[:]), ["iot"], ["iof"])
            A("dve", lambda h: h.tensor_single_scalar(out=identf[:], in_=iof[:], scalar=0.0, op=ALU.is_equal), ["iof"], ["identf"])
            A("dve", lambda h: h.tensor_copy(out=ident[:], in_=identf[:]), ["identf"], ["ident"])

            A("act", lambda h: h.activation(out=sc[:], in_=cT[:], func=AF.Silu), ["cT"], ["sc"])

            wmv = w_mod.rearrange("(k p) c -> p k c", p=128)
            bc_jobs = {2: [(0, 0, 0.5), (1, 1, 0.5)], 5: [(0, 2, 1.0)], 8: [(0, 3, 0.5)]}
            for m in range(9):
                wb = m % 2
                for hh in range(2):
                    A("sp", lambda h, m=m, wb=wb, hh=hh: h.dma_start(out=wm[:, wb, :, hh * 512:(hh + 1) * 512],
                                                                      in_=wmv[:, :, m * 1024 + hh * 512:m * 1024 + (hh + 1) * 512]),
                      (), [("wm", wb, hh)], dma=True)

                def mm_cols(h, m=m, wb=wb):
                    for cc in range(8):
                        for k in range(8):
                            i = h.matmul(ps[:, 0, cc * 2:cc * 2 + 2], lhsT=wm[:, wb, k, cc * 128:(cc + 1) * 128], rhs=sc[:, k, :],
                                         start=(k == 0), stop=(k == 7))
                    return i
                A("pe", mm_cols, [("wm", wb, 0), ("wm", wb, 1), "sc"], [("ps", 0)])
                A("dve", lambda h, m=m: h.tensor_tensor(out=modc[:, m, :, :], in0=ps[:, 0, 0:16].rearrange("p (c n) -> p c n", n=2),
                                                        in1=bmodT[:, m * 8:(m + 1) * 8].unsqueeze(2).to_broadcast([128, 8, 2]), op=ALU.add),
                  [("ps", 0), "bmodT"], [("modc", m)])
                for (n, slot, fac) in bc_jobs.get(m, []):
                    bb = slot % 2
                    for cc in range(8):
                        A("dve", lambda h, m=m, n=n, cc=cc: h.tensor_scalar(out=screp[:, 0, cc, :], in0=onesf[:], scalar1=modc[:, m, cc, n:n + 1],
                                                                            scalar2=None, op0=ALU.mult), [("modc", m), "onesf"], [("crep", cc)])

                    def mm_bc(h):
                        for cc in range(8):
                            i = h.matmul(ps[:, 2 + cc // 4, (cc % 4) * 128:(cc % 4 + 1) * 128], lhsT=screp[:, 0, cc, :], rhs=identf[:],
                                         start=True, stop=True)
                        return i
                    A("pe", mm_bc, [("crep", cc) for cc in range(8)] + ["identf"], [("ps", 2), ("ps", 3)])
                    A("act", lambda h, bb=bb, fac=fac: h.activation(out=gtmp[:, bb, :], in_=ps[:, 2:4, :].rearrange("p a b -> p (a b)"), func=AF.Copy, scale=float(fac)),
                      [("ps", 2), ("ps", 3)], [("gtmp", bb)])
                    A("pool", lambda h, bb=bb, slot=slot: h.dma_start(out=gbc_s[slot], in_=gtmp[:, bb, :]), [("gtmp", bb)], [("gbc_s", slot)], dma=True)
            for i, msc in enumerate((1, 4, 7)):
                A("dve", lambda h, i=i, msc=msc: h.scalar_tensor_tensor(out=gsc[:, i, :, :], in0=modc[:, msc, :, :], scalar=1.0,
                                                                        in1=gT[:, i, :].unsqueeze(2).to_broadcast([128, 8, 2]),
                                                                        op0=ALU.add, op1=ALU.mult),
                  [("modc", msc), "gT"], [("gsc", i)])

            A("dve", lambda h: h.tensor_reduce(out=mx[:, 0:1], in_=qgb[:], axis=AX.X, op=ALU.max, apply_absolute_value=True), ["qgb"], [("mx", 0)])
            A("dve", lambda h: h.tensor_reduce(out=mx[:, 1:2], in_=kgb[:], axis=AX.X, op=ALU.max, apply_absolute_value=True), ["kgb"], [("mx", 1)])
            A("dve", lambda h: h.scalar_tensor_tensor(out=nbias[:], in0=mx[:, 0:1], scalar=-float(np.sqrt(128.0)), in1=mx[:, 1:2],
                                                      op0=ALU.mult, op1=ALU.mult), [("mx", 0), ("mx", 1)], ["nbias"])
            A("act", lambda h: h.activation(out=qgb[:], in_=qgb[:], func=AF.Copy, scale=float(128.0 ** -0.5)),
              ["qgb", ("mx", 0)], ["qgb"])

            A("pool", lambda h: h.iota(iot[:, 0:32], [[1, 32]], channel_multiplier=0), ["iof"], ["iot"])
            A("dve", lambda h: h.tensor_copy(out=iof[:, 0:32], in_=iot[:, 0:32]), ["iot", "identf"], ["iof"])
            A("act", lambda h: h.activation(out=inv[:], in_=iof[:, 0:32], func=AF.Exp, scale=-float(np.log(THETA) / 32.0)), ["iof"], ["inv"])
            A("pool", lambda h: h.iota(iot[:, 32:32 + NSUB], [[2, NSUB]], channel_multiplier=0), (), [("iot2")])
            A("dve", lambda h: h.tensor_copy(out=rowp[:], in_=iot[:, 32:32 + NSUB]), ["iot2"], ["rowp"])
            no_s = NO // 128
            A("dve", lambda h: h.tensor_scalar(out=rowp[:, 0:no_s], in0=rowp[:, 0:no_s], scalar1=meta[:, 0:1], scalar2=None, op0=ALU.add),
              ["rowp", "meta"], ["rowp"])
            A("dve", lambda h: h.tensor_scalar(out=rowp[:, no_s:NSUB], in0=rowp[:, no_s:NSUB], scalar1=meta[:, 1:2], scalar2=float(-2 * no_s),
                                               op0=ALU.add, op1=ALU.add), ["rowp", "meta"], ["rowp"])
            A("dve", lambda h: h.tensor_tensor(out=ang[:, :, 0, :], in0=rowp[:].unsqueeze(2).to_broadcast([128, NSUB, 32]),
                                               in1=inv[:].unsqueeze(1).to_broadcast([128, NSUB, 32]), op=ALU.mult), ["rowp", "inv"], [("ang", 0)])
            A("dve", lambda h: h.tensor_scalar(out=ang[:, :, 1, :], in0=inv[:].unsqueeze(1).to_broadcast([128, NSUB, 32]), scalar1=meta[:, 2:3],
                                               scalar2=None, op0=ALU.mult), ["inv", "meta"], [("ang", 1)])
            A("dve", lambda h: h.tensor_scalar(out=ang2[:, :, 0, :, :], in0=ang[:], scalar1=float(0.5 * np.pi), scalar2=None, op0=ALU.add),
              [("ang", 0), ("ang", 1)], [("ang2", 0)])
            A("dve", lambda h: h.tensor_copy(out=ang2[:, :, 1, :, :], in_=ang[:]), [("ang", 0), ("ang", 1)], [("ang2", 1)])
            a2f = ang2[:].rearrange("p s c a f -> p (s c a f)")
            NA = NSUB * 128
            WM0 = [("wm", 0, 0), ("wm", 0, 1)]
            WM1 = [("wm", 1, 0), ("wm", 1, 1)]
            ki = wm[:, 0].rearrange("p k c -> p (k c)").bitcast(I32)[:, 0:NA]
            kf = wm[:, 1].rearrange("p k c -> p (k c)")[:, 0:NA]
            A2 = [("ang2", 0), ("ang2", 1)]
            A("dve", lambda h: h.tensor_scalar(out=kf, in0=a2f, scalar1=float(1.0 / (2.0 * np.pi)), scalar2=None, op0=ALU.mult), A2, WM1)
            A("dve", lambda h: h.tensor_copy(out=ki, in_=kf), WM1, WM0)
            A("dve", lambda h: h.tensor_copy(out=kf, in_=ki), WM0, WM1)
            A("dve", lambda h: h.scalar_tensor_tensor(out=a2f, in0=kf, scalar=float(-2.0 * np.pi), in1=a2f, op0=ALU.mult, op1=ALU.add), WM1 + A2, A2)
            A("dve", lambda h: h.tensor_single_scalar(out=kf, in_=a2f, scalar=float(np.pi), op=ALU.is_gt), A2, WM1)
            A("dve", lambda h: h.scalar_tensor_tensor(out=a2f, in0=kf, scalar=float(-2.0 * np.pi), in1=a2f, op0=ALU.mult, op1=ALU.add), WM1 + A2, A2)
            A("act", lambda h: h.activation(out=tab[:].rearrange("p s c -> p (s c)"), in_=a2f, func=AF.Sin, scale=0.999999), A2, ["tab"])
            A("pool", lambda h: h.dma_start(out=rope_s.rearrange("s p c -> p s c"), in_=tab[:]), ["tab"], ["rope_s"], dma=True)

            with nc.Block() as block:
                P.emit(block)
        build_phase1p(locals())
        with ExitStack() as ekv:
            KT = sb("KT", [128, 2, NKEY], BF16, ekv)
            Vx = sb("Vx", [128, NKT, 2, 130], BF16, ekv)
            build_phase2(locals())
        build_phase3p(locals())
    return nc


def norm_to_hT(L, P, NT, xres, n_i, kind, dst, dkey, xn, junk, ss, tt, rstd):
    A = P.add
    gsc, modc, mhalf, ident, ps = L["gsc"], L["modc"], L["mhalf"], L["ident"], L["ps"]
    sh_m = (0, 3, 6)[n_i]
    T = NT * 128
    for s in range(NT):
        A("act", lambda h, s=s: h.activation(out=xn[:, s, :], in_=xres[:, s, :], func=AF.Square, accum_out=ss[:, s:s + 1]),
          [("xres", s)], [("ss", s), ("xn", s)])
    A("dve", lambda h: h.tensor_scalar(out=tt[:, 0:NT], in0=ss[:, 0:NT], scalar1=1.0 / D, scalar2=EPS, op0=ALU.mult, op1=ALU.add),
      [("ss", s) for s in range(NT)], ["tt"])
    A("pool", lambda h: h.tensor_tensor(out=rstd[:, 0:NT], in0=tt[:, 0:NT], in1=mhalf[:, 0:NT], op=ALU.pow), ["tt"], ["rstd"])
    for s in range(NT):
        A("dve", lambda h, s=s: h.tensor_scalar(out=xn[:, s, :], in0=xres[:, s, :], scalar1=rstd[:, s:s + 1], scalar2=None, op0=ALU.mult),
          [("xres", s), "rstd"], [("xn", s)])
    for s in range(NT):
        def tr(h, s=s):
            for k in range(8):
                o = ps[:, k // 2, :].bitcast(BF16)[:, (k % 2) * 512 + s * 128:(k % 2) * 512 + (s + 1) * 128]
                i = h.transpose(out=o, in_=xn[:, s, k * 128:(k + 1) * 128], identity=ident[:])
            return i
        A("pe", tr, [("xn", s)], [("ps", b) for b in range(4)])
    for k in range(8):
        A("act", lambda h, k=k: h.activation(out=dst[:, k, 0:T], in_=ps[:, k // 2, :].bitcast(BF16)[:, (k % 2) * 512:(k % 2) * 512 + T],
                                             func=AF.Identity, bias=modc[:, sh_m, k, kind:kind + 1], scale=gsc[:, n_i, k, kind:kind + 1]),
          [("ps", k // 2)], [(dkey, k)])


def ffn_core(L, P, NT, ring, S_I, S_O, ext_i, ext_o, hT, actT, sa, tmp, xres, gate_bc):
    A = P.add
    ps = L["ps"]
    T = NT * 128
    for jj in range(11):
        slot, skey = ring.load(P, S_I[jj], ext=ext_i)
        sv = slot.rearrange("p (k c) -> p k c", k=8)
        base = (jj % 2) * 4
        for m4 in range(4):
            def mm(h, m4=m4, sv=sv, base=base):
                for k in range(8):
                    i = h.matmul(ps[:, base + m4, 0:T], lhsT=sv[:, k, m4 * 128:(m4 + 1) * 128], rhs=hT[:, k, 0:T], start=(k == 0), stop=(k == 7))
                return i
            A("pe", mm, [skey] + [("hT", k) for k in range(8)], [("ps", base + m4)])
        for t in range(2):
            j = 2 * jj + t
            A("act", lambda h, t=t, base=base: h.activation(out=sa[:, t, 0:T], in_=ps[:, base + t, 0:T], func=AF.Silu),
              [("ps", base + t)], [("sa", t)])
            A("dve", lambda h, t=t, base=base, j=j: h.tensor_tensor(out=actT[:, j, 0:T], in0=sa[:, t, 0:T], in1=ps[:, base + 2 + t, 0:T], op=ALU.mult),
              [("sa", t), ("ps", base + 2 + t)], [("actT", j)])
    for c in range(6):
        nj = 4 if c < 5 else 2
        slot, skey = ring.load(P, S_O[c][:, 0:nj * 1024], ncols=nj * 1024, ext=ext_o)
        sv = slot.rearrange("p (j d) -> p j d", j=4)
        for jj in range(nj):
            j = 4 * c + jj

            def mm(h, j=j, jj=jj, sv=sv):
                for s in range(NT):
                    for hh in range(2):
                        i = h.matmul(ps[:, 2 * s + hh, :], lhsT=actT[:, j, s * 128:(s + 1) * 128], rhs=sv[:, jj, hh * 512:(hh + 1) * 512],
                                     start=(j == 0), stop=(j == NJ - 1))
                return i
            A("pe", mm, [skey, ("actT", j)], [("ps", b) for b in range(2 * NT)])
    for s in range(NT):
        for hh in range(2):
            b = 2 * s + hh
            A("dve", lambda h, b=b, hh=hh: h.tensor_tensor(out=tmp[:, b % 2, :], in0=ps[:, b, :], in1=gate_bc[:, hh * 512:(hh + 1) * 512], op=ALU.mult),
              [("ps", b), "gate"], [("tmp", b % 2)])
            A("pool", lambda h, b=b, s=s, hh=hh: h.tensor_tensor(out=xres[:, s, hh * 512:(hh + 1) * 512], in0=tmp[:, b % 2, :],
                                                                 in1=xres[:, s, hh * 512:(hh + 1) * 512], op=ALU.add),
              [("tmp", b % 2), ("xres", s)], [("xres", s)])


def headnorm_rope(L, P, src, skeys, H, gbc, tabv, dst, dkeys, W):
    A = P.add
    mhalf = L["mhalf"]
    sqj, ssq, tq, rq, qn, t1, t2, t3, t4 = W
    A("act", lambda h: h.activation(out=sqj[:, 0:H, :], in_=src, func=AF.Square), skeys, ["sqj"])
    A("dve", lambda h: h.tensor_reduce(out=ssq[:, 0:H], in_=sqj[:, 0:H, :], axis=AX.X, op=ALU.add), ["sqj"], ["ssq"])
    A("dve", lambda h: h.tensor_scalar(out=tq[:, 0:H], in0=ssq[:, 0:H], scalar1=1.0 / 128.0, scalar2=EPS, op0=ALU.mult, op1=ALU.add), ["ssq"], ["tq"])
    A("pool", lambda h: h.tensor_tensor(out=rq[:, 0:H], in0=tq[:, 0:H], in1=mhalf[:, 0:H], op=ALU.pow), ["tq"], ["rq"])
    A("dve", lambda h: h.tensor_tensor(out=qn[:, 0:H, :], in0=src, in1=rq[:, 0:H].unsqueeze(2).to_broadcast([128, H, 128]), op=ALU.mult),
      list(skeys) + ["rq"], ["qn"])
    if tabv is None:
        A("pool", lambda h: h.tensor_tensor(out=dst, in0=qn[:, 0:H, :], in1=gbc[:].unsqueeze(1).to_broadcast([128, H, 128]), op=ALU.mult),
          ["qn"], dkeys)
        return
    A("pool", lambda h: h.tensor_tensor(out=qn[:, 0:H, :], in0=qn[:, 0:H, :], in1=gbc[:].unsqueeze(1).to_broadcast([128, H, 128]), op=ALU.mult),
      ["qn"], ["qn"])
    q5 = qn[:, 0:H, :].rearrange("p h (a t f) -> p h a t f", a=2, t=2, f=32)
    d5 = dst.rearrange("p h (a t f) -> p h a t f", a=2, t=2, f=32)
    x1, x2 = q5[:, :, :, 0, :], q5[:, :, :, 1, :]
    cb = tabv[:, 0:64].rearrange("p (a f) -> p a f", a=2).unsqueeze(1).to_broadcast([128, H, 2, 32])
    sbb = tabv[:, 64:128].rearrange("p (a f) -> p a f", a=2).unsqueeze(1).to_broadcast([128, H, 2, 32])
    A("dve", lambda h: h.tensor_tensor(out=t1[:, 0:H], in0=x1, in1=cb, op=ALU.mult), ["qn", "tab"], ["t1"])
    A("pool", lambda h: h.tensor_tensor(out=t2[:, 0:H], in0=x2, in1=sbb, op=ALU.mult), ["qn", "tab"], ["t2"])
    A("dve", lambda h: h.tensor_tensor(out=d5[:, :, :, 0, :], in0=t1[:, 0:H], in1=t2[:, 0:H], op=ALU.subtract), ["t1", "t2"], dkeys)
    A("pool", lambda h: h.tensor_tensor(out=t3[:, 0:H], in0=x2, in1=cb, op=ALU.mult), ["qn", "tab"], ["t3"])
    A("dve", lambda h: h.tensor_tensor(out=t4[:, 0:H], in0=x1, in1=sbb, op=ALU.mult), ["qn", "tab"], ["t4"])
    A("pool", lambda h: h.tensor_tensor(out=d5[:, :, :, 1, :], in0=t3[:, 0:H], in1=t4[:, 0:H], op=ALU.add), ["t3", "t4"], dkeys)


def hn_work(nc, stack, H, pfx=""):
    def sb(name, shape, dt):
        return stack.enter_context(nc.sbuf_tensor("t_" + pfx + name, list(shape), dt))
    return (sb("sqj", [128, H, 128], F32), sb("ssq", [128, 8], F32), sb("tq", [128, 8], F32), sb("rq", [128, 8], F32),
            sb("qn", [128, H, 128], F32), sb("t1", [128, H, 2, 32], F32), sb("t2", [128, H, 2, 32], F32),
            sb("t3", [128, H, 2, 32], F32), sb("t4", [128, H, 2, 32], F32))


def build_phase1(L):
    nc, ps = L["nc"], L["ps"]
    NO, NH, NCX, MO, MH, MT, NKT = L["NO"], L["NH"], L["NCX"], L["MO"], L["MH"], L["MT"], L["NKT"]
    KT, Vx, ident, halo_all, meta = L["KT"], L["Vx"], L["ident"], L["halo_all"], L["meta"]
    with ExitStack() as e1:
        def sb(name, shape, dt):
            return e1.enter_context(nc.sbuf_tensor("t_" + name, list(shape), dt))
        P = Prog(nc, L["es"], "p1")
        A = P.add
        ringt = sb("ring1", [128, L["NRING1"], 4096], BF16)
        ring = Ring(ringt, L["NRING1"])
        xres = sb("xres", [128, 4, D], F32)
        xn = sb("xn", [128, 4, D], BF16)
        junk = sb("junk", [128, D], BF16)
        hT = sb("hT", [128, 8, MT], BF16)
        actT = sb("actT", [128, NJ, MT], BF16)
        sa = sb("sa", [128, 2, MT], F32)
        tmp = sb("tmp", [128, 2, MT], F32)
        ss = sb("ss", [128, 4], F32)
        tt = sb("tt", [128, 4], F32)
        rstd = sb("rstd", [128, 4], F32)
        wkv = sb("wkv", [128, 8, 512], BF16)
        g2 = sb("g2", [128, 2, D], F32)
        tabt = sb("tabt", [128, 4, 128], F32)
        kb = sb("kb", [128, 2, 128], BF16)
        W = hn_work(nc, e1, 2, "k_")

        A("sp", lambda h: h.dma_start(out=g2[:], in_=L["gbc_s"][0:2].rearrange("a p d -> p a d")), (), ["gate"], dma=True)
        A("sp", lambda h: h.dma_start(out=wkv[:].rearrange("p k c -> p (k c)"), in_=L["S_KV"][0]), (), ["wkv"], dma=True, ext=L["cext"]("kv"))
        A("pool", lambda h: h.memset(Vx[:, :, :, 128:130], 1.0), (), ["Vones"])

        tiles = [("ctx", 0, NCX // 128)] + [("own", m, 4) for m in range(MO)] + [("oth", m, 4) for m in range(MH)]
        late = list(L["late_casts"])
        per_tile = (len(late) + max(1, len(tiles) - 2) - 1) // max(1, len(tiles) - 2)
        for ti, (kind, m, NT) in enumerate(tiles):
            for _ in range(per_tile if ti < len(tiles) - 1 else len(late)):
                if late:
                    P.add_raw("pool", late.pop(0))
            T = NT * 128
            kd = 1 if kind == "ctx" else 0
            if kind == "ctx":
                src, kt0, sub0 = L["ctx"], 0, None
            elif kind == "own":
                src, kt0, sub0 = L["x_own"][m * MT:(m + 1) * MT, :], (NCX + m * MT) // 128, m * 4
            else:
                src, kt0, sub0 = L["x_oth"][m * MT:(m + 1) * MT, :], (NCX + NO + m * MT) // 128, (NO + m * MT) // 128
            A("act", lambda h, src=src, NT=NT: h.dma_start(out=xres[:, 0:NT, :], in_=src.rearrange("(s p) d -> p s d", p=128)),
              (), [("xres", s) for s in range(NT)], dma=True)
            if sub0 is not None:
                A("sp", lambda h, sub0=sub0: h.dma_start(out=tabt[:], in_=L["rope_s"][sub0:sub0 + 4].rearrange("s p c -> p s c")),
                  (), ["tab"], dma=True)
            norm_to_hT(L, P, NT, xres, 0, kd, hT, "hT", xn, junk, ss, tt, rstd)
            ffn_core(L, P, NT, ring, L["S_1I"], L["S_1O"], L["cext"]("1i"), L["cext"]("1o"), hT, actT, sa, tmp, xres, g2[:, kd, :])
            norm_to_hT(L, P, NT, xres, 1, kd, hT, "hT", xn, junk, ss, tt, rstd)
            if kind == "own":
                A("pool", lambda h, m=m: h.dma_start(out=L["x1_s"][m * MT:(m + 1) * MT, :].rearrange("(s p) d -> p s d", p=128), in_=xres[:]),
                  [("xres", s) for s in range(4)], [("x1_s", m)], dma=True)
                A("pool", lambda h, m=m: h.dma_start(out=L["h2T_s"][m], in_=hT[:].rearrange("p k t -> p (k t)")),
                  [("hT", k) for k in range(8)], [("h2T_s", m)], dma=True)
            if kind != "ctx":
                hi = m if kind == "own" else MO + m
                A("pool", lambda h, hi=hi: h.tensor_copy(out=halo_all[:, hi, :, 0], in_=hT[:, :, 0]), [("hT", k) for k in range(8)], ["halo"])
                A("pool", lambda h, hi=hi: h.tensor_copy(out=halo_all[:, hi, :, 1], in_=hT[:, :, MT - 1]), [("hT", k) for k in range(8)], ["halo"])
            for s in range(NT):
                bank = 4 + (s % 2)
                kt = kt0 + s

                def mmkv(h, s=s, bank=bank):
                    for k in range(8):
                        i = h.matmul(ps[:, bank, :], lhsT=hT[:, k, s * 128:(s + 1) * 128], rhs=wkv[:, k, :], start=(k == 0), stop=(k == 7))
                    return i
                A("pe", mmkv, [("hT", k) for k in range(8)] + ["wkv"], [("ps", bank)])
                A("act", lambda h, bank=bank, kt=kt: h.activation(out=Vx[:, kt, :, 0:128], in_=ps[:, bank, 256:512].rearrange("p (g d) -> p g d", g=2),
                                                                  func=AF.Copy), [("ps", bank)], [("V", kt)])
                headnorm_rope(L, P, ps[:, bank, 0:256].rearrange("p (g d) -> p g d", g=2), [("ps", bank)], 2, L["kgb"],
                              None if kind == "ctx" else tabt[:, s, :], kb[:], ["kb"], W)

                def trk(h):
                    for g in range(2):
                        i = h.transpose(out=ps[:, 6, :].bitcast(BF16)[:, g * 128:(g + 1) * 128], in_=kb[:, g, :], identity=ident[:])
                    return i
                A("pe", trk, ["kb"], [("ps", 6)])
                A("dve", lambda h, kt=kt: h.tensor_copy(out=KT[:, :, kt * 128:(kt + 1) * 128],
                                                        in_=ps[:, 6, :].bitcast(BF16)[:, 0:256].rearrange("p (g t) -> p g t", g=2)),
                  [("ps", 6)], [("KT", kt)])
        with nc.Block() as block:
            P.emit(block)


def build_phase2(L):
    nc, ps = L["nc"], L["ps"]
    NO, NH, NCX, MO, MH, MT, NKT = L["NO"], L["NH"], L["NCX"], L["MO"], L["MH"], L["MT"], L["NKT"]
    KT, Vx, ident, halo_all, meta, convc, nbias = L["KT"], L["Vx"], L["ident"], L["halo_all"], L["meta"], L["convc"], L["nbias"]
    cext = L["cext"]
    with ExitStack() as e2:
        def sb(name, shape, dt):
            return e2.enter_context(nc.sbuf_tensor("t_" + name, list(shape), dt))
        P = Prog(nc, L["es"], "p2")
        A = P.add
        ringt = sb("ring2", [128, L["NRING2"], 4096], BF16)
        ring = Ring(ringt, L["NRING2"])
        h2T = sb("h2T", [128, 8, MT], BF16)
        hal = sb("hal", [128, 8, 2], BF16)
        bufA = sb("bufA", [128, 4, D], F32)
        ycT = bufA[:, 0:2, :].bitcast(BF16).rearrange("p a (k t) -> p (a k) t", t=MT) if False else None
        bA16 = bufA[:].rearrange("p a d -> p (a d)").bitcast(BF16)
        ycT = bA16[:, 0:4096].rearrange("p (k t) -> p k t", k=8)
        OT = bA16[:, 4096:8192].rearrange("p (k t) -> p k t", k=8)
        xres = bufA
        qT = sb("qT", [128, 8, MT], BF16)
        mT = qT
        qb = sb("qb", [128, 4, 8, 128], BF16)
        wq = sb("wq", [128, 2, 4096], BF16)
        bsb = sb("bsb", [128, 2, MT], F32)
        tabt = sb("tabt2", [128, 4, 128], F32)
        W = hn_work(nc, e2, 8, "q_")
        csb = sb("csb", [128, 2, MT], F32)
        u = sb("u", [128, 2, MT + 2], F32)
        hps = sb("hps", [128, 2, 4], F32)
        PT = sb("PT", [128, 3, 1024], BF16)
        rden = sb("rden", [128, 4, 1], F32)
        on = sb("on", [128, 4, 128], BF16)
        sg = sb("sg", [128, 4, MT], F32)
        tm = sb("tm", [128, 4, MT], F32)
        cv = tm
        tmp = tm
        g5 = sb("g5", [128, D], F32)

        A("sp", lambda h: h.dma_start(out=g5[:], in_=L["gbc_s"][2]), (), ["gate"], dma=True)
        A("sp", lambda h: h.dma_start(out=wq[:], in_=L["S_Q"].rearrange("c p n -> p c n")), (), ["wq"], dma=True, ext=cext("q"))
        for g_ in range(2):
            A("sp", lambda h, g_=g_: h.dma_start(out=KT[:, g_, :], in_=L["kt_s"][:, g_, :]), (), [("KTl", g_)], dma=True)
        nq = 4
        for q_ in range(nq):
            a_, b_ = (NKT * q_) // nq, (NKT * (q_ + 1)) // nq
            A("sp", lambda h, a_=a_, b_=b_: h.dma_start(out=Vx[:, a_:b_], in_=L["vx_s"][:, a_:b_]), (), [("Vxl", q_)], dma=True)
        KVL = [("KTl", 0), ("KTl", 1)] + [("Vxl", q_) for q_ in range(nq)]
        BUFA = [("bufA", i) for i in range(4)]

        for m in range(MO):
            A("act", lambda h, m=m: h.dma_start(out=h2T[:].rearrange("p k t -> p (k t)"), in_=L["h2T_s"][m]), (), [("h2T", k) for k in range(8)], dma=True)
            A("sp", lambda h, m=m: h.dma_start(out=tabt[:], in_=L["rope_s"][m * 4:m * 4 + 4].rearrange("s p c -> p s c")), (), ["tab"], dma=True)
            li = (m - 1) if m > 0 else (MO + MH - 1)
            ri = (m + 1) if m < MO - 1 else MO
            if m == 0:
                A("dve", lambda h, li=li: h.tensor_scalar(out=hal[:, :, 0], in0=halo_all[:, li, :, 1], scalar1=meta[:, 3:4], scalar2=None, op0=ALU.mult), (), [("hal", 0)])
            else:
                A("dve", lambda h, li=li: h.tensor_copy(out=hal[:, :, 0], in_=halo_all[:, li, :, 1]), (), [("hal", 0)])
            if m == MO - 1:
                A("dve", lambda h, ri=ri: h.tensor_scalar(out=hal[:, :, 1], in0=halo_all[:, ri, :, 0], scalar1=meta[:, 4:5], scalar2=None, op0=ALU.mult), (), [("hal", 1)])
            else:
                A("dve", lambda h, ri=ri: h.tensor_copy(out=hal[:, :, 1], in_=halo_all[:, ri, :, 0]), (), [("hal", 1)])
            H2K = [("h2T", k) for k in range(8)]

            qv = [wq[:, 0].rearrange("p (k c) -> p k c", k=8), wq[:, 1].rearrange("p (k c) -> p k c", k=8)]

            def emit_mmq(s):
                b0 = 2 * (s % 2)

                def mmq(h, s=s, b0=b0):
                    for k in range(8):
                        for hh in range(2):
                            i = h.matmul(ps[:, b0 + hh, :], lhsT=h2T[:, k, s * 128:(s + 1) * 128], rhs=qv[hh][:, k, :], start=(k == 0), stop=(k == 7))
                    return i
                A("pe", mmq, H2K + ["wq"], [("ps", b0), ("ps", b0 + 1)])
                headnorm_rope(L, P, ps[:, b0:b0 + 2, :].rearrange("p a (h d) -> p (a h) d", d=128), [("ps", b0), ("ps", b0 + 1)], 8, L["qgb"],
                              tabt[:, s, :], qb[:, s], [("qb", s)], W)

            def emit_trq(s):
                tb = (0, 2, 1, 3)[s]

                def trq(h, s=s, tb=tb):
                    for hd in range(8):
                        i = h.transpose(out=ps[:, tb, :].bitcast(BF16)[:, hd * 128:(hd + 1) * 128], in_=qb[:, s, hd, :], identity=ident[:])
                    return i
                A("pe", trq, [("qb", s)], [("ps", tb)])
                A("dve", lambda h, s=s, tb=tb: h.tensor_copy(out=qT[:, :, s * 128:(s + 1) * 128], in_=ps[:, tb, :].bitcast(BF16).rearrange("p (h t) -> p h t", h=8)),
                  [("ps", tb)], [("qT", s)])

            conv_w = {}

            def emit_conv(j):
                half, jj = j // 4, j % 4
                if jj == 0:
                    sc_, kc = ring.load(P, L["S_C"][half], ext=cext("c"))
                    sv_, kv = ring.load(P, L["S_V"][half], ext=cext("v"))
                    sb_, kb_ = ring.load(P, L["S_B"][half], ext=cext("b"))
                    conv_w["v"] = tuple(t.rearrange("p (k c) -> p k c", k=8) for t in (sc_, sv_, sb_))
                    conv_w["k"] = (kc, kv, kb_)
                scv, svv, sbv = conv_w["v"]
                kc, kv, kb_ = conv_w["k"]
                par = j % 2
                bC, bV, bB, bH = 4, 5, 6, 7

                def mmc(h, jj=jj, scv=scv, svv=svv):
                    for k in range(8):
                        w = scv[:, k, jj * 128:(jj + 1) * 128]
                        h.matmul(ps[:, bC, :], lhsT=w, rhs=h2T[:, k, :], start=(k == 0), stop=(k == 7))
                        h.matmul(ps[:, bH, 0:2], lhsT=w, rhs=hal[:, k, :], start=(k == 0), stop=(k == 7))
                    for k in range(8):
                        w = svv[:, k, jj * 128:(jj + 1) * 128]
                        h.matmul(ps[:, bV, :], lhsT=w, rhs=h2T[:, k, :], start=(k == 0), stop=(k == 7))
                        i = h.matmul(ps[:, bH, 2:4], lhsT=w, rhs=hal[:, k, :], start=False, stop=(k == 7), skip_group_check=True)
                    return i
                A("pe", mmc, H2K + [kc, kv, ("hal", 0), ("hal", 1)], [("ps", bC), ("ps", bV), ("ps", bH)])

                def mmb(h, jj=jj, sbv=sbv):
                    for k in range(8):
                        i = h.matmul(ps[:, bB, :], lhsT=sbv[:, k, jj * 128:(jj + 1) * 128], rhs=h2T[:, k, :], start=(k == 0), stop=(k == 7))
                    return i
                A("pe", mmb, H2K + [kb_], [("ps", bB)])
                A("act", lambda h, par=par: h.activation(out=csb[:, par, :], in_=ps[:, bC, :], func=AF.Copy), [("ps", bC)], [("csb", par)])
                A("act", lambda h, par=par: h.activation(out=hps[:, par, 0:2], in_=ps[:, bH, 0:2], func=AF.Copy), [("ps", bH)], [("hps", par)])
                A("act", lambda h, par=par: h.activation(out=bsb[:, par, :], in_=ps[:, bB, :], func=AF.Copy), [("ps", bB)], [("bsb", par)])
                A("dve", lambda h, par=par: h.tensor_tensor(out=u[:, par, 1:MT + 1], in0=csb[:, par, :], in1=ps[:, bV, :], op=ALU.mult),
                  [("csb", par), ("ps", bV)], [("u", par)])
                A("dve", lambda h, par=par: h.tensor_tensor(out=u[:, par, 0:MT + 2:MT + 1], in0=hps[:, par, 0:2], in1=ps[:, bH, 2:4], op=ALU.mult),
                  [("hps", par), ("ps", bH)], [("u", par)])
                A("act", lambda h, par=par, j=j: h.activation(out=cv[:, par, :], in_=u[:, par, 0:MT], func=AF.Copy, scale=convc[:, 0, j:j + 1]),
                  [("u", par)], [("tm", par)])
                A("dve", lambda h, par=par, j=j: h.scalar_tensor_tensor(out=cv[:, par, :], in0=u[:, par, 1:MT + 1], scalar=convc[:, 1, j:j + 1], in1=cv[:, par, :],
                                                                        op0=ALU.mult, op1=ALU.add), [("u", par), ("tm", par)], [("tm", par)])
                A("dve", lambda h, par=par, j=j: h.scalar_tensor_tensor(out=cv[:, par, :], in0=u[:, par, 2:MT + 2], scalar=convc[:, 2, j:j + 1], in1=cv[:, par, :],
                                                                        op0=ALU.mult, op1=ALU.add), [("u", par), ("tm", par)], [("tm", par)])
                A("pool", lambda h, par=par, j=j: h.tensor_tensor(out=ycT[:, j, :], in0=cv[:, par, :], in1=bsb[:, par, :], op=ALU.mult),
                  [("tm", par), ("bsb", par)], BUFA[0:2])

            emit_mmq(0)
            emit_mmq(1)
            emit_conv(0)
            emit_mmq(2)
            emit_conv(1)
            emit_mmq(3)
            emit_conv(2)
            emit_trq(0)
            emit_conv(3)
            emit_trq(1)
            emit_conv(4)
            emit_conv(5)
            emit_trq(2)
            emit_conv(6)
            emit_trq(3)
            emit_conv(7)

            npair = NKT // 2
            its = [(s, g) for s in range(4) for g in range(2)]
            jobs = [(it, pi) for it in range(len(its)) for pi in range(npair)]

            def qk(gp):
                it, pi = jobs[gp]
                s, g = its[it]
                buf = gp % 2
                qmov = qT[:, g * 4:(g + 1) * 4, s * 128:(s + 1) * 128]

                def f(h, pi=pi, buf=buf, g=g, qmov=qmov):
                    for kk in range(2):
                        kt = 2 * pi + kk
                        i = h.matmul(ps[:, 2 * buf + kk, :], lhsT=KT[:, g, kt * 128:(kt + 1) * 128], rhs=qmov, start=True, stop=True)
                    return i
                A("pe", f, [("qT", s)] + KVL, [("ps", 2 * buf), ("ps", 2 * buf + 1)])
                A("act", lambda h, gp=gp, buf=buf: h.activation(out=PT[:, gp % 3, :], in_=ps[:, 2 * buf:2 * buf + 2, :].rearrange("p a b -> p (a b)"),
                                                                func=AF.Exp, bias=nbias[:, 0:1], scale=1.0),
                  [("ps", 2 * buf), ("ps", 2 * buf + 1)], [("PT", gp % 3)])

            def pv(gp):
                it, pi = jobs[gp]
                s, g = its[it]

                def f(h, pi=pi, g=g, gp=gp):
                    for kk in range(2):
                        kt = 2 * pi + kk
                        for hd in range(4):
                            i = h.matmul(ps[:, 4 + hd, 0:129], lhsT=PT[:, gp % 3, kk * 512 + hd * 128:kk * 512 + (hd + 1) * 128],
                                         rhs=Vx[:, kt, g, 0:129], start=(kt == 0), stop=(kt == NKT - 1))
                    return i
                A("pe", f, [("PT", gp % 3)], [("ps", 4 + hd) for hd in range(4)])

            OB = [("ps", 4 + hd) for hd in range(4)]

            def evac(it):
                s, g = its[it]
                A("dve", lambda h: h.reciprocal(out=rden[:], in_=ps[:, 4:8, 128:129]), OB, ["rden"])
                A("dve", lambda h: h.tensor_tensor(out=on[:], in0=ps[:, 4:8, 0:128], in1=rden[:].to_broadcast([128, 4, 128]), op=ALU.mult),
                  OB + ["rden"], ["on"])

                def tro(h):
                    for hd in range(4):
                        i = h.transpose(out=ps[:, 4, :].bitcast(BF16)[:, 512 + hd * 128:512 + (hd + 1) * 128], in_=on[:, hd, :], identity=ident[:])
                    return i
                A("pe", tro, ["on"], [("ps", 4)])
                A("dve", lambda h, s=s, g=g: h.tensor_copy(out=OT[:, g * 4:(g + 1) * 4, s * 128:(s + 1) * 128],
                                                           in_=ps[:, 4, :].bitcast(BF16)[:, 512:1024].rearrange("p (h t) -> p h t", h=4)),
                  [("ps", 4)], BUFA[2:4])

            NJOB = len(jobs)
            qk(0)
            if NJOB > 1:
                qk(1)
            for gp in range(NJOB):
                if gp + 2 < NJOB:
                    qk(gp + 2)
                pv(gp)
                if jobs[gp][1] == npair - 1:
                    evac(jobs[gp][0])

            if L["cfg"].get("dbg") and m == 0:
                dbg = L["dbg_t"]
                A("pool", lambda h: h.dma_start(out=dbg["ycT"], in_=bA16[:, 0:4096]), BUFA, ["d1"], dma=True)
                A("pool", lambda h: h.dma_start(out=dbg["OT"], in_=bA16[:, 4096:8192]), BUFA, ["d2"], dma=True)
                A("pool", lambda h: h.dma_start(out=dbg["qT"], in_=qT[:].rearrange("p k t -> p (k t)")), [("qT", s_) for s_ in range(4)], ["d3"], dma=True)
                A("pool", lambda h: h.dma_start(out=dbg["KT"], in_=KT[:].rearrange("p g t -> p (g t)")), (), ["d4"], dma=True)
                A("pool", lambda h: h.dma_start(out=dbg["Vx"], in_=Vx[:].rearrange("p a g d -> p (a g d)")), (), ["d5"], dma=True)
            for jj in range(4):
                sg_, kg_ = ring.load(P, L["S_G"][jj], ext=cext("g"))
                sr_, kr_ = ring.load(P, L["S_R"][jj], ext=cext("r"))
                sgv = sg_.rearrange("p (k c) -> p k c", k=8)
                srv = sr_.rearrange("p (k c) -> p k c", k=8)
                for t in range(2):
                    j = 2 * jj + t
                    base = 4 * (j % 2)

                    def mmg(h, sgv=sgv, srv=srv, t=t, base=base):
                        for c2 in range(2):
                            for k in range(8):
                                h.matmul(ps[:, base + c2, :], lhsT=sgv[:, k, (2 * t + c2) * 128:(2 * t + c2 + 1) * 128], rhs=h2T[:, k, :], start=(k == 0), stop=(k == 7))
                        for c2 in range(2):
                            src = ycT if c2 == 0 else OT
                            for k in range(8):
                                i = h.matmul(ps[:, base + 2 + c2, :], lhsT=srv[:, k, (2 * t + c2) * 128:(2 * t + c2 + 1) * 128], rhs=src[:, k, :], start=(k == 0), stop=(k == 7))
                        return i
                    A("pe", mmg, H2K + [kg_, kr_] + BUFA, [("ps", base + b_) for b_ in range(4)])
                    for c2 in range(2):
                        sl = 2 * (j % 2) + c2
                        A("act", lambda h, sl=sl, base=base, c2=c2: h.activation(out=sg[:, sl, :], in_=ps[:, base + c2, :], func=AF.Sigmoid), [("ps", base + c2)], [("sg", sl)])
                        A("dve", lambda h, sl=sl, base=base, c2=c2: h.tensor_tensor(out=tm[:, sl, :], in0=sg[:, sl, :], in1=ps[:, base + 2 + c2, :], op=ALU.mult),
                          [("sg", sl), ("ps", base + 2 + c2)], [("tm", sl)])
                    s0 = 2 * (j % 2)
                    A("pool", lambda h, j=j, s0=s0: h.tensor_tensor(out=mT[:, j, :], in0=tm[:, s0, :], in1=tm[:, s0 + 1, :], op=ALU.add),
                      [("tm", s0), ("tm", s0 + 1)], [("qT", s_) for s_ in range(4)])

            A("act", lambda h, m=m: h.dma_start(out=xres[:], in_=L["x1_s"][m * MT:(m + 1) * MT, :].rearrange("(s p) d -> p s d", p=128)),
              (), BUFA, dma=True)
            o0, ko0 = ring.load(P, L["S_O"][0], ext=cext("o"))
            o1, ko1 = ring.load(P, L["S_O"][1], ext=cext("o"))
            ov = [o0.rearrange("p (k c) -> p k c", k=8), o1.rearrange("p (k c) -> p k c", k=8)]
            for s in range(4):
                b0 = 2 * (s % 2)

                def mmo(h, s=s, b0=b0, ov=ov):
                    for k in range(8):
                        for hh in range(2):
                            i = h.matmul(ps[:, b0 + hh, :], lhsT=mT[:, k, s * 128:(s + 1) * 128], rhs=ov[hh][:, k, :], start=(k == 0), stop=(k == 7))
                    return i
                A("pe", mmo, [("qT", s_) for s_ in range(4)] + [ko0, ko1], [("ps", b0), ("ps", b0 + 1)])
                for hh in range(2):
                    A("dve", lambda h, b0=b0, hh=hh: h.tensor_tensor(out=tmp[:, hh, :], in0=ps[:, b0 + hh, :], in1=g5[:, hh * 512:(hh + 1) * 512], op=ALU.mult),
                      [("ps", b0 + hh), "gate"], [("tm", hh)])
                    A("pool", lambda h, s=s, hh=hh: h.tensor_tensor(out=xres[:, s, hh * 512:(hh + 1) * 512], in0=tmp[:, hh, :],
                                                                   in1=xres[:, s, hh * 512:(hh + 1) * 512], op=ALU.add),
                      [("tm", hh)] + BUFA, BUFA)
            A("pool", lambda h, m=m: h.dma_start(out=L["xm_s"][m * MT:(m + 1) * MT, :].rearrange("(s p) d -> p s d", p=128), in_=xres[:]),
              BUFA, [("xm_s", m)], dma=True)
        with nc.Block() as block:
            P.emit(block)


def build_phase3(L):
    nc, ps = L["nc"], L["ps"]
    NO, MO, MT = L["NO"], L["MO"], L["MT"]
    with ExitStack() as e3:
        def sb(name, shape, dt):
            return e3.enter_context(nc.sbuf_tensor("t_" + name, list(shape), dt))
        P = Prog(nc, L["es"], "p3")
        A = P.add
        ringt = sb("ring3", [128, L["NRING3"], 4096], BF16)
        ring = Ring(ringt, L["NRING3"])
        xres = sb("xres3", [128, 4, D], F32)
        xn = sb("xn3", [128, 4, D], BF16)
        junk = sb("junk3", [128, D], BF16)
        hT = sb("hT3", [128, 8, MT], BF16)
        actT = sb("actT3", [128, NJ, MT], BF16)
        sa = sb("sa3", [128, 2, MT], F32)
        tmp = sb("tmp3", [128, 2, MT], F32)
        ss = sb("ss3", [128, 4], F32)
        tt = sb("tt3", [128, 4], F32)
        rstd = sb("rstd3", [128, 4], F32)
        g8 = sb("g8", [128, D], F32)
        fgb = sb("fgb", [128, D], F32)
        yo = sb("yo", [128, 2, D], F32)
        A("sp", lambda h: h.dma_start(out=g8[:], in_=L["gbc_s"][3]), (), ["gate"], dma=True)
        A("sp", lambda h: h.dma_start(out=fgb[:], in_=L["fg_in"].partition_broadcast(128)), (), ["fgb"], dma=True)
        for m in range(MO):
            A("act", lambda h, m=m: h.dma_start(out=xres[:], in_=L["xm_s"][m * MT:(m + 1) * MT, :].rearrange("(s p) d -> p s d", p=128)),
              (), [("xres", s) for s in range(4)], dma=True)
            norm_to_hT(L, P, 4, xres, 2, 0, hT, "hT", xn, junk, ss, tt, rstd)
            ffn_core(L, P, 4, ring, L["S_3I"], L["S_3O"], L["cext"]("3i"), L["cext"]("3o"), hT, actT, sa, tmp, xres, g8[:])
            for s in range(4):
                A("act", lambda h, s=s: h.activation(out=xn[:, s, :], in_=xres[:, s, :], func=AF.Square, accum_out=ss[:, s:s + 1]),
                  [("xres", s)], [("ss", s), ("xn", s)])
            A("dve", lambda h: h.tensor_scalar(out=tt[:], in0=ss[:], scalar1=1.0 / D, scalar2=EPS, op0=ALU.mult, op1=ALU.add),
              [("ss", s) for s in range(4)], ["tt"])
            A("pool", lambda h: h.tensor_tensor(out=rstd[:], in0=tt[:], in1=L["mhalf"][:, 0:4], op=ALU.pow), ["tt"], ["rstd"])
            for s in range(4):
                A("dve", lambda h, s=s: h.scalar_tensor_tensor(out=yo[:, s % 2, :], in0=xres[:, s, :], scalar=rstd[:, s:s + 1], in1=fgb[:],
                                                               op0=ALU.mult, op1=ALU.mult), [("xres", s), "rstd", "fgb"], [("yo", s % 2)])
                A("pool", lambda h, s=s, m=m: h.dma_start(out=L["y_out"][m * MT + s * 128:m * MT + (s + 1) * 128, :], in_=yo[:, s % 2, :]),
                  [("yo", s % 2)], [("y", m, s)], dma=True)
        with nc.Block() as block:
            P.emit(block)


def norm_to_hT2(L, P, NT, xres, xk, n_i, kind, dst, dkey, W, pk, bank):
    A = P.add
    gsc, modc, mhalf, ident, ps = L["gsc"], L["modc"], L["mhalf"], L["ident"], L["ps"]
    xn, junk, ss, tt, rstd = W
    sh_m = (0, 3, 6)[n_i]
    for s in range(NT):
        A("act", lambda h, s=s: h.activation(out=junk[:, s % 2, :], in_=xres[:, s, :], func=AF.Square, accum_out=ss[:, s:s + 1]),
          [(xk, s)], [(pk + "ss", s), (pk + "junk", s % 2)])
        if s % 2 == 1:
            yield
    A("dve", lambda h: h.tensor_scalar(out=tt[:, 0:NT], in0=ss[:, 0:NT], scalar1=1.0 / D, scalar2=EPS, op0=ALU.mult, op1=ALU.add),
      [(pk + "ss", s) for s in range(NT)], [pk + "tt"])
    A("pool", lambda h: h.tensor_tensor(out=rstd[:, 0:NT], in0=tt[:, 0:NT], in1=mhalf[:, 0:NT], op=ALU.pow), [pk + "tt"], [pk + "rstd"])
    yield
    for s in range(NT):
        A("dve", lambda h, s=s: h.tensor_scalar(out=xn[:, s % 2, :], in0=xres[:, s, :], scalar1=rstd[:, s:s + 1], scalar2=None, op0=ALU.mult),
          [(xk, s), pk + "rstd"], [(pk + "xn", s % 2)])
        yield

        def tr(h, s=s):
            for k in range(8):
                i = h.transpose(out=ps[:, bank, :].bitcast(BF16)[:, k * 128:(k + 1) * 128], in_=xn[:, s % 2, k * 128:(k + 1) * 128], identity=ident[:])
            return i
        A("pe", tr, [(pk + "xn", s % 2)], [("ps", bank)])
        yield
        for k in range(8):
            src = ps[:, bank, :].bitcast(BF16)[:, k * 128:(k + 1) * 128]
            o = dst[:, k, s * 128:(s + 1) * 128]
            A("act", lambda h, k=k, o=o, src=src: h.activation(out=o, in_=src, func=AF.Identity, bias=modc[:, sh_m, k, kind:kind + 1],
                                                               scale=gsc[:, n_i, k, kind:kind + 1]), [("ps", bank)], [(dkey, k)])
            if k == 3:
                yield


def norm_work(nc, stack, pfx):
    def sb(name, shape, dt):
        return stack.enter_context(nc.sbuf_tensor("t_" + pfx + name, list(shape), dt))
    return (sb("xn", [128, 2, D], BF16), sb("junk", [128, 2, D], BF16), sb("ss", [128, 4], F32), sb("tt", [128, 4], F32), sb("rstd", [128, 4], F32))


def drain(gens):
    for g in gens:
        for _ in g:
            pass


def ffn_B(L, P, T, ring, S_I, ext_i, hT, hkey, actT, sa, sched):
    A = P.add
    ps = L["ps"]
    gens = []
    for jj in range(11):
        slot, skey = ring.load(P, S_I[jj], ext=ext_i)
        sv = slot.rearrange("p (k c) -> p k c", k=8)
        for t in range(2):
            q = 2 * jj + t
            gens.extend(sched.get(q, []))
            for g in list(gens):
                try:
                    next(g)
                except StopIteration:
                    gens.remove(g)
            st = q % 2
            ba, bb = 2 * st, 2 * st + 1

            def mm(h, t=t, sv=sv, ba=ba, bb=bb):
                for k in range(8):
                    h.matmul(ps[:, ba, 0:T], lhsT=sv[:, k, t * 128:(t + 1) * 128], rhs=hT[:, k, 0:T], start=(k == 0), stop=(k == 7))
                for k in range(8):
                    i = h.matmul(ps[:, bb, 0:T], lhsT=sv[:, k, (2 + t) * 128:(3 + t) * 128], rhs=hT[:, k, 0:T], start=(k == 0), stop=(k == 7))
                return i
            A("pe", mm, [skey] + [(hkey, k) for k in range(8)], [("ps", ba), ("ps", bb)])
            A("act", lambda h, q=q, ba=ba: h.activation(out=sa[:, q % 2, 0:T], in_=ps[:, ba, 0:T], func=AF.Silu), [("ps", ba)], [("sa", q % 2)])
            A("dve", lambda h, q=q, bb=bb: h.tensor_tensor(out=actT[:, q, 0:T], in0=sa[:, q % 2, 0:T], in1=ps[:, bb, 0:T], op=ALU.mult),
              [("sa", q % 2), ("ps", bb)], [("actT", q)])
    drain(gens)


def ffn_C(L, P, NT, ring, S_O, ext_o, actT):
    A = P.add
    ps = L["ps"]
    for c in range(6):
        nj = 4 if c < 5 else 2
        slot, skey = ring.load(P, S_O[c][:, 0:nj * 1024], ncols=nj * 1024, ext=ext_o)
        sv = slot.rearrange("p (j d) -> p j d", j=4)
        for jj in range(nj):
            j = 4 * c + jj

            def mm(h, j=j, jj=jj, sv=sv):
                for s in range(NT):
                    for hh in range(2):
                        i = h.matmul(ps[:, 2 * s + hh, :], lhsT=actT[:, j, s * 128:(s + 1) * 128], rhs=sv[:, jj, hh * 512:(hh + 1) * 512],
                                     start=(j == 0), stop=(j == NJ - 1))
                return i
            A("pe", mm, [skey, ("actT", j)], [("ps", b) for b in range(2 * NT)])


def ffn_evac(L, P, NT, tmp, xres, xk, gate_bc):
    A = P.add
    ps = L["ps"]
    for s in range(NT):
        for hh in range(2):
            b = 2 * s + hh
            A("dve", lambda h, b=b, hh=hh: h.tensor_tensor(out=tmp[:, b % 2, :], in0=ps[:, b, :], in1=gate_bc[:, hh * 512:(hh + 1) * 512], op=ALU.mult),
              [("ps", b), "gate"], [("tmp", b % 2)])
            A("pool", lambda h, b=b, s=s, hh=hh: h.tensor_tensor(out=xres[:, s, hh * 512:(hh + 1) * 512], in0=tmp[:, b % 2, :],
                                                                 in1=xres[:, s, hh * 512:(hh + 1) * 512], op=ALU.add),
              [("tmp", b % 2), (xk, s)], [(xk, s)])


def build_phase1p(L):
    nc, ps = L["nc"], L["ps"]
    NO, NH, NCX, MO, MH, MT, NKT = L["NO"], L["NH"], L["NCX"], L["MO"], L["MH"], L["MT"], L["NKT"]
    ident, halo_all, meta = L["ident"], L["halo_all"], L["meta"]
    kt_s, vx_s = L["kt_s"], L["vx_s"]
    with ExitStack() as e1:
        def sb(name, shape, dt):
            return e1.enter_context(nc.sbuf_tensor("t_" + name, list(shape), dt))
        P = Prog(nc, L["es"], "p1")
        A = P.add
        NR = L["NRING1"]
        ringt = sb("ring1", [128, NR, 4096], BF16)
        ring = Ring(ringt, NR)
        xresb = [sb("xres_a", [128, 4, D], F32), sb("xres_b", [128, 4, D], F32), sb("xres_c", [128, 4, D], F32)]
        hTb = [sb("hT_a", [128, 8, MT], BF16), sb("hT_b", [128, 8, MT], BF16)]
        h2T = sb("h2T1", [128, 8, MT], BF16)
        actT = sb("actT", [128, NJ, MT], BF16)
        sa = sb("sa", [128, 2, MT], F32)
        tmp = sb("tmp", [128, 2, MT], F32)
        WH = norm_work(nc, e1, "nh_")
        WT = norm_work(nc, e1, "nt_")
        wkv = sb("wkv", [128, 8, 512], BF16)
        g2 = sb("g2", [128, 2, D], F32)
        tabt = sb("tabt", [128, 4, 128], F32)
        kb = sb("kb", [128, 2, 128], BF16)
        KTs = sb("KTs", [128, 2, 2, MT], BF16)
        Vs = sb("Vs", [128, 2, 4, 2, 130], BF16)
        W = hn_work(nc, e1, 2, "k_")

        A("sp", lambda h: h.dma_start(out=g2[:], in_=L["gbc_s"][0:2].rearrange("a p d -> p a d")), (), ["gate"], dma=True)
        A("sp", lambda h: h.dma_start(out=wkv[:].rearrange("p k c -> p (k c)"), in_=L["S_KV"][0]), (), ["wkv"], dma=True, ext=L["cext"]("kv"))
        A("pool", lambda h: h.memset(Vs[:, :, :, :, 128:130], 1.0), (), [("Vs", 0), ("Vs", 1)])

        tiles = [("own", m, 4) for m in range(MO)] + [("oth", m, 4) for m in range(MH)] + [("ctx", 0, NCX // 128)]
        NTI = len(tiles)
        late = list(L["late_casts"])
        per_tile = (len(late) + max(1, NTI - 2) - 1) // max(1, NTI - 2)

        def tile_src(i):
            kind, m, NT = tiles[i]
            if kind == "ctx":
                return L["ctx"], 0, None
            if kind == "own":
                return L["x_own"][m * MT:(m + 1) * MT, :], (NCX + m * MT) // 128, m * 4
            return L["x_oth"][m * MT:(m + 1) * MT, :], (NCX + NO + m * MT) // 128, (NO + m * MT) // 128

        def Hload(i):
            kind, m, NT = tiles[i]
            src, kt0, sub0 = tile_src(i)
            xr = xresb[i % 3]
            A("act", lambda h, src=src, NT=NT, xr=xr: h.dma_start(out=xr[:, 0:NT, :], in_=src.rearrange("(s p) d -> p s d", p=128)),
              (), [(("xres", i % 3), s) for s in range(NT)], dma=True)

        def Hnorm(i):
            kind, m, NT = tiles[i]
            kd = 1 if kind == "ctx" else 0
            yield from norm_to_hT2(L, P, NT, xresb[i % 3], ("xres", i % 3), 0, kd, hTb[i % 2], ("hT", i % 2), WH, "nh_", 4)

        def Trest(i):
            kind, m, NT = tiles[i]
            kd = 1 if kind == "ctx" else 0
            src, kt0, sub0 = tile_src(i)
            xr = xresb[i % 3]
            xk = ("xres", i % 3)
            T = NT * 128
            if sub0 is not None:
                A("sp", lambda h, sub0=sub0: h.dma_start(out=tabt[:], in_=L["rope_s"][sub0:sub0 + 4].rearrange("s p c -> p s c")),
                  (), ["tab"], dma=True)
            yield from norm_to_hT2(L, P, NT, xr, xk, 1, kd, h2T, "h2T", WT, "nt_", 5)
            H2 = [("h2T", k) for k in range(8)]
            if kind == "own":
                A("pool", lambda h, m=m, xr=xr: h.dma_start(out=L["x1_s"][m * MT:(m + 1) * MT, :].rearrange("(s p) d -> p s d", p=128), in_=xr[:]),
                  [(xk, s) for s in range(4)], [("x1_s", m)], dma=True)
                A("pool", lambda h, m=m: h.dma_start(out=L["h2T_s"][m], in_=h2T[:].rearrange("p k t -> p (k t)")), H2, [("h2T_s", m)], dma=True)
            if kind != "ctx":
                hi = m if kind == "own" else MO + m
                A("pool", lambda h, hi=hi: h.tensor_copy(out=halo_all[:, hi, :, 0], in_=h2T[:, :, 0]), H2, ["halo"])
                A("pool", lambda h, hi=hi: h.tensor_copy(out=halo_all[:, hi, :, 1], in_=h2T[:, :, MT - 1]), H2, ["halo"])
            sbuf = i % 2
            yield
            for s in range(NT):
                bank = 6
                tbank = 7

                def mmkv(h, s=s, bank=bank):
                    for k in range(8):
                        i_ = h.matmul(ps[:, bank, :], lhsT=h2T[:, k, s * 128:(s + 1) * 128], rhs=wkv[:, k, :], start=(k == 0), stop=(k == 7))
                    return i_
                A("pe", mmkv, H2 + ["wkv"], [("ps", bank)])
                yield
                A("act", lambda h, bank=bank, s=s, sbuf=sbuf: h.activation(out=Vs[:, sbuf, s, :, 0:128], in_=ps[:, bank, 256:512].rearrange("p (g d) -> p g d", g=2),
                                                                          func=AF.Copy), [("ps", bank)], [("Vs", sbuf)])
                headnorm_rope(L, P, ps[:, bank, 0:256].rearrange("p (g d) -> p g d", g=2), [("ps", bank)], 2, L["kgb"],
                              None if kind == "ctx" else tabt[:, s, :], kb[:], ["kb"], W)

                yield
                yield

                def trk(h, tbank=tbank):
                    for g in range(2):
                        i_ = h.transpose(out=ps[:, tbank, :].bitcast(BF16)[:, g * 128:(g + 1) * 128], in_=kb[:, g, :], identity=ident[:])
                    return i_
                A("pe", trk, ["kb"], [("ps", tbank)])
                A("dve", lambda h, s=s, sbuf=sbuf, tbank=tbank: h.tensor_copy(out=KTs[:, sbuf, :, s * 128:(s + 1) * 128],
                                                                              in_=ps[:, tbank, :].bitcast(BF16)[:, 0:256].rearrange("p (g t) -> p g t", g=2)),
                  [("ps", tbank)], [("KTs", sbuf)])
            A("pool", lambda h, sbuf=sbuf, kt0=kt0, T=T: h.dma_start(out=kt_s[:, :, kt0 * 128:kt0 * 128 + T], in_=KTs[:, sbuf, :, 0:T]),
              [("KTs", sbuf)], [("kt_s", i)], dma=True)
            A("pool", lambda h, sbuf=sbuf, kt0=kt0, NT=NT: h.dma_start(out=vx_s[:, kt0:kt0 + NT, :, :], in_=Vs[:, sbuf, 0:NT, :, :]),
              [("Vs", sbuf)], [("vx_s", i)], dma=True)

        Hload(0)
        drain([Hnorm(0)])
        for i, (kind, m, NT) in enumerate(tiles):
            for _ in range(per_tile if i < NTI - 1 else len(late)):
                if late:
                    P.add_raw("pool", late.pop(0))
            kd = 1 if kind == "ctx" else 0
            sched = {}
            if i >= 1:
                sched[1] = [Trest(i - 1)]
            if i + 1 < NTI:
                Hload(i + 1)
                sched[5] = [Hnorm(i + 1)]
            ffn_B(L, P, NT * 128, ring, L["S_1I"], L["cext"]("1i"), hTb[i % 2], ("hT", i % 2), actT, sa, sched)
            ffn_C(L, P, NT, ring, L["S_1O"], L["cext"]("1o"), actT)
            ffn_evac(L, P, NT, tmp, xresb[i % 3], ("xres", i % 3), g2[:, kd, :])
        drain([Trest(NTI - 1)])
        with nc.Block() as block:
            P.emit(block)


def build_phase3p(L):
    nc, ps = L["nc"], L["ps"]
    NO, MO, MT = L["NO"], L["MO"], L["MT"]
    with ExitStack() as e3:
        def sb(name, shape, dt):
            return e3.enter_context(nc.sbuf_tensor("t_" + name, list(shape), dt))
        P = Prog(nc, L["es"], "p3")
        A = P.add
        NR = L["NRING3"]
        ringt = sb("ring3", [128, NR, 4096], BF16)
        ring = Ring(ringt, NR)
        xresb = [sb("xres3a", [128, 4, D], F32), sb("xres3b", [128, 4, D], F32), sb("xres3c", [128, 4, D], F32)]
        hTb = [sb("hT3a", [128, 8, MT], BF16), sb("hT3b", [128, 8, MT], BF16)]
        actT = sb("actT3", [128, NJ, MT], BF16)
        sa = sb("sa3", [128, 2, MT], F32)
        tmp = sb("tmp3", [128, 2, MT], F32)
        WH = norm_work(nc, e3, "n3_")
        junk = sb("fjunk", [128, 2, D], BF16)
        ss = sb("fss", [128, 4], F32)
        tt = sb("ftt", [128, 4], F32)
        rstd = sb("frstd", [128, 4], F32)
        g8 = sb("g8", [128, D], F32)
        fgb = sb("fgb", [128, D], F32)
        yo = sb("yo", [128, 2, D], F32)
        A("sp", lambda h: h.dma_start(out=g8[:], in_=L["gbc_s"][3]), (), ["gate"], dma=True)
        A("sp", lambda h: h.dma_start(out=fgb[:], in_=L["fg_in"].partition_broadcast(128)), (), ["fgb"], dma=True)

        def Hload(m):
            xr = xresb[m % 3]
            A("act", lambda h, m=m, xr=xr: h.dma_start(out=xr[:], in_=L["xm_s"][m * MT:(m + 1) * MT, :].rearrange("(s p) d -> p s d", p=128)),
              (), [(("xres", m % 3), s) for s in range(4)], dma=True)

        def Hnorm(m):
            yield from norm_to_hT2(L, P, 4, xresb[m % 3], ("xres", m % 3), 2, 0, hTb[m % 2], ("hT", m % 2), WH, "n3_", 4)

        def Trest(m):
            xr = xresb[m % 3]
            xk = ("xres", m % 3)
            for s in range(4):
                A("act", lambda h, s=s, xr=xr: h.activation(out=junk[:, s % 2, :], in_=xr[:, s, :], func=AF.Square, accum_out=ss[:, s:s + 1]),
                  [(xk, s)], [("fss", s), ("fjunk", s % 2)])
                if s % 2 == 1:
                    yield
            A("dve", lambda h: h.tensor_scalar(out=tt[:], in0=ss[:], scalar1=1.0 / D, scalar2=EPS, op0=ALU.mult, op1=ALU.add),
              [("fss", s) for s in range(4)], ["ftt"])
            A("pool", lambda h: h.tensor_tensor(out=rstd[:], in0=tt[:], in1=L["mhalf"][:, 0:4], op=ALU.pow), ["ftt"], ["frstd"])
            for s in range(4):
                A("dve", lambda h, s=s, xr=xr: h.scalar_tensor_tensor(out=yo[:, s % 2, :], in0=xr[:, s, :], scalar=rstd[:, s:s + 1], in1=fgb[:],
                                                                      op0=ALU.mult, op1=ALU.mult), [(xk, s), "frstd", "fgb"], [("yo", s % 2)])
                A("pool", lambda h, s=s, m=m: h.dma_start(out=L["y_out"][m * MT + s * 128:m * MT + (s + 1) * 128, :], in_=yo[:, s % 2, :]),
                  [("yo", s % 2)], [("y", m, s)], dma=True)
                yield

        Hload(0)
        drain([Hnorm(0)])
        for m in range(MO):
            sched = {}
            if m >= 1:
                sched[1] = [Trest(m - 1)]
            if m + 1 < MO:
                Hload(m + 1)
                sched[5] = [Hnorm(m + 1)]
            ffn_B(L, P, MT, ring, L["S_3I"], L["cext"]("3i"), hTb[m % 2], ("hT", m % 2), actT, sa, sched)
            ffn_C(L, P, 4, ring, L["S_3O"], L["cext"]("3o"), actT)
            ffn_evac(L, P, 4, tmp, xresb[m % 3], ("xres", m % 3), g8[:])
        drain([Trest(MO - 1)])
        with nc.Block() as block:
            P.emit(block)

def host_inputs(b, hh, NO, x, c, ctx, c_ctx, w_mod, b_mod, norm1_g, norm2_g, norm3_g, ffn1_w_in, ffn1_w_out, w_in, conv_w,
                q_norm_g, k_norm_g, w_branch_conv, w_branch_attn, w_out, ffn2_w_in, ffn2_w_out, final_g):
    f = np.float32
    own = slice(hh * NO, (hh + 1) * NO)
    oth = slice((1 - hh) * NO, (2 - hh) * NO)
    cT = np.stack([c[b].reshape(8, 128).T, c_ctx.reshape(8, 128).T], axis=-1).reshape(128, 16)
    gT = np.stack([g[0].reshape(8, 128).T for g in (norm1_g, norm2_g, norm3_g)], axis=1).reshape(128, 24)
    convT = np.stack([conv_w[0, j].reshape(8, 128).T for j in range(3)], axis=1).reshape(128, 24)
    p = np.arange(128)
    meta = np.zeros((128, 8), f)
    meta[:, 0] = hh * NO // 64 + (p >> 6)
    meta[:, 1] = (1 - hh) * NO // 64 + (p >> 6)
    meta[:, 2] = p & 63
    meta[:, 3] = 0.0 if hh == 0 else 1.0
    meta[:, 4] = 1.0 if hh == 0 else 0.0
    return {
        "x_own": np.ascontiguousarray(x[b, own]), "x_oth": np.ascontiguousarray(x[b, oth]), "ctx": np.ascontiguousarray(ctx[b]),
        "cT": np.ascontiguousarray(cT, f), "bmodT": np.ascontiguousarray(b_mod[0].reshape(72, 128).T, f), "bmod": np.ascontiguousarray(b_mod[0], f),
        "gT": np.ascontiguousarray(gT, f), "convT": np.ascontiguousarray(convT, f), "final_g": np.ascontiguousarray(final_g, f),
        "q_norm_g": np.ascontiguousarray(q_norm_g[0], f), "k_norm_g": np.ascontiguousarray(k_norm_g[0], f), "meta": meta,
        "w_mod": w_mod[0], "ffn1_w_in": ffn1_w_in[0], "ffn1_w_out": ffn1_w_out[0], "w_in": w_in[0],
        "w_branch_conv": w_branch_conv[0], "w_branch_attn": w_branch_attn[0], "w_out": w_out[0],
        "ffn2_w_in": ffn2_w_in[0], "ffn2_w_out": ffn2_w_out[0],
    }


def kernel(**inputs):
    inputs = {k: np.asarray(v) for k, v in inputs.items()}
    x = inputs["x"]
    B, S, _ = x.shape
    NO = S // 2
    nc = build({"NO": NO, "NH": NO, "NCX": inputs["ctx"].shape[1]})
    in_maps = []
    for core in range(2 * B):
        in_maps.append(host_inputs(core // 2, core % 2, NO, **inputs))
    res = run_bass_kernel_spmd(nc, in_maps, core_ids=list(range(2 * B)))
    out = np.empty((B, S, D), np.float32)
    for core in range(2 * B):
        b, hh = core // 2, core % 2
        out[b, hh * NO:(hh + 1) * NO] = np.asarray(res.results[core]["y"], dtype=np.float32)
    return out
```

```python
import numpy as np
from contextlib import ExitStack
import concourse.bass as bass
import concourse.mybir as mybir
from concourse.bass_utils import run_bass_kernel_spmd

F32 = mybir.dt.float32
BF16 = mybir.dt.bfloat16
I32 = mybir.dt.int32
AF = mybir.ActivationFunctionType
ALU = mybir.AluOpType
AX = mybir.AxisListType

D = 1024
DFF = 2816
NJ = DFF // 128
EPS = 1e-6
THETA = 10000.0
COMPUTE = ("pe", "act", "dve", "pool")


class Op:
    __slots__ = ("id", "eng", "fn", "deps", "dma", "sem", "val", "needed", "ext", "prewait")


class Prog:
    NRING = {"sp": 8, "act": 2, "pool": 4}

    def __init__(self, nc, es, name):
        self.nc = nc
        self.name = name
        self.ops = []
        self.eng_ops = {e: [] for e in ("pe", "act", "dve", "pool", "sp")}
        self.res = {}
        self.sems = {e: es.enter_context(nc.semaphore(f"{name}_{e}")) for e in COMPUTE}
        self.dsems = {q: [es.enter_context(nc.semaphore(f"{name}_d{q}{i}")) for i in range(self.NRING[q])]
                      for q in ("sp", "act", "pool")}
        self.dcount = {q: 0 for q in ("sp", "act", "pool")}

    def add(self, eng, fn, reads=(), writes=(), dma=False, ext=()):
        op = Op()
        op.id = len(self.ops)
        op.eng = eng
        op.fn = fn
        op.dma = dma
        op.ext = list(ext)
        op.needed = False
        op.sem = None
        op.val = 0
        op.prewait = None
        deps = {}
        for k in reads:
            st = self.res.get(k)
            if st is not None and st[0] is not None:
                deps[st[0]] = "raw"
        for k in writes:
            st = self.res.get(k)
            if st is not None:
                if st[0] is not None and st[0] not in deps:
                    deps[st[0]] = "waw"
                for r in st[1]:
                    if r not in deps:
                        deps[r] = "war"
        deps.pop(op.id, None)
        for k in reads:
            st = self.res.setdefault(k, [None, []])
            st[1].append(op.id)
        for k in writes:
            self.res[k] = [op.id, []]
        keep = []
        for pid, kind in deps.items():
            p = self.ops[pid]
            if (not dma) and (not p.dma) and p.eng == eng:
                if eng == "pe":
                    continue
            keep.append(pid)
            p.needed = True
        op.deps = keep
        if dma:
            q = eng
            i = self.dcount[q]
            self.dcount[q] = i + 1
            nr = self.NRING[q]
            op.sem = self.dsems[q][i % nr]
            op.val = 16 * (i // nr + 1)
            if i >= nr:
                op.prewait = (op.sem, op.val - 16)
        self.ops.append(op)
        self.eng_ops[eng].append(op)
        return op

    def add_raw(self, eng, fn, reads=()):
        op = Op()
        op.id = len(self.ops)
        op.eng, op.fn, op.dma, op.ext, op.needed, op.sem, op.val, op.prewait, op.deps = eng, fn, "raw", [], False, None, 0, None, []
        for k in reads:
            st = self.res.get(k)
            if st is not None and st[0] is not None:
                op.deps.append(st[0])
                self.ops[st[0]].needed = True
        self.ops.append(op)
        self.eng_ops[eng].append(op)
        return op

    def emit(self, block):
        for e in COMPUTE:
            ops = [o for o in self.eng_ops[e] if not o.dma]
            if ops:
                ops[-1].needed = True
            c = 0
            for o in ops:
                if o.needed:
                    c += 1
                    o.sem = self.sems[e]
                    o.val = c
        finals = []
        for e in COMPUTE:
            ops = [o for o in self.eng_ops[e] if not o.dma]
            if ops:
                finals.append((self.sems[e], ops[-1].val))
        for q in ("sp", "act", "pool"):
            n = self.dcount[q]
            nr = self.NRING[q]
            for r in range(min(n, nr)):
                cnt = (n - r + nr - 1) // nr
                finals.append((self.dsems[q][r], 16 * cnt))

        def run(eng):
            def f(h):
                waited = {}

                def wait(sem, val):
                    key = id(sem)
                    if waited.get(key, 0) >= val:
                        return
                    h.wait_ge(sem, val)
                    waited[key] = val

                for o in self.eng_ops[eng]:
                    for pid in o.deps:
                        p = self.ops[pid]
                        wait(p.sem, p.val)
                    for (s, v) in o.ext:
                        wait(s, v)
                    if o.prewait is not None:
                        wait(*o.prewait)
                    ins = o.fn(h)
                    if o.dma == "raw":
                        continue
                    if o.dma:
                        ins.then_inc(o.sem, 16)
                    elif o.needed:
                        ins.then_inc(o.sem, 1)
                for (s, v) in finals:
                    wait(s, v)
            return f

        block.tensor(run("pe"))
        block.scalar(run("act"))
        block.vector(run("dve"))
        block.gpsimd(run("pool"))
        block.sync(run("sp"))


class Ring:
    def __init__(self, tile, nslot):
        self.tile = tile
        self.nslot = nslot
        self.i = 0

    def load(self, P, src_ap, ncols=4096, ext=()):
        s = self.i % self.nslot
        self.i += 1
        dst = self.tile[:, s, 0:ncols]
        P.add("sp", lambda h, d=dst, a=src_ap: h.dma_start(out=d, in_=a), reads=(), writes=[("ring", s)],
              dma=True, ext=ext)
        return self.tile[:, s, :], ("ring", s)


def build(cfg):
    NO, NH, NCX = cfg["NO"], cfg["NH"], cfg["NCX"]
    MT = 512
    assert NO % MT == 0 and NH % MT == 0 and NCX % 128 == 0 and NCX <= MT
    MO, MH = NO // MT, NH // MT
    NKEY = NCX + NO + NH
    NKT = NKEY // 128
    assert NKT % 2 == 0
    NSUB = (NO + NH) // 128
    NRING1 = cfg.get("NRING1", 6)
    NRING2 = cfg.get("NRING2", 3)
    NRING3 = cfg.get("NRING3", 6)

    nc = bass.Bass("TRN2", target_bir_lowering=False)

    def din(name, shape, dt=F32):
        return nc.dram_tensor(name, list(shape), dt, kind="ExternalInput").ap()

    def dscr(name, shape, dt):
        return nc.dram_tensor(name, list(shape), dt, kind="Internal").ap()

    x_own = din("x_own", [NO, D])
    x_oth = din("x_oth", [NH, D])
    ctx = din("ctx", [NCX, D])
    cT_in = din("cT", [128, 16])
    bmodT_in = din("bmodT", [128, 72])
    bmod_in = din("bmod", [9 * D])
    gT_in = din("gT", [128, 24])
    convT_in = din("convT", [128, 24])
    fg_in = din("final_g", [D])
    qg_in = din("q_norm_g", [128])
    kg_in = din("k_norm_g", [128])
    meta_in = din("meta", [128, 8])
    w_mod = din("w_mod", [D, 9 * D])
    w1i = din("ffn1_w_in", [D, 2 * DFF])
    w1o = din("ffn1_w_out", [DFF, D])
    w_in = din("w_in", [D, 6656])
    w_bc = din("w_branch_conv", [D, D])
    w_ba = din("w_branch_attn", [D, D])
    w_o = din("w_out", [D, D])
    w3i = din("ffn2_w_in", [D, 2 * DFF])
    w3o = din("ffn2_w_out", [DFF, D])
    y_out = nc.dram_tensor("y", [NO, D], F32, kind="ExternalOutput").ap()

    S_1I = dscr("s_1i", [11, 128, 4096], BF16)
    S_1O = dscr("s_1o", [6, 128, 4096], BF16)
    S_3I = dscr("s_3i", [11, 128, 4096], BF16)
    S_3O = dscr("s_3o", [6, 128, 4096], BF16)
    S_KV = dscr("s_kv", [1, 128, 4096], BF16)
    S_Q = dscr("s_q", [2, 128, 4096], BF16)
    S_C = dscr("s_c", [2, 128, 4096], BF16)
    S_V = dscr("s_v", [2, 128, 4096], BF16)
    S_B = dscr("s_b", [2, 128, 4096], BF16)
    S_G = dscr("s_g", [4, 128, 4096], BF16)
    S_R = dscr("s_r", [4, 128, 4096], BF16)
    S_O = dscr("s_o", [2, 128, 4096], BF16)
    x1_s = dscr("x1_s", [NO, D], F32)
    xm_s = dscr("xm_s", [NO, D], F32)
    h2T_s = dscr("h2T_s", [MO, 128, 4096], BF16)
    rope_s = dscr("rope_s", [NSUB, 128, 128], F32)
    gbc_s = dscr("gbc_s", [5, 128, D], F32)
    kt_s = dscr("kt_s", [128, 2, NKEY], BF16)
    vx_s = dscr("vx_s", [128, NKT, 2, 130], BF16)

    dbg_t = {}
    if cfg.get("dbg"):
        for nm, n in (("ycT", 4096), ("OT", 4096), ("qT", 4096), ("KT", 2 * NKEY), ("Vx", NKT * 2 * 130)):
            dbg_t[nm] = nc.dram_tensor("dbg_" + nm, [128, n], BF16, kind="ExternalOutput").ap()
    es = ExitStack()
    with es:
        def sb(name, shape, dt, stack=es):
            return stack.enter_context(nc.sbuf_tensor("t_" + name, list(shape), dt))

        ps = es.enter_context(nc.psum_tensor("ps", [128, 8, 512], F32))

        def psb(b):
            return ps[:, b, :]

        def psb16(b):
            return ps[:, b, :].bitcast(BF16)

        ident = sb("ident", [128, 128], BF16)
        identf = sb("identf", [128, 128], F32)
        onesf = sb("onesf", [128, 128], F32)
        mhalf = sb("mhalf", [128, 16], F32)
        modc = sb("modc", [128, 9, 8, 2], F32)
        gsc = sb("gsc", [128, 3, 8, 2], F32)
        meta = sb("meta", [128, 8], F32)
        convc = sb("convc", [128, 3, 8], F32)
        qgb = sb("qgb", [128, 128], F32)
        kgb = sb("kgb", [128, 128], F32)
        nbias = sb("nbias", [128, 1], F32)
        halo_all = sb("halo_all", [128, MO + MH, 8, 2], BF16)

        cast_groups = {}

        def cast_group(name):
            s = es.enter_context(nc.semaphore("cg_" + name))
            cast_groups[name] = [s, 0, []]
            return cast_groups[name]

        def castdma(g, out, in_):
            g[2].append((out, in_))
            g[1] += 16

        def in_view(w, c0, n):
            return w.rearrange("(k p) c -> p k c", p=128)[:, :, c0:c0 + n]

        def chunk_view(S, c, off, n):
            return S[c].rearrange("p (k c) -> p k c", k=8)[:, :, off:off + n]

        cg = {}
        for nm, S, w in (("1i", S_1I, w1i), ("3i", S_3I, w3i)):
            g = cast_group(nm)
            cg[nm] = g
            for jj in range(11):
                castdma(g, chunk_view(S, jj, 0, 256), in_view(w, jj * 256, 256))
                castdma(g, chunk_view(S, jj, 256, 256), in_view(w, DFF + jj * 256, 256))
        for nm, S, w in (("1o", S_1O, w1o), ("3o", S_3O, w3o)):
            g = cast_group(nm)
            cg[nm] = g
            wv = w.rearrange("(j p) d -> p j d", p=128)
            for c in range(6):
                nj = 4 if c < 5 else 2
                castdma(g, S[c].rearrange("p (j d) -> p j d", j=4)[:, 0:nj, :], wv[:, 4 * c:4 * c + nj, :])
        g = cast_group("kv")
        cg["kv"] = g
        castdma(g, chunk_view(S_KV, 0, 0, 512), in_view(w_in, 4096, 512))
        for nm, S, c0 in (("q", S_Q, 3072), ("c", S_C, 1024), ("v", S_V, 2048), ("b", S_B, 0)):
            g = cast_group(nm)
            cg[nm] = g
            for c in range(2):
                castdma(g, chunk_view(S, c, 0, 512), in_view(w_in, c0 + c * 512, 512))
        g = cast_group("g")
        cg["g"] = g
        for jj in range(4):
            for t in range(2):
                j = 2 * jj + t
                castdma(g, chunk_view(S_G, jj, t * 256, 128), in_view(w_in, 4608 + j * 128, 128))
                castdma(g, chunk_view(S_G, jj, t * 256 + 128, 128), in_view(w_in, 5632 + j * 128, 128))
        g = cast_group("r")
        cg["r"] = g
        for jj in range(4):
            for t in range(2):
                j = 2 * jj + t
                castdma(g, chunk_view(S_R, jj, t * 256, 128), in_view(w_bc, j * 128, 128))
                castdma(g, chunk_view(S_R, jj, t * 256 + 128, 128), in_view(w_ba, j * 128, 128))
        g = cast_group("o")
        cg["o"] = g
        for c in range(2):
            castdma(g, chunk_view(S_O, c, 0, 512), in_view(w_o, c * 512, 512))
        cast_order = ["1i", "1o", "kv", "q", "c", "v", "b", "g", "r", "o", "3i", "3o"]

        def cext(nm):
            return [(cg[nm][0], cg[nm][1])]

        with ExitStack() as e0:
            P = Prog(nc, es, "p0")
            A = P.add
            cT = sb("cT", [128, 8, 2], F32, e0)
            sc = sb("sc", [128, 8, 2], F32, e0)
            bmodT = sb("bmodT", [128, 72], F32, e0)
            gT = sb("gT", [128, 3, 8], F32, e0)
            wm = sb("wm", [128, 2, 8, 1024], F32, e0)
            screp = sb("screp", [128, 2, 8, 128], F32, e0)
            bbc = sb("bbc", [128, 2, 1024], F32, e0)
            gtmp = sb("gtmp", [128, 2, 1024], F32, e0)
            rowm = sb("rowm", [2, 1024], F32, e0)
            iot = sb("iot", [128, 128], I32, e0)
            iof = sb("iof", [128, 128], F32, e0)
            inv = sb("inv", [128, 32], F32, e0)
            rowp = sb("rowp", [128, NSUB], F32, e0)
            ang = sb("ang", [128, NSUB, 2, 32], F32, e0)
            ang2 = sb("ang2", [128, NSUB, 2, 2, 32], F32, e0)
            tab = sb("tab", [128, NSUB, 128], F32, e0)
            twopi = sb("twopi", [128, 1], F32, e0)
            negpi = sb("negpi", [128, 1], F32, e0)
            mx = sb("mx", [128, 2], F32, e0)

            def cast_fn(nm, i0, i1):
                def f(h):
                    for (o, i) in cg[nm][2][i0:i1]:
                        h.dma_start(out=o, in_=i).then_inc(cg[nm][0], 16)
                return f
            late_casts = []
            for nm in cast_order[3:]:
                n = len(cg[nm][2])
                for i0 in range(0, n, 6):
                    late_casts.append(cast_fn(nm, i0, min(n, i0 + 6)))

            A("sp", lambda h: h.dma_start(out=cT[:].rearrange("p k n -> p (k n)"), in_=cT_in), (), ["cT"], dma=True)
            A("sp", lambda h: h.dma_start(out=bmodT[:], in_=bmodT_in), (), ["bmodT"], dma=True)
            A("sp", lambda h: h.dma_start(out=gT[:].rearrange("p a b -> p (a b)"), in_=gT_in), (), ["gT"], dma=True)
            A("sp", lambda h: h.dma_start(out=convc[:].rearrange("p a b -> p (a b)"), in_=convT_in), (), ["convc"], dma=True)
            A("sp", lambda h: h.dma_start(out=meta[:], in_=meta_in), (), ["meta"], dma=True)
            A("sp", lambda h: h.dma_start(out=qgb[:], in_=qg_in.partition_broadcast(128)), (), ["qgb"], dma=True)
            A("sp", lambda h: h.dma_start(out=kgb[:], in_=kg_in.partition_broadcast(128)), (), ["kgb"], dma=True)

            A("pool", lambda h: h.memset(onesf[:], 1.0), (), ["onesf"])
            A("pool", lambda h: h.memset(mhalf[:], -0.5), (), ["mhalf"])
            A("pool", lambda h: h.memset(twopi[:], 2.0 * np.pi), (), ["twopi"])
            A("pool", lambda h: h.memset(negpi[:], -np.pi * 0.999999), (), ["negpi"])
            A("pool", lambda h: h.iota(iot[:], [[1, 128]], channel_multiplier=-1), (), ["iot"])
            A("dve", lambda h: h.tensor_copy(out=iof[:], in_=iot[:]), ["iot"], ["iof"])
            A("dve", lambda h: h.tensor_single_scalar(out=identf[:], in_=iof[:], scalar=0.0, op=ALU.is_equal), ["iof"], ["identf"])
            A("dve", lambda h: h.tensor_copy(out=ident[:], in_=identf[:]), ["identf"], ["ident"])

            A("act", lambda h: h.activation(out=sc[:], in_=cT[:], func=AF.Silu), ["cT"], ["sc"])

            wmv = w_mod.rearrange("(k p) c -> p k c", p=128)
            bc_jobs = {2: [(0, 0, 0.5), (1, 1, 0.5)], 5: [(0, 2, 1.0)], 8: [(0, 3, 0.5)]}
            for m in range(9):
                wb = m % 2
                for hh in range(2):
                    A("sp", lambda h, m=m, wb=wb, hh=hh: h.dma_start(out=wm[:, wb, :, hh * 512:(hh + 1) * 512],
                                                                      in_=wmv[:, :, m * 1024 + hh * 512:m * 1024 + (hh + 1) * 512]),
                      (), [("wm", wb, hh)], dma=True)

                def mm_rows(h, m=m, wb=wb):
                    for hh in range(2):
                        for k in range(8):
                            i = h.matmul(ps[0:2, hh, :], lhsT=sc[:, k, :], rhs=wm[:, wb, k, hh * 512:(hh + 1) * 512], start=(k == 0), stop=(k == 7))
                    return i
                A("pe", mm_rows, [("wm", wb, 0), ("wm", wb, 1), "sc"], [("ps", 0), ("ps", 1)])
                A("act", lambda h: h.activation(out=rowm[:], in_=ps[0:2, 0:2, :].rearrange("p a b -> p (a b)"), func=AF.Copy), [("ps", 0), ("ps", 1)], ["rowm"])

                def tr_cols(h):
                    for cc in range(8):
                        i = h.transpose(out=ps[:, 2, cc * 2:cc * 2 + 2], in_=rowm[0:2, cc * 128:(cc + 1) * 128], identity=identf[0:2, 0:2])
                    return i
                A("pe", tr_cols, ["rowm", "identf"], [("ps", 2)])
                A("dve", lambda h, m=m: h.tensor_tensor(out=modc[:, m, :, :], in0=ps[:, 2, 0:16].rearrange("p (c n) -> p c n", n=2),
                                                        in1=bmodT[:, m * 8:(m + 1) * 8].unsqueeze(2).to_broadcast([128, 8, 2]), op=ALU.add),
                  [("ps", 2), "bmodT"], [("modc", m)])
                if m == 1:
                    for nm in ("1i", "1o", "kv"):
                        P.add_raw("pool", cast_fn(nm, 0, len(cg[nm][2])), reads=[("modc", 1)])
                for (n, slot, fac) in bc_jobs.get(m, []):
                    bb = slot % 2
                    for cc in range(8):
                        A("dve", lambda h, m=m, n=n, cc=cc: h.tensor_scalar(out=screp[:, 0, cc, :], in0=onesf[:], scalar1=modc[:, m, cc, n:n + 1],
                                                                            scalar2=None, op0=ALU.mult), [("modc", m), "onesf"], [("crep", cc)])

                    def mm_bc(h):
                        for cc in range(8):
                            i = h.matmul(ps[:, 4 + cc // 4, (cc % 4) * 128:(cc % 4 + 1) * 128], lhsT=screp[:, 0, cc, :], rhs=identf[:],
                                         start=True, stop=True)
                        return i
                    A("pe", mm_bc, [("crep", cc) for cc in range(8)] + ["identf"], [("ps", 4), ("ps", 5)])
                    A("act", lambda h, bb=bb, fac=fac: h.activation(out=gtmp[:, bb, :], in_=ps[:, 4:6, :].rearrange("p a b -> p (a b)"), func=AF.Copy, scale=float(fac)),
                      [("ps", 4), ("ps", 5)], [("gtmp", bb)])
                    A("pool", lambda h, bb=bb, slot=slot: h.dma_start(out=gbc_s[slot], in_=gtmp[:, bb, :]), [("gtmp", bb)], [("gbc_s", slot)], dma=True)
            for i, msc in enumerate((1, 4, 7)):
                A("dve", lambda h, i=i, msc=msc: h.scalar_tensor_tensor(out=gsc[:, i, :, :], in0=modc[:, msc, :, :], scalar=1.0,
                                                                        in1=gT[:, i, :].unsqueeze(2).to_broadcast([128, 8, 2]),
                                                                        op0=ALU.add, op1=ALU.mult),
                  [("modc", msc), "gT"], [("gsc", i)])

            A("dve", lambda h: h.tensor_reduce(out=mx[:, 0:1], in_=qgb[:], axis=AX.X, op=ALU.max, apply_absolute_value=True), ["qgb"], [("mx", 0)])
            A("dve", lambda h: h.tensor_reduce(out=mx[:, 1:2], in_=kgb[:], axis=AX.X, op=ALU.max, apply_absolute_value=True), ["kgb"], [("mx", 1)])
            A("dve", lambda h: h.scalar_tensor_tensor(out=nbias[:], in0=mx[:, 0:1], scalar=-float(np.sqrt(128.0)), in1=mx[:, 1:2],
                                                      op0=ALU.mult, op1=ALU.mult), [("mx", 0), ("mx", 1)], ["nbias"])
            A("act", lambda h: h.activation(out=qgb[:], in_=qgb[:], func=AF.Copy, scale=float(128.0 ** -0.5)),
              ["qgb", ("mx", 0)], ["qgb"])

            A("pool", lambda h: h.iota(iot[:, 0:32], [[1, 32]], channel_multiplier=0), ["iof"], ["iot"])
            A("dve", lambda h: h.tensor_copy(out=iof[:, 0:32], in_=iot[:, 0:32]), ["iot", "identf"], ["iof"])
            A("act", lambda h: h.activation(out=inv[:], in_=iof[:, 0:32], func=AF.Exp, scale=-float(np.log(THETA) / 32.0)), ["iof"], ["inv"])
            A("pool", lambda h: h.iota(iot[:, 32:32 + NSUB], [[2, NSUB]], channel_multiplier=0), (), [("iot2")])
            A("dve", lambda h: h.tensor_copy(out=rowp[:], in_=iot[:, 32:32 + NSUB]), ["iot2"], ["rowp"])
            no_s = NO // 128
            A("dve", lambda h: h.tensor_scalar(out=rowp[:, 0:no_s], in0=rowp[:, 0:no_s], scalar1=meta[:, 0:1], scalar2=None, op0=ALU.add),
              ["rowp", "meta"], ["rowp"])
            A("dve", lambda h: h.tensor_scalar(out=rowp[:, no_s:NSUB], in0=rowp[:, no_s:NSUB], scalar1=meta[:, 1:2], scalar2=float(-2 * no_s),
                                               op0=ALU.add, op1=ALU.add), ["rowp", "meta"], ["rowp"])
            A("dve", lambda h: h.tensor_tensor(out=ang[:, :, 0, :], in0=rowp[:].unsqueeze(2).to_broadcast([128, NSUB, 32]),
                                               in1=inv[:].unsqueeze(1).to_broadcast([128, NSUB, 32]), op=ALU.mult), ["rowp", "inv"], [("ang", 0)])
            A("dve", lambda h: h.tensor_scalar(out=ang[:, :, 1, :], in0=inv[:].unsqueeze(1).to_broadcast([128, NSUB, 32]), scalar1=meta[:, 2:3],
                                               scalar2=None, op0=ALU.mult), ["inv", "meta"], [("ang", 1)])
            A("dve", lambda h: h.tensor_scalar(out=ang2[:, :, 0, :, :], in0=ang[:], scalar1=float(0.5 * np.pi), scalar2=None, op0=ALU.add),
              [("ang", 0), ("ang", 1)], [("ang2", 0)])
            A("dve", lambda h: h.tensor_copy(out=ang2[:, :, 1, :, :], in_=ang[:]), [("ang", 0), ("ang", 1)], [("ang2", 1)])
            a2f = ang2[:].rearrange("p s c a f -> p (s c a f)")
            NA = NSUB * 128
            WM0 = [("wm", 0, 0), ("wm", 0, 1)]
            WM1 = [("wm", 1, 0), ("wm", 1, 1)]
            ki = wm[:, 0].rearrange("p k c -> p (k c)").bitcast(I32)[:, 0:NA]
            kf = wm[:, 1].rearrange("p k c -> p (k c)")[:, 0:NA]
            A2 = [("ang2", 0), ("ang2", 1)]
            A("dve", lambda h: h.tensor_scalar(out=kf, in0=a2f, scalar1=float(1.0 / (2.0 * np.pi)), scalar2=None, op0=ALU.mult), A2, WM1)
            A("dve", lambda h: h.tensor_copy(out=ki, in_=kf), WM1, WM0)
            A("dve", lambda h: h.tensor_copy(out=kf, in_=ki), WM0, WM1)
            A("dve", lambda h: h.scalar_tensor_tensor(out=a2f, in0=kf, scalar=float(-2.0 * np.pi), in1=a2f, op0=ALU.mult, op1=ALU.add), WM1 + A2, A2)
            A("dve", lambda h: h.tensor_single_scalar(out=kf, in_=a2f, scalar=float(np.pi), op=ALU.is_gt), A2, WM1)
            A("dve", lambda h: h.scalar_tensor_tensor(out=a2f, in0=kf, scalar=float(-2.0 * np.pi), in1=a2f, op0=ALU.mult, op1=ALU.add), WM1 + A2, A2)
            A("act", lambda h: h.activation(out=tab[:].rearrange("p s c -> p (s c)"), in_=a2f, func=AF.Sin, scale=0.999999), A2, ["tab"])
            A("pool", lambda h: h.dma_start(out=rope_s.rearrange("s p c -> p s c"), in_=tab[:]), ["tab"], ["rope_s"], dma=True)

            with nc.Block() as block:
                P.emit(block)
        build_phase1p(locals())
        with ExitStack() as ekv:
            KT = sb("KT", [128, 2, NKEY], BF16, ekv)
            Vx = sb("Vx", [128, NKT, 2, 130], BF16, ekv)
            build_phase2(locals())
        build_phase3p(locals())
    return nc


def norm_to_hT(L, P, NT, xres, n_i, kind, dst, dkey, xn, junk, ss, tt, rstd):
    A = P.add
    gsc, modc, mhalf, ident, ps = L["gsc"], L["modc"], L["mhalf"], L["ident"], L["ps"]
    sh_m = (0, 3, 6)[n_i]
    T = NT * 128
    for s in range(NT):
        A("act", lambda h, s=s: h.activation(out=xn[:, s, :], in_=xres[:, s, :], func=AF.Square, accum_out=ss[:, s:s + 1]),
          [("xres", s)], [("ss", s), ("xn", s)])
    A("dve", lambda h: h.tensor_scalar(out=tt[:, 0:NT], in0=ss[:, 0:NT], scalar1=1.0 / D, scalar2=EPS, op0=ALU.mult, op1=ALU.add),
      [("ss", s) for s in range(NT)], ["tt"])
    A("pool", lambda h: h.tensor_tensor(out=rstd[:, 0:NT], in0=tt[:, 0:NT], in1=mhalf[:, 0:NT], op=ALU.pow), ["tt"], ["rstd"])
    for s in range(NT):
        A("dve", lambda h, s=s: h.tensor_scalar(out=xn[:, s, :], in0=xres[:, s, :], scalar1=rstd[:, s:s + 1], scalar2=None, op0=ALU.mult),
          [("xres", s), "rstd"], [("xn", s)])
    for s in range(NT):
        def tr(h, s=s):
            for k in range(8):
                o = ps[:, k // 2, :].bitcast(BF16)[:, (k % 2) * 512 + s * 128:(k % 2) * 512 + (s + 1) * 128]
                i = h.transpose(out=o, in_=xn[:, s, k * 128:(k + 1) * 128], identity=ident[:])
            return i
        A("pe", tr, [("xn", s)], [("ps", b) for b in range(4)])
    for k in range(8):
        A("act", lambda h, k=k: h.activation(out=dst[:, k, 0:T], in_=ps[:, k // 2, :].bitcast(BF16)[:, (k % 2) * 512:(k % 2) * 512 + T],
                                             func=AF.Identity, bias=modc[:, sh_m, k, kind:kind + 1], scale=gsc[:, n_i, k, kind:kind + 1]),
          [("ps", k // 2)], [(dkey, k)])


def ffn_core(L, P, NT, ring, S_I, S_O, ext_i, ext_o, hT, actT, sa, tmp, xres, gate_bc):
    A = P.add
    ps = L["ps"]
    T = NT * 128
    for jj in range(11):
        slot, skey = ring.load(P, S_I[jj], ext=ext_i)
        sv = slot.rearrange("p (k c) -> p k c", k=8)
        base = (jj % 2) * 4
        for m4 in range(4):
            def mm(h, m4=m4, sv=sv, base=base):
                for k in range(8):
                    i = h.matmul(ps[:, base + m4, 0:T], lhsT=sv[:, k, m4 * 128:(m4 + 1) * 128], rhs=hT[:, k, 0:T], start=(k == 0), stop=(k == 7))
                return i
            A("pe", mm, [skey] + [("hT", k) for k in range(8)], [("ps", base + m4)])
        for t in range(2):
            j = 2 * jj + t
            A("act", lambda h, t=t, base=base: h.activation(out=sa[:, t, 0:T], in_=ps[:, base + t, 0:T], func=AF.Silu),
              [("ps", base + t)], [("sa", t)])
            A("dve", lambda h, t=t, base=base, j=j: h.tensor_tensor(out=actT[:, j, 0:T], in0=sa[:, t, 0:T], in1=ps[:, base + 2 + t, 0:T], op=ALU.mult),
              [("sa", t), ("ps", base + 2 + t)], [("actT", j)])
    for c in range(6):
        nj = 4 if c < 5 else 2
        slot, skey = ring.load(P, S_O[c][:, 0:nj * 1024], ncols=nj * 1024, ext=ext_o)
        sv = slot.rearrange("p (j d) -> p j d", j=4)
        for jj in range(nj):
            j = 4 * c + jj

            def mm(h, j=j, jj=jj, sv=sv):
                for s in range(NT):
                    for hh in range(2):
                        i = h.matmul(ps[:, 2 * s + hh, :], lhsT=actT[:, j, s * 128:(s + 1) * 128], rhs=sv[:, jj, hh * 512:(hh + 1) * 512],
                                     start=(j == 0), stop=(j == NJ - 1))
                return i
            A("pe", mm, [skey, ("actT", j)], [("ps", b) for b in range(2 * NT)])
    for s in range(NT):
        for hh in range(2):
            b = 2 * s + hh
            A("dve", lambda h, b=b, hh=hh: h.tensor_tensor(out=tmp[:, b % 2, :], in0=ps[:, b, :], in1=gate_bc[:, hh * 512:(hh + 1) * 512], op=ALU.mult),
              [("ps", b), "gate"], [("tmp", b % 2)])
            A("pool", lambda h, b=b, s=s, hh=hh: h.tensor_tensor(out=xres[:, s, hh * 512:(hh + 1) * 512], in0=tmp[:, b % 2, :],
                                                                 in1=xres[:, s, hh * 512:(hh + 1) * 512], op=ALU.add),
              [("tmp", b % 2), ("xres", s)], [("xres", s)])


def headnorm_rope(L, P, src, skeys, H, gbc, tabv, dst, dkeys, W):
    A = P.add
    mhalf = L["mhalf"]
    sqj, ssq, tq, rq, qn, t1, t2, t3, t4 = W
    A("act", lambda h: h.activation(out=sqj[:, 0:H, :], in_=src, func=AF.Square), skeys, ["sqj"])
    A("dve", lambda h: h.tensor_reduce(out=ssq[:, 0:H], in_=sqj[:, 0:H, :], axis=AX.X, op=ALU.add), ["sqj"], ["ssq"])
    A("dve", lambda h: h.tensor_scalar(out=tq[:, 0:H], in0=ssq[:, 0:H], scalar1=1.0 / 128.0, scalar2=EPS, op0=ALU.mult, op1=ALU.add), ["ssq"], ["tq"])
    A("pool", lambda h: h.tensor_tensor(out=rq[:, 0:H], in0=tq[:, 0:H], in1=mhalf[:, 0:H], op=ALU.pow), ["tq"], ["rq"])
    A("dve", lambda h: h.tensor_tensor(out=qn[:, 0:H, :], in0=src, in1=rq[:, 0:H].unsqueeze(2).to_broadcast([128, H, 128]), op=ALU.mult),
      list(skeys) + ["rq"], ["qn"])
    if tabv is None:
        A("pool", lambda h: h.tensor_tensor(out=dst, in0=qn[:, 0:H, :], in1=gbc[:].unsqueeze(1).to_broadcast([128, H, 128]), op=ALU.mult),
          ["qn"], dkeys)
        return
    A("pool", lambda h: h.tensor_tensor(out=qn[:, 0:H, :], in0=qn[:, 0:H, :], in1=gbc[:].unsqueeze(1).to_broadcast([128, H, 128]), op=ALU.mult),
      ["qn"], ["qn"])
    q5 = qn[:, 0:H, :].rearrange("p h (a t f) -> p h a t f", a=2, t=2, f=32)
    d5 = dst.rearrange("p h (a t f) -> p h a t f", a=2, t=2, f=32)
    x1, x2 = q5[:, :, :, 0, :], q5[:, :, :, 1, :]
    cb = tabv[:, 0:64].rearrange("p (a f) -> p a f", a=2).unsqueeze(1).to_broadcast([128, H, 2, 32])
    sbb = tabv[:, 64:128].rearrange("p (a f) -> p a f", a=2).unsqueeze(1).to_broadcast([128, H, 2, 32])
    A("dve", lambda h: h.tensor_tensor(out=t1[:, 0:H], in0=x1, in1=cb, op=ALU.mult), ["qn", "tab"], ["t1"])
    A("pool", lambda h: h.tensor_tensor(out=t2[:, 0:H], in0=x2, in1=sbb, op=ALU.mult), ["qn", "tab"], ["t2"])
    A("dve", lambda h: h.tensor_tensor(out=d5[:, :, :, 0, :], in0=t1[:, 0:H], in1=t2[:, 0:H], op=ALU.subtract), ["t1", "t2"], dkeys)
    A("pool", lambda h: h.tensor_tensor(out=t3[:, 0:H], in0=x2, in1=cb, op=ALU.mult), ["qn", "tab"], ["t3"])
    A("dve", lambda h: h.tensor_tensor(out=t4[:, 0:H], in0=x1, in1=sbb, op=ALU.mult), ["qn", "tab"], ["t4"])
    A("pool", lambda h: h.tensor_tensor(out=d5[:, :, :, 1, :], in0=t3[:, 0:H], in1=t4[:, 0:H], op=ALU.add), ["t3", "t4"], dkeys)


def hn_work(nc, stack, H, pfx=""):
    def sb(name, shape, dt):
        return stack.enter_context(nc.sbuf_tensor("t_" + pfx + name, list(shape), dt))
    return (sb("sqj", [128, H, 128], F32), sb("ssq", [128, 8], F32), sb("tq", [128, 8], F32), sb("rq", [128, 8], F32),
            sb("qn", [128, H, 128], F32), sb("t1", [128, H, 2, 32], F32), sb("t2", [128, H, 2, 32], F32),
            sb("t3", [128, H, 2, 32], F32), sb("t4", [128, H, 2, 32], F32))


def build_phase1(L):
    nc, ps = L["nc"], L["ps"]
    NO, NH, NCX, MO, MH, MT, NKT = L["NO"], L["NH"], L["NCX"], L["MO"], L["MH"], L["MT"], L["NKT"]
    KT, Vx, ident, halo_all, meta = L["KT"], L["Vx"], L["ident"], L["halo_all"], L["meta"]
    with ExitStack() as e1:
        def sb(name, shape, dt):
            return e1.enter_context(nc.sbuf_tensor("t_" + name, list(shape), dt))
        P = Prog(nc, L["es"], "p1")
        A = P.add
        ringt = sb("ring1", [128, L["NRING1"], 4096], BF16)
        ring = Ring(ringt, L["NRING1"])
        xres = sb("xres", [128, 4, D], F32)
        xn = sb("xn", [128, 4, D], BF16)
        junk = sb("junk", [128, D], BF16)
        hT = sb("hT", [128, 8, MT], BF16)
        actT = sb("actT", [128, NJ, MT], BF16)
        sa = sb("sa", [128, 2, MT], F32)
        tmp = sb("tmp", [128, 2, MT], F32)
        ss = sb("ss", [128, 4], F32)
        tt = sb("tt", [128, 4], F32)
        rstd = sb("rstd", [128, 4], F32)
        wkv = sb("wkv", [128, 8, 512], BF16)
        g2 = sb("g2", [128, 2, D], F32)
        tabt = sb("tabt", [128, 4, 128], F32)
        kb = sb("kb", [128, 2, 128], BF16)
        W = hn_work(nc, e1, 2, "k_")

        A("sp", lambda h: h.dma_start(out=g2[:], in_=L["gbc_s"][0:2].rearrange("a p d -> p a d")), (), ["gate"], dma=True)
        A("sp", lambda h: h.dma_start(out=wkv[:].rearrange("p k c -> p (k c)"), in_=L["S_KV"][0]), (), ["wkv"], dma=True, ext=L["cext"]("kv"))
        A("pool", lambda h: h.memset(Vx[:, :, :, 128:130], 1.0), (), ["Vones"])

        tiles = [("ctx", 0, NCX // 128)] + [("own", m, 4) for m in range(MO)] + [("oth", m, 4) for m in range(MH)]
        late = list(L["late_casts"])
        per_tile = (len(late) + max(1, len(tiles) - 2) - 1) // max(1, len(tiles) - 2)
        for ti, (kind, m, NT) in enumerate(tiles):
            for _ in range(per_tile if ti < len(tiles) - 1 else len(late)):
                if late:
                    P.add_raw("pool", late.pop(0))
            T = NT * 128
            kd = 1 if kind == "ctx" else 0
            if kind == "ctx":
                src, kt0, sub0 = L["ctx"], 0, None
            elif kind == "own":
                src, kt0, sub0 = L["x_own"][m * MT:(m + 1) * MT, :], (NCX + m * MT) // 128, m * 4
            else:
                src, kt0, sub0 = L["x_oth"][m * MT:(m + 1) * MT, :], (NCX + NO + m * MT) // 128, (NO + m * MT) // 128
            A("act", lambda h, src=src, NT=NT: h.dma_start(out=xres[:, 0:NT, :], in_=src.rearrange("(s p) d -> p s d", p=128)),
              (), [("xres", s) for s in range(NT)], dma=True)
            if sub0 is not None:
                A("sp", lambda h, sub0=sub0: h.dma_start(out=tabt[:], in_=L["rope_s"][sub0:sub0 + 4].rearrange("s p c -> p s c")),
                  (), ["tab"], dma=True)
            norm_to_hT(L, P, NT, xres, 0, kd, hT, "hT", xn, junk, ss, tt, rstd)
            ffn_core(L, P, NT, ring, L["S_1I"], L["S_1O"], L["cext"]("1i"), L["cext"]("1o"), hT, actT, sa, tmp, xres, g2[:, kd, :])
            norm_to_hT(L, P, NT, xres, 1, kd, hT, "hT", xn, junk, ss, tt, rstd)
            if kind == "own":
                A("pool", lambda h, m=m: h.dma_start(out=L["x1_s"][m * MT:(m + 1) * MT, :].rearrange("(s p) d -> p s d", p=128), in_=xres[:]),
                  [("xres", s) for s in range(4)], [("x1_s", m)], dma=True)
                A("pool", lambda h, m=m: h.dma_start(out=L["h2T_s"][m], in_=hT[:].rearrange("p k t -> p (k t)")),
                  [("hT", k) for k in range(8)], [("h2T_s", m)], dma=True)
            if kind != "ctx":
                hi = m if kind == "own" else MO + m
                A("pool", lambda h, hi=hi: h.tensor_copy(out=halo_all[:, hi, :, 0], in_=hT[:, :, 0]), [("hT", k) for k in range(8)], ["halo"])
                A("pool", lambda h, hi=hi: h.tensor_copy(out=halo_all[:, hi, :, 1], in_=hT[:, :, MT - 1]), [("hT", k) for k in range(8)], ["halo"])
            for s in range(NT):
                bank = 4 + (s % 2)
                kt = kt0 + s

                def mmkv(h, s=s, bank=bank):
                    for k in range(8):
                        i = h.matmul(ps[:, bank, :], lhsT=hT[:, k, s * 128:(s + 1) * 128], rhs=wkv[:, k, :], start=(k == 0), stop=(k == 7))
                    return i
                A("pe", mmkv, [("hT", k) for k in range(8)] + ["wkv"], [("ps", bank)])
                A("act", lambda h, bank=bank, kt=kt: h.activation(out=Vx[:, kt, :, 0:128], in_=ps[:, bank, 256:512].rearrange("p (g d) -> p g d", g=2),
                                                                  func=AF.Copy), [("ps", bank)], [("V", kt)])
                headnorm_rope(L, P, ps[:, bank, 0:256].rearrange("p (g d) -> p g d", g=2), [("ps", bank)], 2, L["kgb"],
                              None if kind == "ctx" else tabt[:, s, :], kb[:], ["kb"], W)

                def trk(h):
                    for g in range(2):
                        i = h.transpose(out=ps[:, 6, :].bitcast(BF16)[:, g * 128:(g + 1) * 128], in_=kb[:, g, :], identity=ident[:])
                    return i
                A("pe", trk, ["kb"], [("ps", 6)])
                A("dve", lambda h, kt=kt: h.tensor_copy(out=KT[:, :, kt * 128:(kt + 1) * 128],
                                                        in_=ps[:, 6, :].bitcast(BF16)[:, 0:256].rearrange("p (g t) -> p g t", g=2)),
                  [("ps", 6)], [("KT", kt)])
        with nc.Block() as block:
            P.emit(block)


def build_phase2(L):
    nc, ps = L["nc"], L["ps"]
    NO, NH, NCX, MO, MH, MT, NKT = L["NO"], L["NH"], L["NCX"], L["MO"], L["MH"], L["MT"], L["NKT"]
    KT, Vx, ident, halo_all, meta, convc, nbias = L["KT"], L["Vx"], L["ident"], L["halo_all"], L["meta"], L["convc"], L["nbias"]
    cext = L["cext"]
    with ExitStack() as e2:
        def sb(name, shape, dt):
            return e2.enter_context(nc.sbuf_tensor("t_" + name, list(shape), dt))
        P = Prog(nc, L["es"], "p2")
        A = P.add
        ringt = sb("ring2", [128, L["NRING2"], 4096], BF16)
        ring = Ring(ringt, L["NRING2"])
        h2T = sb("h2T", [128, 8, MT], BF16)
        hal = sb("hal", [128, 8, 2], BF16)
        bufA = sb("bufA", [128, 4, D], F32)
        ycT = bufA[:, 0:2, :].bitcast(BF16).rearrange("p a (k t) -> p (a k) t", t=MT) if False else None
        bA16 = bufA[:].rearrange("p a d -> p (a d)").bitcast(BF16)
        ycT = bA16[:, 0:4096].rearrange("p (k t) -> p k t", k=8)
        OT = bA16[:, 4096:8192].rearrange("p (k t) -> p k t", k=8)
        xres = bufA
        qT = sb("qT", [128, 8, MT], BF16)
        mT = qT
        qb = sb("qb", [128, 4, 8, 128], BF16)
        wq = sb("wq", [128, 2, 4096], BF16)
        bsb = sb("bsb", [128, 2, MT], F32)
        tabt = sb("tabt2", [128, 4, 128], F32)
        W = hn_work(nc, e2, 8, "q_")
        csb = sb("csb", [128, 2, MT], F32)
        u = sb("u", [128, 2, MT + 2], F32)
        hps = sb("hps", [128, 2, 4], F32)
        PT = sb("PT", [128, 3, 1024], BF16)
        rden = sb("rden", [128, 4, 1], F32)
        on = sb("on", [128, 4, 128], BF16)
        sg = sb("sg", [128, 4, MT], F32)
        tm = sb("tm", [128, 4, MT], F32)
        cv = tm
        tmp = tm
        g5 = sb("g5", [128, D], F32)

        A("sp", lambda h: h.dma_start(out=g5[:], in_=L["gbc_s"][2]), (), ["gate"], dma=True)
        A("sp", lambda h: h.dma_start(out=wq[:], in_=L["S_Q"].rearrange("c p n -> p c n")), (), ["wq"], dma=True, ext=cext("q"))
        for g_ in range(2):
            A("sp", lambda h, g_=g_: h.dma_start(out=KT[:, g_, :], in_=L["kt_s"][:, g_, :]), (), [("KTl", g_)], dma=True)
        nq = 4
        for q_ in range(nq):
            a_, b_ = (NKT * q_) // nq, (NKT * (q_ + 1)) // nq
            A("sp", lambda h, a_=a_, b_=b_: h.dma_start(out=Vx[:, a_:b_], in_=L["vx_s"][:, a_:b_]), (), [("Vxl", q_)], dma=True)
        KVL = [("KTl", 0), ("KTl", 1)] + [("Vxl", q_) for q_ in range(nq)]
        BUFA = [("bufA", i) for i in range(4)]

        for m in range(MO):
            A("act", lambda h, m=m: h.dma_start(out=h2T[:].rearrange("p k t -> p (k t)"), in_=L["h2T_s"][m]), (), [("h2T", k) for k in range(8)], dma=True)
            A("sp", lambda h, m=m: h.dma_start(out=tabt[:], in_=L["rope_s"][m * 4:m * 4 + 4].rearrange("s p c -> p s c")), (), ["tab"], dma=True)
            li = (m - 1) if m > 0 else (MO + MH - 1)
            ri = (m + 1) if m < MO - 1 else MO
            if m == 0:
                A("dve", lambda h, li=li: h.tensor_scalar(out=hal[:, :, 0], in0=halo_all[:, li, :, 1], scalar1=meta[:, 3:4], scalar2=None, op0=ALU.mult), (), [("hal", 0)])
            else:
                A("dve", lambda h, li=li: h.tensor_copy(out=hal[:, :, 0], in_=halo_all[:, li, :, 1]), (), [("hal", 0)])
            if m == MO - 1:
                A("dve", lambda h, ri=ri: h.tensor_scalar(out=hal[:, :, 1], in0=halo_all[:, ri, :, 0], scalar1=meta[:, 4:5], scalar2=None, op0=ALU.mult), (), [("hal", 1)])
            else:
                A("dve", lambda h, ri=ri: h.tensor_copy(out=hal[:, :, 1], in_=halo_all[:, ri, :, 0]), (), [("hal", 1)])
            H2K = [("h2T", k) for k in range(8)]

            qv = [wq[:, 0].rearrange("p (k c) -> p k c", k=8), wq[:, 1].rearrange("p (k c) -> p k c", k=8)]

            def emit_mmq(s):
                b0 = 2 * (s % 2)

                def mmq(h, s=s, b0=b0):
                    for k in range(8):
                        for hh in range(2):
                            i = h.matmul(ps[:, b0 + hh, :], lhsT=h2T[:, k, s * 128:(s + 1) * 128], rhs=qv[hh][:, k, :], start=(k == 0), stop=(k == 7))
                    return i
                A("pe", mmq, H2K + ["wq"], [("ps", b0), ("ps", b0 + 1)])
                headnorm_rope(L, P, ps[:, b0:b0 + 2, :].rearrange("p a (h d) -> p (a h) d", d=128), [("ps", b0), ("ps", b0 + 1)], 8, L["qgb"],
                              tabt[:, s, :], qb[:, s], [("qb", s)], W)

            def emit_trq(s):
                tb = (0, 2, 1, 3)[s]

                def trq(h, s=s, tb=tb):
                    for hd in range(8):
                        i = h.transpose(out=ps[:, tb, :].bitcast(BF16)[:, hd * 128:(hd + 1) * 128], in_=qb[:, s, hd, :], identity=ident[:])
                    return i
                A("pe", trq, [("qb", s)], [("ps", tb)])
                A("dve", lambda h, s=s, tb=tb: h.tensor_copy(out=qT[:, :, s * 128:(s + 1) * 128], in_=ps[:, tb, :].bitcast(BF16).rearrange("p (h t) -> p h t", h=8)),
                  [("ps", tb)], [("qT", s)])

            conv_w = {}

            def emit_conv(j):
                half, jj = j // 4, j % 4
                if jj == 0:
                    sc_, kc = ring.load(P, L["S_C"][half], ext=cext("c"))
                    sv_, kv = ring.load(P, L["S_V"][half], ext=cext("v"))
                    sb_, kb_ = ring.load(P, L["S_B"][half], ext=cext("b"))
                    conv_w["v"] = tuple(t.rearrange("p (k c) -> p k c", k=8) for t in (sc_, sv_, sb_))
                    conv_w["k"] = (kc, kv, kb_)
                scv, svv, sbv = conv_w["v"]
                kc, kv, kb_ = conv_w["k"]
                par = j % 2
                bC, bV, bB, bH = 4, 5, 6, 7

                def mmc(h, jj=jj, scv=scv, svv=svv):
                    for k in range(8):
                        w = scv[:, k, jj * 128:(jj + 1) * 128]
                        h.matmul(ps[:, bC, :], lhsT=w, rhs=h2T[:, k, :], start=(k == 0), stop=(k == 7))
                        h.matmul(ps[:, bH, 0:2], lhsT=w, rhs=hal[:, k, :], start=(k == 0), stop=(k == 7))
                    for k in range(8):
                        w = svv[:, k, jj * 128:(jj + 1) * 128]
                        h.matmul(ps[:, bV, :], lhsT=w, rhs=h2T[:, k, :], start=(k == 0), stop=(k == 7))
                        i = h.matmul(ps[:, bH, 2:4], lhsT=w, rhs=hal[:, k, :], start=False, stop=(k == 7), skip_group_check=True)
                    return i
                A("pe", mmc, H2K + [kc, kv, ("hal", 0), ("hal", 1)], [("ps", bC), ("ps", bV), ("ps", bH)])

                def mmb(h, jj=jj, sbv=sbv):
                    for k in range(8):
                        i = h.matmul(ps[:, bB, :], lhsT=sbv[:, k, jj * 128:(jj + 1) * 128], rhs=h2T[:, k, :], start=(k == 0), stop=(k == 7))
                    return i
                A("pe", mmb, H2K + [kb_], [("ps", bB)])
                A("act", lambda h, par=par: h.activation(out=csb[:, par, :], in_=ps[:, bC, :], func=AF.Copy), [("ps", bC)], [("csb", par)])
                A("act", lambda h, par=par: h.activation(out=hps[:, par, 0:2], in_=ps[:, bH, 0:2], func=AF.Copy), [("ps", bH)], [("hps", par)])
                A("act", lambda h, par=par: h.activation(out=bsb[:, par, :], in_=ps[:, bB, :], func=AF.Copy), [("ps", bB)], [("bsb", par)])
                A("dve", lambda h, par=par: h.tensor_tensor(out=u[:, par, 1:MT + 1], in0=csb[:, par, :], in1=ps[:, bV, :], op=ALU.mult),
                  [("csb", par), ("ps", bV)], [("u", par)])
                A("dve", lambda h, par=par: h.tensor_tensor(out=u[:, par, 0:MT + 2:MT + 1], in0=hps[:, par, 0:2], in1=ps[:, bH, 2:4], op=ALU.mult),
                  [("hps", par), ("ps", bH)], [("u", par)])
                A("act", lambda h, par=par, j=j: h.activation(out=cv[:, par, :], in_=u[:, par, 0:MT], func=AF.Copy, scale=convc[:, 0, j:j + 1]),
                  [("u", par)], [("tm", par)])
                A("dve", lambda h, par=par, j=j: h.scalar_tensor_tensor(out=cv[:, par, :], in0=u[:, par, 1:MT + 1], scalar=convc[:, 1, j:j + 1], in1=cv[:, par, :],
                                                                        op0=ALU.mult, op1=ALU.add), [("u", par), ("tm", par)], [("tm", par)])
                A("dve", lambda h, par=par, j=j: h.scalar_tensor_tensor(out=cv[:, par, :], in0=u[:, par, 2:MT + 2], scalar=convc[:, 2, j:j + 1], in1=cv[:, par, :],
                                                                        op0=ALU.mult, op1=ALU.add), [("u", par), ("tm", par)], [("tm", par)])
                A("pool", lambda h, par=par, j=j: h.tensor_tensor(out=ycT[:, j, :], in0=cv[:, par, :], in1=bsb[:, par, :], op=ALU.mult),
                  [("tm", par), ("bsb", par)], BUFA[0:2])

            emit_mmq(0)
            emit_mmq(1)
            emit_conv(0)
            emit_mmq(2)
            emit_conv(1)
            emit_mmq(3)
            emit_conv(2)
            emit_trq(0)
            emit_conv(3)
            emit_trq(1)
            emit_conv(4)
            emit_conv(5)
            emit_trq(2)
            emit_conv(6)
            emit_trq(3)
            emit_conv(7)

            npair = NKT // 2
            its = [(s, g) for s in range(4) for g in range(2)]
            jobs = [(it, pi) for it in range(len(its)) for pi in range(npair)]

            def qk(gp):
                it, pi = jobs[gp]
                s, g = its[it]
                buf = gp % 2
                qmov = qT[:, g * 4:(g + 1) * 4, s * 128:(s + 1) * 128]

                def f(h, pi=pi, buf=buf, g=g, qmov=qmov):
                    for kk in range(2):
                        kt = 2 * pi + kk
                        i = h.matmul(ps[:, 2 * buf + kk, :], lhsT=KT[:, g, kt * 128:(kt + 1) * 128], rhs=qmov, start=True, stop=True)
                    return i
                A("pe", f, [("qT", s)] + KVL, [("ps", 2 * buf), ("ps", 2 * buf + 1)])
                A("act", lambda h, gp=gp, buf=buf: h.activation(out=PT[:, gp % 3, :], in_=ps[:, 2 * buf:2 * buf + 2, :].rearrange("p a b -> p (a b)"),
                                                                func=AF.Exp, bias=nbias[:, 0:1], scale=1.0),
                  [("ps", 2 * buf), ("ps", 2 * buf + 1)], [("PT", gp % 3)])

            def pv(gp):
                it, pi = jobs[gp]
                s, g = its[it]

                def f(h, pi=pi, g=g, gp=gp):
                    for kk in range(2):
                        kt = 2 * pi + kk
                        for hd in range(4):
                            i = h.matmul(ps[:, 4 + hd, 0:129], lhsT=PT[:, gp % 3, kk * 512 + hd * 128:kk * 512 + (hd + 1) * 128],
                                         rhs=Vx[:, kt, g, 0:129], start=(kt == 0), stop=(kt == NKT - 1))
                    return i
                A("pe", f, [("PT", gp % 3)], [("ps", 4 + hd) for hd in range(4)])

            OB = [("ps", 4 + hd) for hd in range(4)]

            def evac(it):
                s, g = its[it]
                A("dve", lambda h: h.reciprocal(out=rden[:], in_=ps[:, 4:8, 128:129]), OB, ["rden"])
                A("dve", lambda h: h.tensor_tensor(out=on[:], in0=ps[:, 4:8, 0:128], in1=rden[:].to_broadcast([128, 4, 128]), op=ALU.mult),
                  OB + ["rden"], ["on"])

                def tro(h):
                    for hd in range(4):
                        i = h.transpose(out=ps[:, 4, :].bitcast(BF16)[:, 512 + hd * 128:512 + (hd + 1) * 128], in_=on[:, hd, :], identity=ident[:])
                    return i
                A("pe", tro, ["on"], [("ps", 4)])
                A("dve", lambda h, s=s, g=g: h.tensor_copy(out=OT[:, g * 4:(g + 1) * 4, s * 128:(s + 1) * 128],
                                                           in_=ps[:, 4, :].bitcast(BF16)[:, 512:1024].rearrange("p (h t) -> p h t", h=4)),
                  [("ps", 4)], BUFA[2:4])

            NJOB = len(jobs)
            qk(0)
            if NJOB > 1:
                qk(1)
            for gp in range(NJOB):
                if gp + 2 < NJOB:
                    qk(gp + 2)
                pv(gp)
                if jobs[gp][1] == npair - 1:
                    evac(jobs[gp][0])

            if L["cfg"].get("dbg") and m == 0:
                dbg = L["dbg_t"]
                A("pool", lambda h: h.dma_start(out=dbg["ycT"], in_=bA16[:, 0:4096]), BUFA, ["d1"], dma=True)
                A("pool", lambda h: h.dma_start(out=dbg["OT"], in_=bA16[:, 4096:8192]), BUFA, ["d2"], dma=True)
                A("pool", lambda h: h.dma_start(out=dbg["qT"], in_=qT[:].rearrange("p k t -> p (k t)")), [("qT", s_) for s_ in range(4)], ["d3"], dma=True)
                A("pool", lambda h: h.dma_start(out=dbg["KT"], in_=KT[:].rearrange("p g t -> p (g t)")), (), ["d4"], dma=True)
                A("pool", lambda h: h.dma_start(out=dbg["Vx"], in_=Vx[:].rearrange("p a g d -> p (a g d)")), (), ["d5"], dma=True)
            for jj in range(4):
                sg_, kg_ = ring.load(P, L["S_G"][jj], ext=cext("g"))
                sr_, kr_ = ring.load(P, L["S_R"][jj], ext=cext("r"))
                sgv = sg_.rearrange("p (k c) -> p k c", k=8)
                srv = sr_.rearrange("p (k c) -> p k c", k=8)
                for t in range(2):
                    j = 2 * jj + t
                    base = 4 * (j % 2)

                    def mmg(h, sgv=sgv, srv=srv, t=t, base=base):
                        for c2 in range(2):
                            for k in range(8):
                                h.matmul(ps[:, base + c2, :], lhsT=sgv[:, k, (2 * t + c2) * 128:(2 * t + c2 + 1) * 128], rhs=h2T[:, k, :], start=(k == 0), stop=(k == 7))
                        for c2 in range(2):
                            src = ycT if c2 == 0 else OT
                            for k in range(8):
                                i = h.matmul(ps[:, base + 2 + c2, :], lhsT=srv[:, k, (2 * t + c2) * 128:(2 * t + c2 + 1) * 128], rhs=src[:, k, :], start=(k == 0), stop=(k == 7))
                        return i
                    A("pe", mmg, H2K + [kg_, kr_] + BUFA, [("ps", base + b_) for b_ in range(4)])
                    for c2 in range(2):
                        sl = 2 * (j % 2) + c2
                        A("act", lambda h, sl=sl, base=base, c2=c2: h.activation(out=sg[:, sl, :], in_=ps[:, base + c2, :], func=AF.Sigmoid), [("ps", base + c2)], [("sg", sl)])
                        A("dve", lambda h, sl=sl, base=base, c2=c2: h.tensor_tensor(out=tm[:, sl, :], in0=sg[:, sl, :], in1=ps[:, base + 2 + c2, :], op=ALU.mult),
                          [("sg", sl), ("ps", base + 2 + c2)], [("tm", sl)])
                    s0 = 2 * (j % 2)
                    A("pool", lambda h, j=j, s0=s0: h.tensor_tensor(out=mT[:, j, :], in0=tm[:, s0, :], in1=tm[:, s0 + 1, :], op=ALU.add),
                      [("tm", s0), ("tm", s0 + 1)], [("qT", s_) for s_ in range(4)])

            A("act", lambda h, m=m: h.dma_start(out=xres[:], in_=L["x1_s"][m * MT:(m + 1) * MT, :].rearrange("(s p) d -> p s d", p=128)),
              (), BUFA, dma=True)
            o0, ko0 = ring.load(P, L["S_O"][0], ext=cext("o"))
            o1, ko1 = ring.load(P, L["S_O"][1], ext=cext("o"))
            ov = [o0.rearrange("p (k c) -> p k c", k=8), o1.rearrange("p (k c) -> p k c", k=8)]
            for s in range(4):
                b0 = 2 * (s % 2)

                def mmo(h, s=s, b0=b0, ov=ov):
                    for k in range(8):
                        for hh in range(2):
                            i = h.matmul(ps[:, b0 + hh, :], lhsT=mT[:, k, s * 128:(s + 1) * 128], rhs=ov[hh][:, k, :], start=(k == 0), stop=(k == 7))
                    return i
                A("pe", mmo, [("qT", s_) for s_ in range(4)] + [ko0, ko1], [("ps", b0), ("ps", b0 + 1)])
                for hh in range(2):
                    A("dve", lambda h, b0=b0, hh=hh: h.tensor_tensor(out=tmp[:, hh, :], in0=ps[:, b0 + hh, :], in1=g5[:, hh * 512:(hh + 1) * 512], op=ALU.mult),
                      [("ps", b0 + hh), "gate"], [("tm", hh)])
                    A("pool", lambda h, s=s, hh=hh: h.tensor_tensor(out=xres[:, s, hh * 512:(hh + 1) * 512], in0=tmp[:, hh, :],
                                                                   in1=xres[:, s, hh * 512:(hh + 1) * 512], op=ALU.add),
                      [("tm", hh)] + BUFA, BUFA)
            A("pool", lambda h, m=m: h.dma_start(out=L["xm_s"][m * MT:(m + 1) * MT, :].rearrange("(s p) d -> p s d", p=128), in_=xres[:]),
              BUFA, [("xm_s", m)], dma=True)
        with nc.Block() as block:
            P.emit(block)


def build_phase3(L):
    nc, ps = L["nc"], L["ps"]
    NO, MO, MT = L["NO"], L["MO"], L["MT"]
    with ExitStack() as e3:
        def sb(name, shape, dt):
            return e3.enter_context(nc.sbuf_tensor("t_" + name, list(shape), dt))
        P = Prog(nc, L["es"], "p3")
        A = P.add
        ringt = sb("ring3", [128, L["NRING3"], 4096], BF16)
        ring = Ring(ringt, L["NRING3"])
        xres = sb("xres3", [128, 4, D], F32)
        xn = sb("xn3", [128, 4, D], BF16)
        junk = sb("junk3", [128, D], BF16)
        hT = sb("hT3", [128, 8, MT], BF16)
        actT = sb("actT3", [128, NJ, MT], BF16)
        sa = sb("sa3", [128, 2, MT], F32)
        tmp = sb("tmp3", [128, 2, MT], F32)
        ss = sb("ss3", [128, 4], F32)
        tt = sb("tt3", [128, 4], F32)
        rstd = sb("rstd3", [128, 4], F32)
        g8 = sb("g8", [128, D], F32)
        fgb = sb("fgb", [128, D], F32)
        yo = sb("yo", [128, 2, D], F32)
        A("sp", lambda h: h.dma_start(out=g8[:], in_=L["gbc_s"][3]), (), ["gate"], dma=True)
        A("sp", lambda h: h.dma_start(out=fgb[:], in_=L["fg_in"].partition_broadcast(128)), (), ["fgb"], dma=True)
        for m in range(MO):
            A("act", lambda h, m=m: h.dma_start(out=xres[:], in_=L["xm_s"][m * MT:(m + 1) * MT, :].rearrange("(s p) d -> p s d", p=128)),
              (), [("xres", s) for s in range(4)], dma=True)
            norm_to_hT(L, P, 4, xres, 2, 0, hT, "hT", xn, junk, ss, tt, rstd)
            ffn_core(L, P, 4, ring, L["S_3I"], L["S_3O"], L["cext"]("3i"), L["cext"]("3o"), hT, actT, sa, tmp, xres, g8[:])
            for s in range(4):
                A("act", lambda h, s=s: h.activation(out=xn[:, s, :], in_=xres[:, s, :], func=AF.Square, accum_out=ss[:, s:s + 1]),
                  [("xres", s)], [("ss", s), ("xn", s)])
            A("dve", lambda h: h.tensor_scalar(out=tt[:], in0=ss[:], scalar1=1.0 / D, scalar2=EPS, op0=ALU.mult, op1=ALU.add),
              [("ss", s) for s in range(4)], ["tt"])
            A("pool", lambda h: h.tensor_tensor(out=rstd[:], in0=tt[:], in1=L["mhalf"][:, 0:4], op=ALU.pow), ["tt"], ["rstd"])
            for s in range(4):
                A("dve", lambda h, s=s: h.scalar_tensor_tensor(out=yo[:, s % 2, :], in0=xres[:, s, :], scalar=rstd[:, s:s + 1], in1=fgb[:],
                                                               op0=ALU.mult, op1=ALU.mult), [("xres", s), "rstd", "fgb"], [("yo", s % 2)])
                A("pool", lambda h, s=s, m=m: h.dma_start(out=L["y_out"][m * MT + s * 128:m * MT + (s + 1) * 128, :], in_=yo[:, s % 2, :]),
                  [("yo", s % 2)], [("y", m, s)], dma=True)
        with nc.Block() as block:
            P.emit(block)


def norm_to_hT2(L, P, NT, xres, xk, n_i, kind, dst, dkey, W, pk, bank):
    A = P.add
    gsc, modc, mhalf, ident, ps = L["gsc"], L["modc"], L["mhalf"], L["ident"], L["ps"]
    xn, junk, ss, tt, rstd = W
    sh_m = (0, 3, 6)[n_i]
    for s in range(NT):
        A("act", lambda h, s=s: h.activation(out=junk[:, s % 2, :], in_=xres[:, s, :], func=AF.Square, accum_out=ss[:, s:s + 1]),
          [(xk, s)], [(pk + "ss", s), (pk + "junk", s % 2)])
        if s % 2 == 1:
            yield
    A("dve", lambda h: h.tensor_scalar(out=tt[:, 0:NT], in0=ss[:, 0:NT], scalar1=1.0 / D, scalar2=EPS, op0=ALU.mult, op1=ALU.add),
      [(pk + "ss", s) for s in range(NT)], [pk + "tt"])
    A("pool", lambda h: h.tensor_tensor(out=rstd[:, 0:NT], in0=tt[:, 0:NT], in1=mhalf[:, 0:NT], op=ALU.pow), [pk + "tt"], [pk + "rstd"])
    yield
    for s in range(NT):
        A("dve", lambda h, s=s: h.tensor_scalar(out=xn[:, s % 2, :], in0=xres[:, s, :], scalar1=rstd[:, s:s + 1], scalar2=None, op0=ALU.mult),
          [(xk, s), pk + "rstd"], [(pk + "xn", s % 2)])
        yield

        def tr(h, s=s):
            for k in range(8):
                i = h.transpose(out=ps[:, bank, :].bitcast(BF16)[:, k * 128:(k + 1) * 128], in_=xn[:, s % 2, k * 128:(k + 1) * 128], identity=ident[:])
            return i
        A("pe", tr, [(pk + "xn", s % 2)], [("ps", bank)])
        yield
        for k in range(8):
            src = ps[:, bank, :].bitcast(BF16)[:, k * 128:(k + 1) * 128]
            o = dst[:, k, s * 128:(s + 1) * 128]
            A("act", lambda h, k=k, o=o, src=src: h.activation(out=o, in_=src, func=AF.Identity, bias=modc[:, sh_m, k, kind:kind + 1],
                                                               scale=gsc[:, n_i, k, kind:kind + 1]), [("ps", bank)], [(dkey, k)])
            if k == 3:
                yield


def norm_work(nc, stack, pfx):
    def sb(name, shape, dt):
        return stack.enter_context(nc.sbuf_tensor("t_" + pfx + name, list(shape), dt))
    return (sb("xn", [128, 2, D], BF16), sb("junk", [128, 2, D], BF16), sb("ss", [128, 4], F32), sb("tt", [128, 4], F32), sb("rstd", [128, 4], F32))


def drain(gens):
    for g in gens:
        for _ in g:
            pass


def ffn_B(L, P, T, ring, S_I, ext_i, hT, hkey, actT, sa, sched):
    A = P.add
    ps = L["ps"]
    gens = []
    for jj in range(11):
        slot, skey = ring.load(P, S_I[jj], ext=ext_i)
        sv = slot.rearrange("p (k c) -> p k c", k=8)
        for t in range(2):
            q = 2 * jj + t
            gens.extend(sched.get(q, []))
            for g in list(gens):
                try:
                    next(g)
                except StopIteration:
                    gens.remove(g)
            st = q % 2
            ba, bb = 2 * st, 2 * st + 1

            def mm(h, t=t, sv=sv, ba=ba, bb=bb):
                for k in range(8):
                    h.matmul(ps[:, ba, 0:T], lhsT=sv[:, k, t * 128:(t + 1) * 128], rhs=hT[:, k, 0:T], start=(k == 0), stop=(k == 7))
                for k in range(8):
                    i = h.matmul(ps[:, bb, 0:T], lhsT=sv[:, k, (2 + t) * 128:(3 + t) * 128], rhs=hT[:, k, 0:T], start=(k == 0), stop=(k == 7))
                return i
            A("pe", mm, [skey] + [(hkey, k) for k in range(8)], [("ps", ba), ("ps", bb)])
            A("act", lambda h, q=q, ba=ba: h.activation(out=sa[:, q % 2, 0:T], in_=ps[:, ba, 0:T], func=AF.Silu), [("ps", ba)], [("sa", q % 2)])
            A("dve", lambda h, q=q, bb=bb: h.tensor_tensor(out=actT[:, q, 0:T], in0=sa[:, q % 2, 0:T], in1=ps[:, bb, 0:T], op=ALU.mult),
              [("sa", q % 2), ("ps", bb)], [("actT", q)])
    drain(gens)


def ffn_C(L, P, NT, ring, S_O, ext_o, actT):
    A = P.add
    ps = L["ps"]
    for c in range(6):
        nj = 4 if c < 5 else 2
        slot, skey = ring.load(P, S_O[c][:, 0:nj * 1024], ncols=nj * 1024, ext=ext_o)
        sv = slot.rearrange("p (j d) -> p j d", j=4)
        for jj in range(nj):
            j = 4 * c + jj

            def mm(h, j=j, jj=jj, sv=sv):
                for s in range(NT):
                    for hh in range(2):
                        i = h.matmul(ps[:, 2 * s + hh, :], lhsT=actT[:, j, s * 128:(s + 1) * 128], rhs=sv[:, jj, hh * 512:(hh + 1) * 512],
                                     start=(j == 0), stop=(j == NJ - 1))
                return i
            A("pe", mm, [skey, ("actT", j)], [("ps", b) for b in range(2 * NT)])


def ffn_evac(L, P, NT, tmp, xres, xk, gate_bc):
    A = P.add
    ps = L["ps"]
    for s in range(NT):
        for hh in range(2):
            b = 2 * s + hh
            A("dve", lambda h, b=b, hh=hh: h.tensor_tensor(out=tmp[:, b % 2, :], in0=ps[:, b, :], in1=gate_bc[:, hh * 512:(hh + 1) * 512], op=ALU.mult),
              [("ps", b), "gate"], [("tmp", b % 2)])
            A("pool", lambda h, b=b, s=s, hh=hh: h.tensor_tensor(out=xres[:, s, hh * 512:(hh + 1) * 512], in0=tmp[:, b % 2, :],
                                                                 in1=xres[:, s, hh * 512:(hh + 1) * 512], op=ALU.add),
              [("tmp", b % 2), (xk, s)], [(xk, s)])


def build_phase1p(L):
    nc, ps = L["nc"], L["ps"]
    NO, NH, NCX, MO, MH, MT, NKT = L["NO"], L["NH"], L["NCX"], L["MO"], L["MH"], L["MT"], L["NKT"]
    ident, halo_all, meta = L["ident"], L["halo_all"], L["meta"]
    kt_s, vx_s = L["kt_s"], L["vx_s"]
    with ExitStack() as e1:
        def sb(name, shape, dt):
            return e1.enter_context(nc.sbuf_tensor("t_" + name, list(shape), dt))
        P = Prog(nc, L["es"], "p1")
        A = P.add
        NR = L["NRING1"]
        ringt = sb("ring1", [128, NR, 4096], BF16)
        ring = Ring(ringt, NR)
        xresb = [sb("xres_a", [128, 4, D], F32), sb("xres_b", [128, 4, D], F32), sb("xres_c", [128, 4, D], F32)]
        hTb = [sb("hT_a", [128, 8, MT], BF16), sb("hT_b", [128, 8, MT], BF16)]
        h2T = sb("h2T1", [128, 8, MT], BF16)
        actT = sb("actT", [128, NJ, MT], BF16)
        sa = sb("sa", [128, 2, MT], F32)
        tmp = sb("tmp", [128, 2, MT], F32)
        WH = norm_work(nc, e1, "nh_")
        WT = norm_work(nc, e1, "nt_")
        wkv = sb("wkv", [128, 8, 512], BF16)
        g2 = sb("g2", [128, 2, D], F32)
        tabt = sb("tabt", [128, 4, 128], F32)
        kb = sb("kb", [128, 2, 128], BF16)
        KTs = sb("KTs", [128, 2, 2, MT], BF16)
        Vs = sb("Vs", [128, 2, 4, 2, 130], BF16)
        W = hn_work(nc, e1, 2, "k_")

        A("sp", lambda h: h.dma_start(out=g2[:], in_=L["gbc_s"][0:2].rearrange("a p d -> p a d")), (), ["gate"], dma=True)
        A("sp", lambda h: h.dma_start(out=wkv[:].rearrange("p k c -> p (k c)"), in_=L["S_KV"][0]), (), ["wkv"], dma=True, ext=L["cext"]("kv"))
        A("pool", lambda h: h.memset(Vs[:, :, :, :, 128:130], 1.0), (), [("Vs", 0), ("Vs", 1)])

        tiles = [("own", m, 4) for m in range(MO)] + [("oth", m, 4) for m in range(MH)] + [("ctx", 0, NCX // 128)]
        NTI = len(tiles)
        late = list(L["late_casts"])
        per_tile = (len(late) + max(1, NTI - 2) - 1) // max(1, NTI - 2)

        def tile_src(i):
            kind, m, NT = tiles[i]
            if kind == "ctx":
                return L["ctx"], 0, None
            if kind == "own":
                return L["x_own"][m * MT:(m + 1) * MT, :], (NCX + m * MT) // 128, m * 4
            return L["x_oth"][m * MT:(m + 1) * MT, :], (NCX + NO + m * MT) // 128, (NO + m * MT) // 128

        def Hload(i):
            kind, m, NT = tiles[i]
            src, kt0, sub0 = tile_src(i)
            xr = xresb[i % 3]
            A("act", lambda h, src=src, NT=NT, xr=xr: h.dma_start(out=xr[:, 0:NT, :], in_=src.rearrange("(s p) d -> p s d", p=128)),
              (), [(("xres", i % 3), s) for s in range(NT)], dma=True)

        def Hnorm(i):
            kind, m, NT = tiles[i]
            kd = 1 if kind == "ctx" else 0
            yield from norm_to_hT2(L, P, NT, xresb[i % 3], ("xres", i % 3), 0, kd, hTb[i % 2], ("hT", i % 2), WH, "nh_", 4)

        def Trest(i):
            kind, m, NT = tiles[i]
            kd = 1 if kind == "ctx" else 0
            src, kt0, sub0 = tile_src(i)
            xr = xresb[i % 3]
            xk = ("xres", i % 3)
            T = NT * 128
            if sub0 is not None:
                A("sp", lambda h, sub0=sub0: h.dma_start(out=tabt[:], in_=L["rope_s"][sub0:sub0 + 4].rearrange("s p c -> p s c")),
                  (), ["tab"], dma=True)
            yield from norm_to_hT2(L, P, NT, xr, xk, 1, kd, h2T, "h2T", WT, "nt_", 5)
            H2 = [("h2T", k) for k in range(8)]
            if kind == "own":
                A("pool", lambda h, m=m, xr=xr: h.dma_start(out=L["x1_s"][m * MT:(m + 1) * MT, :].rearrange("(s p) d -> p s d", p=128), in_=xr[:]),
                  [(xk, s) for s in range(4)], [("x1_s", m)], dma=True)
                A("pool", lambda h, m=m: h.dma_start(out=L["h2T_s"][m], in_=h2T[:].rearrange("p k t -> p (k t)")), H2, [("h2T_s", m)], dma=True)
            if kind != "ctx":
                hi = m if kind == "own" else MO + m
                A("pool", lambda h, hi=hi: h.tensor_copy(out=halo_all[:, hi, :, 0], in_=h2T[:, :, 0]), H2, ["halo"])
                A("pool", lambda h, hi=hi: h.tensor_copy(out=halo_all[:, hi, :, 1], in_=h2T[:, :, MT - 1]), H2, ["halo"])
            sbuf = i % 2
            yield
            for s in range(NT):
                bank = 6
                tbank = 7

                def mmkv(h, s=s, bank=bank):
                    for k in range(8):
                        i_ = h.matmul(ps[:, bank, :], lhsT=h2T[:, k, s * 128:(s + 1) * 128], rhs=wkv[:, k, :], start=(k == 0), stop=(k == 7))
                    return i_
                A("pe", mmkv, H2 + ["wkv"], [("ps", bank)])
                yield
                A("act", lambda h, bank=bank, s=s, sbuf=sbuf: h.activation(out=Vs[:, sbuf, s, :, 0:128], in_=ps[:, bank, 256:512].rearrange("p (g d) -> p g d", g=2),
                                                                          func=AF.Copy), [("ps", bank)], [("Vs", sbuf)])
                headnorm_rope(L, P, ps[:, bank, 0:256].rearrange("p (g d) -> p g d", g=2), [("ps", bank)], 2, L["kgb"],
                              None if kind == "ctx" else tabt[:, s, :], kb[:], ["kb"], W)

                yield
                yield

                def trk(h, tbank=tbank):
                    for g in range(2):
                        i_ = h.transpose(out=ps[:, tbank, :].bitcast(BF16)[:, g * 128:(g + 1) * 128], in_=kb[:, g, :], identity=ident[:])
                    return i_
                A("pe", trk, ["kb"], [("ps", tbank)])
                A("dve", lambda h, s=s, sbuf=sbuf, tbank=tbank: h.tensor_copy(out=KTs[:, sbuf, :, s * 128:(s + 1) * 128],
                                                                              in_=ps[:, tbank, :].bitcast(BF16)[:, 0:256].rearrange("p (g t) -> p g t", g=2)),
                  [("ps", tbank)], [("KTs", sbuf)])
            A("pool", lambda h, sbuf=sbuf, kt0=kt0, T=T: h.dma_start(out=kt_s[:, :, kt0 * 128:kt0 * 128 + T], in_=KTs[:, sbuf, :, 0:T]),
              [("KTs", sbuf)], [("kt_s", i)], dma=True)
            A("pool", lambda h, sbuf=sbuf, kt0=kt0, NT=NT: h.dma_start(out=vx_s[:, kt0:kt0 + NT, :, :], in_=Vs[:, sbuf, 0:NT, :, :]),
              [("Vs", sbuf)], [("vx_s", i)], dma=True)

        Hload(0)
        drain([Hnorm(0)])
        for i, (kind, m, NT) in enumerate(tiles):
            for _ in range(per_tile if i < NTI - 1 else len(late)):
                if late:
                    P.add_raw("pool", late.pop(0))
            kd = 1 if kind == "ctx" else 0
            sched = {}
            if i >= 1:
                sched[1] = [Trest(i - 1)]
            if i + 1 < NTI:
                Hload(i + 1)
                sched[5] = [Hnorm(i + 1)]
            ffn_B(L, P, NT * 128, ring, L["S_1I"], L["cext"]("1i"), hTb[i % 2], ("hT", i % 2), actT, sa, sched)
            ffn_C(L, P, NT, ring, L["S_1O"], L["cext"]("1o"), actT)
            ffn_evac(L, P, NT, tmp, xresb[i % 3], ("xres", i % 3), g2[:, kd, :])
        drain([Trest(NTI - 1)])
        with nc.Block() as block:
            P.emit(block)


def build_phase3p(L):
    nc, ps = L["nc"], L["ps"]
    NO, MO, MT = L["NO"], L["MO"], L["MT"]
    with ExitStack() as e3:
        def sb(name, shape, dt):
            return e3.enter_context(nc.sbuf_tensor("t_" + name, list(shape), dt))
        P = Prog(nc, L["es"], "p3")
        A = P.add
        NR = L["NRING3"]
        ringt = sb("ring3", [128, NR, 4096], BF16)
        ring = Ring(ringt, NR)
        xresb = [sb("xres3a", [128, 4, D], F32), sb("xres3b", [128, 4, D], F32), sb("xres3c", [128, 4, D], F32)]
        hTb = [sb("hT3a", [128, 8, MT], BF16), sb("hT3b", [128, 8, MT], BF16)]
        actT = sb("actT3", [128, NJ, MT], BF16)
        sa = sb("sa3", [128, 2, MT], F32)
        tmp = sb("tmp3", [128, 2, MT], F32)
        WH = norm_work(nc, e3, "n3_")
        junk = sb("fjunk", [128, 2, D], BF16)
        ss = sb("fss", [128, 4], F32)
        tt = sb("ftt", [128, 4], F32)
        rstd = sb("frstd", [128, 4], F32)
        g8 = sb("g8", [128, D], F32)
        fgb = sb("fgb", [128, D], F32)
        yo = sb("yo", [128, 2, D], F32)
        A("sp", lambda h: h.dma_start(out=g8[:], in_=L["gbc_s"][3]), (), ["gate"], dma=True)
        A("sp", lambda h: h.dma_start(out=fgb[:], in_=L["fg_in"].partition_broadcast(128)), (), ["fgb"], dma=True)

        def Hload(m):
            xr = xresb[m % 3]
            A("act", lambda h, m=m, xr=xr: h.dma_start(out=xr[:], in_=L["xm_s"][m * MT:(m + 1) * MT, :].rearrange("(s p) d -> p s d", p=128)),
              (), [(("xres", m % 3), s) for s in range(4)], dma=True)

        def Hnorm(m):
            yield from norm_to_hT2(L, P, 4, xresb[m % 3], ("xres", m % 3), 2, 0, hTb[m % 2], ("hT", m % 2), WH, "n3_", 4)

        def Trest(m):
            xr = xresb[m % 3]
            xk = ("xres", m % 3)
            for s in range(4):
                A("act", lambda h, s=s, xr=xr: h.activation(out=junk[:, s % 2, :], in_=xr[:, s, :], func=AF.Square, accum_out=ss[:, s:s + 1]),
                  [(xk, s)], [("fss", s), ("fjunk", s % 2)])
                if s % 2 == 1:
                    yield
            A("dve", lambda h: h.tensor_scalar(out=tt[:], in0=ss[:], scalar1=1.0 / D, scalar2=EPS, op0=ALU.mult, op1=ALU.add),
              [("fss", s) for s in range(4)], ["ftt"])
            A("pool", lambda h: h.tensor_tensor(out=rstd[:], in0=tt[:], in1=L["mhalf"][:, 0:4], op=ALU.pow), ["ftt"], ["frstd"])
            for s in range(4):
                A("dve", lambda h, s=s, xr=xr: h.scalar_tensor_tensor(out=yo[:, s % 2, :], in0=xr[:, s, :], scalar=rstd[:, s:s + 1], in1=fgb[:],
                                                                      op0=ALU.mult, op1=ALU.mult), [(xk, s), "frstd", "fgb"], [("yo", s % 2)])
                A("pool", lambda h, s=s, m=m: h.dma_start(out=L["y_out"][m * MT + s * 128:m * MT + (s + 1) * 128, :], in_=yo[:, s % 2, :]),
                  [("yo", s % 2)], [("y", m, s)], dma=True)
                yield

        Hload(0)
        drain([Hnorm(0)])
        for m in range(MO):
            sched = {}
            if m >= 1:
                sched[1] = [Trest(m - 1)]
            if m + 1 < MO:
                Hload(m + 1)
                sched[5] = [Hnorm(m + 1)]
            ffn_B(L, P, MT, ring, L["S_3I"], L["cext"]("3i"), hTb[m % 2], ("hT", m % 2), actT, sa, sched)
            ffn_C(L, P, 4, ring, L["S_3O"], L["cext"]("3o"), actT)
            ffn_evac(L, P, 4, tmp, xresb[m % 3], ("xres", m % 3), g8[:])
        drain([Trest(MO - 1)])
        with nc.Block() as block:
            P.emit(block)

def host_inputs(b, hh, NO, x, c, ctx, c_ctx, w_mod, b_mod, norm1_g, norm2_g, norm3_g, ffn1_w_in, ffn1_w_out, w_in, conv_w,
                q_norm_g, k_norm_g, w_branch_conv, w_branch_attn, w_out, ffn2_w_in, ffn2_w_out, final_g):
    f = np.float32
    own = slice(hh * NO, (hh + 1) * NO)
    oth = slice((1 - hh) * NO, (2 - hh) * NO)
    cT = np.stack([c[b].reshape(8, 128).T, c_ctx.reshape(8, 128).T], axis=-1).reshape(128, 16)
    gT = np.stack([g[0].reshape(8, 128).T for g in (norm1_g, norm2_g, norm3_g)], axis=1).reshape(128, 24)
    convT = np.stack([conv_w[0, j].reshape(8, 128).T for j in range(3)], axis=1).reshape(128, 24)
    p = np.arange(128)
    meta = np.zeros((128, 8), f)
    meta[:, 0] = hh * NO // 64 + (p >> 6)
    meta[:, 1] = (1 - hh) * NO // 64 + (p >> 6)
    meta[:, 2] = p & 63
    meta[:, 3] = 0.0 if hh == 0 else 1.0
    meta[:, 4] = 1.0 if hh == 0 else 0.0
    return {
        "x_own": np.ascontiguousarray(x[b, own]), "x_oth": np.ascontiguousarray(x[b, oth]), "ctx": np.ascontiguousarray(ctx[b]),
        "cT": np.ascontiguousarray(cT, f), "bmodT": np.ascontiguousarray(b_mod[0].reshape(72, 128).T, f), "bmod": np.ascontiguousarray(b_mod[0], f),
        "gT": np.ascontiguousarray(gT, f), "convT": np.ascontiguousarray(convT, f), "final_g": np.ascontiguousarray(final_g, f),
        "q_norm_g": np.ascontiguousarray(q_norm_g[0], f), "k_norm_g": np.ascontiguousarray(k_norm_g[0], f), "meta": meta,
        "w_mod": w_mod[0], "ffn1_w_in": ffn1_w_in[0], "ffn1_w_out": ffn1_w_out[0], "w_in": w_in[0],
        "w_branch_conv": w_branch_conv[0], "w_branch_attn": w_branch_attn[0], "w_out": w_out[0],
        "ffn2_w_in": ffn2_w_in[0], "ffn2_w_out": ffn2_w_out[0],
    }


def kernel(**inputs):
    inputs = {k: np.asarray(v) for k, v in inputs.items()}
    x = inputs["x"]
    B, S, _ = x.shape
    NO = S // 2
    nc = build({"NO": NO, "NH": NO, "NCX": inputs["ctx"].shape[1]})
    in_maps = []
    for core in range(2 * B):
        in_maps.append(host_inputs(core // 2, core % 2, NO, **inputs))
    res = run_bass_kernel_spmd(nc, in_maps, core_ids=list(range(2 * B)))
    out = np.empty((B, S, D), np.float32)
    for core in range(2 * B):
        b, hh = core // 2, core % 2
        out[b, hh * NO:(hh + 1) * NO] = np.asarray(res.results[core]["y"], dtype=np.float32)
    return out
```

```python
import numpy as np
from contextlib import ExitStack
import concourse.bass as bass
import concourse.mybir as mybir
from concourse.bass_utils import run_bass_kernel_spmd

F32 = mybir.dt.float32
BF16 = mybir.dt.bfloat16
I32 = mybir.dt.int32
AF = mybir.ActivationFunctionType
ALU = mybir.AluOpType
AX = mybir.AxisListType

D = 1024
DFF = 2816
NJ = DFF // 128
EPS = 1e-6
THETA = 10000.0
COMPUTE = ("pe", "act", "dve", "pool")


class Op:
    __slots__ = ("id", "eng", "fn", "deps", "dma", "sem", "val", "needed", "ext", "prewait")


class Prog:
    NRING = {"sp": 8, "act": 2, "pool": 4}

    def __init__(self, nc, es, name):
        self.nc = nc
        self.name = name
        self.ops = []
        self.eng_ops = {e: [] for e in ("pe", "act", "dve", "pool", "sp")}
        self.res = {}
        self.sems = {e: es.enter_context(nc.semaphore(f"{name}_{e}")) for e in COMPUTE}
        self.dsems = {q: [es.enter_context(nc.semaphore(f"{name}_d{q}{i}")) for i in range(self.NRING[q])]
                      for q in ("sp", "act", "pool")}
        self.dcount = {q: 0 for q in ("sp", "act", "pool")}

    def add(self, eng, fn, reads=(), writes=(), dma=False, ext=()):
        op = Op()
        op.id = len(self.ops)
        op.eng = eng
        op.fn = fn
        op.dma = dma
        op.ext = list(ext)
        op.needed = False
        op.sem = None
        op.val = 0
        op.prewait = None
        deps = {}
        for k in reads:
            st = self.res.get(k)
            if st is not None and st[0] is not None:
                deps[st[0]] = "raw"
        for k in writes:
            st = self.res.get(k)
            if st is not None:
                if st[0] is not None and st[0] not in deps:
                    deps[st[0]] = "waw"
                for r in st[1]:
                    if r not in deps:
                        deps[r] = "war"
        deps.pop(op.id, None)
        for k in reads:
            st = self.res.setdefault(k, [None, []])
            st[1].append(op.id)
        for k in writes:
            self.res[k] = [op.id, []]
        keep = []
        for pid, kind in deps.items():
            p = self.ops[pid]
            if (not dma) and (not p.dma) and p.eng == eng:
                if eng == "pe":
                    continue
            keep.append(pid)
            p.needed = True
        op.deps = keep
        if dma:
            q = eng
            i = self.dcount[q]
            self.dcount[q] = i + 1
            nr = self.NRING[q]
            op.sem = self.dsems[q][i % nr]
            op.val = 16 * (i // nr + 1)
            if i >= nr:
                op.prewait = (op.sem, op.val - 16)
        self.ops.append(op)
        self.eng_ops[eng].append(op)
        return op

    def add_raw(self, eng, fn, reads=()):
        op = Op()
        op.id = len(self.ops)
        op.eng, op.fn, op.dma, op.ext, op.needed, op.sem, op.val, op.prewait, op.deps = eng, fn, "raw", [], False, None, 0, None, []
        for k in reads:
            st = self.res.get(k)
            if st is not None and st[0] is not None:
                op.deps.append(st[0])
                self.ops[st[0]].needed = True
        self.ops.append(op)
        self.eng_ops[eng].append(op)
        return op

    def emit(self, block):
        for e in COMPUTE:
            ops = [o for o in self.eng_ops[e] if not o.dma]
            if ops:
                ops[-1].needed = True
            c = 0
            for o in ops:
                if o.needed:
                    c += 1
                    o.sem = self.sems[e]
                    o.val = c
        finals = []
        for e in COMPUTE:
            ops = [o for o in self.eng_ops[e] if not o.dma]
            if ops:
                finals.append((self.sems[e], ops[-1].val))
        for q in ("sp", "act", "pool"):
            n = self.dcount[q]
            nr = self.NRING[q]
            for r in range(min(n, nr)):
                cnt = (n - r + nr - 1) // nr
                finals.append((self.dsems[q][r], 16 * cnt))

        def run(eng):
            def f(h):
                waited = {}

                def wait(sem, val):
                    key = id(sem)
                    if waited.get(key, 0) >= val:
                        return
                    h.wait_ge(sem, val)
                    waited[key] = val

                for o in self.eng_ops[eng]:
                    for pid in o.deps:
                        p = self.ops[pid]
                        wait(p.sem, p.val)
                    for (s, v) in o.ext:
                        wait(s, v)
                    if o.prewait is not None:
                        wait(*o.prewait)
                    ins = o.fn(h)
                    if o.dma == "raw":
                        continue
                    if o.dma:
                        ins.then_inc(o.sem, 16)
                    elif o.needed:
                        ins.then_inc(o.sem, 1)
                for (s, v) in finals:
                    wait(s, v)
            return f

        block.tensor(run("pe"))
        block.scalar(run("act"))
        block.vector(run("dve"))
        block.gpsimd(run("pool"))
        block.sync(run("sp"))


class Ring:
    def __init__(self, tile, nslot):
        self.tile = tile
        self.nslot = nslot
        self.i = 0

    def load(self, P, src_ap, ncols=4096, ext=()):
        s = self.i % self.nslot
        self.i += 1
        dst = self.tile[:, s, 0:ncols]
        P.add("sp", lambda h, d=dst, a=src_ap: h.dma_start(out=d, in_=a), reads=(), writes=[("ring", s)],
              dma=True, ext=ext)
        return self.tile[:, s, :], ("ring", s)


def build(cfg):
    NO, NH, NCX = cfg["NO"], cfg["NH"], cfg["NCX"]
    MT = 512
    assert NO % MT == 0 and NH % MT == 0 and NCX % 128 == 0 and NCX <= MT
    MO, MH = NO // MT, NH // MT
    NKEY = NCX + NO + NH
    NKT = NKEY // 128
    assert NKT % 2 == 0
    NSUB = (NO + NH) // 128
    NRING1 = cfg.get("NRING1", 6)
    NRING2 = cfg.get("NRING2", 3)
    NRING3 = cfg.get("NRING3", 6)

    nc = bass.Bass("TRN2", target_bir_lowering=False)

    def din(name, shape, dt=F32):
        return nc.dram_tensor(name, list(shape), dt, kind="ExternalInput").ap()

    def dscr(name, shape, dt):
        return nc.dram_tensor(name, list(shape), dt, kind="Internal").ap()

    x_own = din("x_own", [NO, D])
    x_oth = din("x_oth", [NH, D])
    ctx = din("ctx", [NCX, D])
    cT_in = din("cT", [128, 16])
    bmodT_in = din("bmodT", [128, 72])
    bmod_in = din("bmod", [9 * D])
    gT_in = din("gT", [128, 24])
    convT_in = din("convT", [128, 24])
    fg_in = din("final_g", [D])
    qg_in = din("q_norm_g", [128])
    kg_in = din("k_norm_g", [128])
    meta_in = din("meta", [128, 8])
    w_mod = din("w_mod", [D, 9 * D])
    w1i = din("ffn1_w_in", [D, 2 * DFF])
    w1o = din("ffn1_w_out", [DFF, D])
    w_in = din("w_in", [D, 6656])
    w_bc = din("w_branch_conv", [D, D])
    w_ba = din("w_branch_attn", [D, D])
    w_o = din("w_out", [D, D])
    w3i = din("ffn2_w_in", [D, 2 * DFF])
    w3o = din("ffn2_w_out", [DFF, D])
    y_out = nc.dram_tensor("y", [NO, D], F32, kind="ExternalOutput").ap()

    S_1I = dscr("s_1i", [11, 128, 4096], BF16)
    S_1O = dscr("s_1o", [6, 128, 4096], BF16)
    S_3I = dscr("s_3i", [11, 128, 4096], BF16)
    S_3O = dscr("s_3o", [6, 128, 4096], BF16)
    S_KV = dscr("s_kv", [1, 128, 4096], BF16)
    S_Q = dscr("s_q", [2, 128, 4096], BF16)
    S_C = dscr("s_c", [2, 128, 4096], BF16)
    S_V = dscr("s_v", [2, 128, 4096], BF16)
    S_B = dscr("s_b", [2, 128, 4096], BF16)
    S_G = dscr("s_g", [4, 128, 4096], BF16)
    S_R = dscr("s_r", [4, 128, 4096], BF16)
    S_O = dscr("s_o", [2, 128, 4096], BF16)
    x1_s = dscr("x1_s", [NO, D], F32)
    xm_s = dscr("xm_s", [NO, D], F32)
    h2T_s = dscr("h2T_s", [MO, 128, 4096], BF16)
    rope_s = dscr("rope_s", [NSUB, 128, 128], F32)
    gbc_s = dscr("gbc_s", [5, 128, D], F32)
    kt_s = dscr("kt_s", [128, 2, NKEY], BF16)
    vx_s = dscr("vx_s", [128, NKT, 2, 130], BF16)

    dbg_t = {}
    if cfg.get("dbg"):
        for nm, n in (("ycT", 4096), ("OT", 4096), ("qT", 4096), ("KT", 2 * NKEY), ("Vx", NKT * 2 * 130)):
            dbg_t[nm] = nc.dram_tensor("dbg_" + nm, [128, n], BF16, kind="ExternalOutput").ap()
    es = ExitStack()
    with es:
        def sb(name, shape, dt, stack=es):
            return stack.enter_context(nc.sbuf_tensor("t_" + name, list(shape), dt))

        ps = es.enter_context(nc.psum_tensor("ps", [128, 8, 512], F32))

        def psb(b):
            return ps[:, b, :]

        def psb16(b):
            return ps[:, b, :].bitcast(BF16)

        ident = sb("ident", [128, 128], BF16)
        identf = sb("identf", [128, 128], F32)
        onesf = sb("onesf", [128, 128], F32)
        mhalf = sb("mhalf", [128, 16], F32)
        modc = sb("modc", [128, 9, 8, 2], F32)
        gsc = sb("gsc", [128, 3, 8, 2], F32)
        meta = sb("meta", [128, 8], F32)
        convc = sb("convc", [128, 3, 8], F32)
        qgb = sb("qgb", [128, 128], F32)
        kgb = sb("kgb", [128, 128], F32)
        nbias = sb("nbias", [128, 1], F32)
        halo_all = sb("halo_all", [128, MO + MH, 8, 2], BF16)

        cast_groups = {}

        def cast_group(name):
            s = es.enter_context(nc.semaphore("cg_" + name))
            cast_groups[name] = [s, 0, []]
            return cast_groups[name]

        def castdma(g, out, in_):
            g[2].append((out, in_))
            g[1] += 16

        def in_view(w, c0, n):
            return w.rearrange("(k p) c -> p k c", p=128)[:, :, c0:c0 + n]

        def chunk_view(S, c, off, n):
            return S[c].rearrange("p (k c) -> p k c", k=8)[:, :, off:off + n]

        cg = {}
        for nm, S, w in (("1i", S_1I, w1i), ("3i", S_3I, w3i)):
            g = cast_group(nm)
            cg[nm] = g
            for jj in range(11):
                castdma(g, chunk_view(S, jj, 0, 256), in_view(w, jj * 256, 256))
                castdma(g, chunk_view(S, jj, 256, 256), in_view(w, DFF + jj * 256, 256))
        for nm, S, w in (("1o", S_1O, w1o), ("3o", S_3O, w3o)):
            g = cast_group(nm)
            cg[nm] = g
            wv = w.rearrange("(j p) d -> p j d", p=128)
            for c in range(6):
                nj = 4 if c < 5 else 2
                castdma(g, S[c].rearrange("p (j d) -> p j d", j=4)[:, 0:nj, :], wv[:, 4 * c:4 * c + nj, :])
        g = cast_group("kv")
        cg["kv"] = g
        castdma(g, chunk_view(S_KV, 0, 0, 512), in_view(w_in, 4096, 512))
        for nm, S, c0 in (("q", S_Q, 3072), ("c", S_C, 1024), ("v", S_V, 2048), ("b", S_B, 0)):
            g = cast_group(nm)
            cg[nm] = g
            for c in range(2):
                castdma(g, chunk_view(S, c, 0, 512), in_view(w_in, c0 + c * 512, 512))
        g = cast_group("g")
        cg["g"] = g
        for jj in range(4):
            for t in range(2):
                j = 2 * jj + t
                castdma(g, chunk_view(S_G, jj, t * 256, 128), in_view(w_in, 4608 + j * 128, 128))
                castdma(g, chunk_view(S_G, jj, t * 256 + 128, 128), in_view(w_in, 5632 + j * 128, 128))
        g = cast_group("r")
        cg["r"] = g
        for jj in range(4):
            for t in range(2):
                j = 2 * jj + t
                castdma(g, chunk_view(S_R, jj, t * 256, 128), in_view(w_bc, j * 128, 128))
                castdma(g, chunk_view(S_R, jj, t * 256 + 128, 128), in_view(w_ba, j * 128, 128))
        g = cast_group("o")
        cg["o"] = g
        for c in range(2):
            castdma(g, chunk_view(S_O, c, 0, 512), in_view(w_o, c * 512, 512))
        cast_order = ["1i", "1o", "kv", "q", "c", "v", "b", "g", "r", "o", "3i", "3o"]

        def cext(nm):
            return [(cg[nm][0], cg[nm][1])]

        with ExitStack() as e0:
            P = Prog(nc, es, "p0")
            A = P.add
            cT = sb("cT", [128, 8, 2], F32, e0)
            sc = sb("sc", [128, 8, 2], F32, e0)
            bmodT = sb("bmodT", [128, 72], F32, e0)
            gT = sb("gT", [128, 3, 8], F32, e0)
            wm = sb("wm", [128, 2, 8, 1024], F32, e0)
            screp = sb("screp", [128, 2, 8, 128], F32, e0)
            bbc = sb("bbc", [128, 2, 1024], F32, e0)
            gtmp = sb("gtmp", [128, 2, 1024], F32, e0)
            rowm = sb("rowm", [2, 1024], F32, e0)
            iot = sb("iot", [128, 128], I32, e0)
            iof = sb("iof", [128, 128], F32, e0)
            inv = sb("inv", [128, 32], F32, e0)
            rowp = sb("rowp", [128, NSUB], F32, e0)
            ang = sb("ang", [128, NSUB, 2, 32], F32, e0)
            ang2 = sb("ang2", [128, NSUB, 2, 2, 32], F32, e0)
            tab = sb("tab", [128, NSUB, 128], F32, e0)
            twopi = sb("twopi", [128, 1], F32, e0)
            negpi = sb("negpi", [128, 1], F32, e0)
            mx = sb("mx", [128, 2], F32, e0)

            def cast_fn(nm, i0, i1):
                def f(h):
                    for (o, i) in cg[nm][2][i0:i1]:
                        h.dma_start(out=o, in_=i).then_inc(cg[nm][0], 16)
                return f
            late_casts = []
            for nm in cast_order[3:]:
                n = len(cg[nm][2])
                for i0 in range(0, n, 6):
                    late_casts.append(cast_fn(nm, i0, min(n, i0 + 6)))

            A("sp", lambda h: h.dma_start(out=cT[:].rearrange("p k n -> p (k n)"), in_=cT_in), (), ["cT"], dma=True)
            A("sp", lambda h: h.dma_start(out=bmodT[:], in_=bmodT_in), (), ["bmodT"], dma=True)
            A("sp", lambda h: h.dma_start(out=gT[:].rearrange("p a b -> p (a b)"), in_=gT_in), (), ["gT"], dma=True)
            A("sp", lambda h: h.dma_start(out=convc[:].rearrange("p a b -> p (a b)"), in_=convT_in), (), ["convc"], dma=True)
            A("sp", lambda h: h.dma_start(out=meta[:], in_=meta_in), (), ["meta"], dma=True)
            A("sp", lambda h: h.dma_start(out=qgb[:], in_=qg_in.partition_broadcast(128)), (), ["qgb"], dma=True)
            A("sp", lambda h: h.dma_start(out=kgb[:], in_=kg_in.partition_broadcast(128)), (), ["kgb"], dma=True)

            A("pool", lambda h: h.memset(onesf[:], 1.0), (), ["onesf"])
            A("pool", lambda h: h.memset(mhalf[:], -0.5), (), ["mhalf"])
            A("pool", lambda h: h.memset(twopi[:], 2.0 * np.pi), (), ["twopi"])
            A("pool", lambda h: h.memset(negpi[:], -np.pi * 0.999999), (), ["negpi"])
            A("pool", lambda h: h.iota(iot[:], [[1, 128]], channel_multiplier=-1), (), ["iot"])
            A("dve", lambda h: h.tensor_copy(out=iof[:], in_=iot[:]), ["iot"], ["iof"])
            A("dve", lambda h: h.tensor_single_scalar(out=identf[:], in_=iof[:], scalar=0.0, op=ALU.is_equal), ["iof"], ["identf"])
            A("dve", lambda h: h.tensor_copy(out=ident[:], in_=identf[:]), ["identf"], ["ident"])

            A("act", lambda h: h.activation(out=sc[:], in_=cT[:], func=AF.Silu), ["cT"], ["sc"])

            wmv = w_mod.rearrange("(k p) c -> p k c", p=128)
            bc_jobs = {2: [(0, 0, 0.5), (1, 1, 0.5)], 5: [(0, 2, 1.0)], 8: [(0, 3, 0.5)]}
            for m in range(9):
                wb = m % 2
                for hh in range(2):
                    A("sp", lambda h, m=m, wb=wb, hh=hh: h.dma_start(out=wm[:, wb, :, hh * 512:(hh + 1) * 512],
                                                                      in_=wmv[:, :, m * 1024 + hh * 512:m * 1024 + (hh + 1) * 512]),
                      (), [("wm", wb, hh)], dma=True)

                def mm_rows(h, m=m, wb=wb):
                    for hh in range(2):
                        for k in range(8):
                            i = h.matmul(ps[0:2, hh, :], lhsT=sc[:, k, :], rhs=wm[:, wb, k, hh * 512:(hh + 1) * 512], start=(k == 0), stop=(k == 7))
                    return i
                A("pe", mm_rows, [("wm", wb, 0), ("wm", wb, 1), "sc"], [("ps", 0), ("ps", 1)])
                A("act", lambda h: h.activation(out=rowm[:], in_=ps[0:2, 0:2, :].rearrange("p a b -> p (a b)"), func=AF.Copy), [("ps", 0), ("ps", 1)], ["rowm"])

                def tr_cols(h):
                    for cc in range(8):
                        i = h.transpose(out=ps[:, 2, cc * 2:cc * 2 + 2], in_=rowm[0:2, cc * 128:(cc + 1) * 128], identity=identf[0:2, 0:2])
                    return i
                A("pe", tr_cols, ["rowm", "identf"], [("ps", 2)])
                A("dve", lambda h, m=m: h.tensor_tensor(out=modc[:, m, :, :], in0=ps[:, 2, 0:16].rearrange("p (c n) -> p c n", n=2),
                                                        in1=bmodT[:, m * 8:(m + 1) * 8].unsqueeze(2).to_broadcast([128, 8, 2]), op=ALU.add),
                  [("ps", 2), "bmodT"], [("modc", m)])
                if m == 1:
                    P.add_raw("pool", cast_fn("1i", 0, len(cg["1i"][2])), reads=[("modc", 1)])
                if m == 6:
                    for nm in ("1o", "kv"):
                        P.add_raw("pool", cast_fn(nm, 0, len(cg[nm][2])), reads=[("modc", 6)])
                for (n, slot, fac) in bc_jobs.get(m, []):
                    bb = slot % 2
                    for cc in range(8):
                        A("dve", lambda h, m=m, n=n, cc=cc: h.tensor_scalar(out=screp[:, 0, cc, :], in0=onesf[:], scalar1=modc[:, m, cc, n:n + 1],
                                                                            scalar2=None, op0=ALU.mult), [("modc", m), "onesf"], [("crep", cc)])

                    def mm_bc(h):
                        for cc in range(8):
                            i = h.matmul(ps[:, 4 + cc // 4, (cc % 4) * 128:(cc % 4 + 1) * 128], lhsT=screp[:, 0, cc, :], rhs=identf[:],
                                         start=True, stop=True)
                        return i
                    A("pe", mm_bc, [("crep", cc) for cc in range(8)] + ["identf"], [("ps", 4), ("ps", 5)])
                    A("act", lambda h, bb=bb, fac=fac: h.activation(out=gtmp[:, bb, :], in_=ps[:, 4:6, :].rearrange("p a b -> p (a b)"), func=AF.Copy, scale=float(fac)),
                      [("ps", 4), ("ps", 5)], [("gtmp", bb)])
                    A("pool", lambda h, bb=bb, slot=slot: h.dma_start(out=gbc_s[slot], in_=gtmp[:, bb, :]), [("gtmp", bb)], [("gbc_s", slot)], dma=True)
            for i, msc in enumerate((1, 4, 7)):
                A("dve", lambda h, i=i, msc=msc: h.scalar_tensor_tensor(out=gsc[:, i, :, :], in0=modc[:, msc, :, :], scalar=1.0,
                                                                        in1=gT[:, i, :].unsqueeze(2).to_broadcast([128, 8, 2]),
                                                                        op0=ALU.add, op1=ALU.mult),
                  [("modc", msc), "gT"], [("gsc", i)])

            A("dve", lambda h: h.tensor_reduce(out=mx[:, 0:1], in_=qgb[:], axis=AX.X, op=ALU.max, apply_absolute_value=True), ["qgb"], [("mx", 0)])
            A("dve", lambda h: h.tensor_reduce(out=mx[:, 1:2], in_=kgb[:], axis=AX.X, op=ALU.max, apply_absolute_value=True), ["kgb"], [("mx", 1)])
            A("dve", lambda h: h.scalar_tensor_tensor(out=nbias[:], in0=mx[:, 0:1], scalar=-float(np.sqrt(128.0)), in1=mx[:, 1:2],
                                                      op0=ALU.mult, op1=ALU.mult), [("mx", 0), ("mx", 1)], ["nbias"])
            A("act", lambda h: h.activation(out=qgb[:], in_=qgb[:], func=AF.Copy, scale=float(128.0 ** -0.5)),
              ["qgb", ("mx", 0)], ["qgb"])

            A("pool", lambda h: h.iota(iot[:, 0:32], [[1, 32]], channel_multiplier=0), ["iof"], ["iot"])
            A("dve", lambda h: h.tensor_copy(out=iof[:, 0:32], in_=iot[:, 0:32]), ["iot", "identf"], ["iof"])
            A("act", lambda h: h.activation(out=inv[:], in_=iof[:, 0:32], func=AF.Exp, scale=-float(np.log(THETA) / 32.0)), ["iof"], ["inv"])
            A("pool", lambda h: h.iota(iot[:, 32:32 + NSUB], [[2, NSUB]], channel_multiplier=0), (), [("iot2")])
            A("dve", lambda h: h.tensor_copy(out=rowp[:], in_=iot[:, 32:32 + NSUB]), ["iot2"], ["rowp"])
            no_s = NO // 128
            A("dve", lambda h: h.tensor_scalar(out=rowp[:, 0:no_s], in0=rowp[:, 0:no_s], scalar1=meta[:, 0:1], scalar2=None, op0=ALU.add),
              ["rowp", "meta"], ["rowp"])
            A("dve", lambda h: h.tensor_scalar(out=rowp[:, no_s:NSUB], in0=rowp[:, no_s:NSUB], scalar1=meta[:, 1:2], scalar2=float(-2 * no_s),
                                               op0=ALU.add, op1=ALU.add), ["rowp", "meta"], ["rowp"])
            A("dve", lambda h: h.tensor_tensor(out=ang[:, :, 0, :], in0=rowp[:].unsqueeze(2).to_broadcast([128, NSUB, 32]),
                                               in1=inv[:].unsqueeze(1).to_broadcast([128, NSUB, 32]), op=ALU.mult), ["rowp", "inv"], [("ang", 0)])
            A("dve", lambda h: h.tensor_scalar(out=ang[:, :, 1, :], in0=inv[:].unsqueeze(1).to_broadcast([128, NSUB, 32]), scalar1=meta[:, 2:3],
                                               scalar2=None, op0=ALU.mult), ["inv", "meta"], [("ang", 1)])
            A("dve", lambda h: h.tensor_scalar(out=ang2[:, :, 0, :, :], in0=ang[:], scalar1=float(0.5 * np.pi), scalar2=None, op0=ALU.add),
              [("ang", 0), ("ang", 1)], [("ang2", 0)])
            A("dve", lambda h: h.tensor_copy(out=ang2[:, :, 1, :, :], in_=ang[:]), [("ang", 0), ("ang", 1)], [("ang2", 1)])
            a2f = ang2[:].rearrange("p s c a f -> p (s c a f)")
            NA = NSUB * 128
            WM0 = [("wm", 0, 0), ("wm", 0, 1)]
            WM1 = [("wm", 1, 0), ("wm", 1, 1)]
            ki = wm[:, 0].rearrange("p k c -> p (k c)").bitcast(I32)[:, 0:NA]
            kf = wm[:, 1].rearrange("p k c -> p (k c)")[:, 0:NA]
            A2 = [("ang2", 0), ("ang2", 1)]
            A("dve", lambda h: h.tensor_scalar(out=kf, in0=a2f, scalar1=float(1.0 / (2.0 * np.pi)), scalar2=None, op0=ALU.mult), A2, WM1)
            A("dve", lambda h: h.tensor_copy(out=ki, in_=kf), WM1, WM0)
            A("dve", lambda h: h.tensor_copy(out=kf, in_=ki), WM0, WM1)
            A("dve", lambda h: h.scalar_tensor_tensor(out=a2f, in0=kf, scalar=float(-2.0 * np.pi), in1=a2f, op0=ALU.mult, op1=ALU.add), WM1 + A2, A2)
            A("dve", lambda h: h.tensor_single_scalar(out=kf, in_=a2f, scalar=float(np.pi), op=ALU.is_gt), A2, WM1)
            A("dve", lambda h: h.scalar_tensor_tensor(out=a2f, in0=kf, scalar=float(-2.0 * np.pi), in1=a2f, op0=ALU.mult, op1=ALU.add), WM1 + A2, A2)
            A("act", lambda h: h.activation(out=tab[:].rearrange("p s c -> p (s c)"), in_=a2f, func=AF.Sin, scale=0.999999), A2, ["tab"])
            A("pool", lambda h: h.dma_start(out=rope_s.rearrange("s p c -> p s c"), in_=tab[:]), ["tab"], ["rope_s"], dma=True)

            with nc.Block() as block:
                P.emit(block)
        build_phase1p(locals())
        with ExitStack() as ekv:
            KT = sb("KT", [128, 2, NKEY], BF16, ekv)
            Vx = sb("Vx", [128, NKT, 2, 130], BF16, ekv)
            build_phase2(locals())
        build_phase3p(locals())
    return nc


def norm_to_hT(L, P, NT, xres, n_i, kind, dst, dkey, xn, junk, ss, tt, rstd):
    A = P.add
    gsc, modc, mhalf, ident, ps = L["gsc"], L["modc"], L["mhalf"], L["ident"], L["ps"]
    sh_m = (0, 3, 6)[n_i]
    T = NT * 128
    for s in range(NT):
        A("act", lambda h, s=s: h.activation(out=xn[:, s, :], in_=xres[:, s, :], func=AF.Square, accum_out=ss[:, s:s + 1]),
          [("xres", s)], [("ss", s), ("xn", s)])
    A("dve", lambda h: h.tensor_scalar(out=tt[:, 0:NT], in0=ss[:, 0:NT], scalar1=1.0 / D, scalar2=EPS, op0=ALU.mult, op1=ALU.add),
      [("ss", s) for s in range(NT)], ["tt"])
    A("pool", lambda h: h.tensor_tensor(out=rstd[:, 0:NT], in0=tt[:, 0:NT], in1=mhalf[:, 0:NT], op=ALU.pow), ["tt"], ["rstd"])
    for s in range(NT):
        A("dve", lambda h, s=s: h.tensor_scalar(out=xn[:, s, :], in0=xres[:, s, :], scalar1=rstd[:, s:s + 1], scalar2=None, op0=ALU.mult),
          [("xres", s), "rstd"], [("xn", s)])
    for s in range(NT):
        def tr(h, s=s):
            for k in range(8):
                o = ps[:, k // 2, :].bitcast(BF16)[:, (k % 2) * 512 + s * 128:(k % 2) * 512 + (s + 1) * 128]
                i = h.transpose(out=o, in_=xn[:, s, k * 128:(k + 1) * 128], identity=ident[:])
            return i
        A("pe", tr, [("xn", s)], [("ps", b) for b in range(4)])
    for k in range(8):
        A("act", lambda h, k=k: h.activation(out=dst[:, k, 0:T], in_=ps[:, k // 2, :].bitcast(BF16)[:, (k % 2) * 512:(k % 2) * 512 + T],
                                             func=AF.Identity, bias=modc[:, sh_m, k, kind:kind + 1], scale=gsc[:, n_i, k, kind:kind + 1]),
          [("ps", k // 2)], [(dkey, k)])


def ffn_core(L, P, NT, ring, S_I, S_O, ext_i, ext_o, hT, actT, sa, tmp, xres, gate_bc):
    A = P.add
    ps = L["ps"]
    T = NT * 128
    for jj in range(11):
        slot, skey = ring.load(P, S_I[jj], ext=ext_i)
        sv = slot.rearrange("p (k c) -> p k c", k=8)
        base = (jj % 2) * 4
        for m4 in range(4):
            def mm(h, m4=m4, sv=sv, base=base):
                for k in range(8):
                    i = h.matmul(ps[:, base + m4, 0:T], lhsT=sv[:, k, m4 * 128:(m4 + 1) * 128], rhs=hT[:, k, 0:T], start=(k == 0), stop=(k == 7))
                return i
            A("pe", mm, [skey] + [("hT", k) for k in range(8)], [("ps", base + m4)])
        for t in range(2):
            j = 2 * jj + t
            A("act", lambda h, t=t, base=base: h.activation(out=sa[:, t, 0:T], in_=ps[:, base + t, 0:T], func=AF.Silu),
              [("ps", base + t)], [("sa", t)])
            A("dve", lambda h, t=t, base=base, j=j: h.tensor_tensor(out=actT[:, j, 0:T], in0=sa[:, t, 0:T], in1=ps[:, base + 2 + t, 0:T], op=ALU.mult),
              [("sa", t), ("ps", base + 2 + t)], [("actT", j)])
    for c in range(6):
        nj = 4 if c < 5 else 2
        slot, skey = ring.load(P, S_O[c][:, 0:nj * 1024], ncols=nj * 1024, ext=ext_o)
        sv = slot.rearrange("p (j d) -> p j d", j=4)
        for jj in range(nj):
            j = 4 * c + jj

            def mm(h, j=j, jj=jj, sv=sv):
                for s in range(NT):
                    for hh in range(2):
                        i = h.matmul(ps[:, 2 * s + hh, :], lhsT=actT[:, j, s * 128:(s + 1) * 128], rhs=sv[:, jj, hh * 512:(hh + 1) * 512],
                                     start=(j == 0), stop=(j == NJ - 1))
                return i
            A("pe", mm, [skey, ("actT", j)], [("ps", b) for b in range(2 * NT)])
    for s in range(NT):
        for hh in range(2):
            b = 2 * s + hh
            A("dve", lambda h, b=b, hh=hh: h.tensor_tensor(out=tmp[:, b % 2, :], in0=ps[:, b, :], in1=gate_bc[:, hh * 512:(hh + 1) * 512], op=ALU.mult),
              [("ps", b), "gate"], [("tmp", b % 2)])
            A("pool", lambda h, b=b, s=s, hh=hh: h.tensor_tensor(out=xres[:, s, hh * 512:(hh + 1) * 512], in0=tmp[:, b % 2, :],
                                                                 in1=xres[:, s, hh * 512:(hh + 1) * 512], op=ALU.add),
              [("tmp", b % 2), ("xres", s)], [("xres", s)])


def headnorm_rope(L, P, src, skeys, H, gbc, tabv, dst, dkeys, W):
    A = P.add
    mhalf = L["mhalf"]
    sqj, ssq, tq, rq, qn, t1, t2, t3, t4 = W
    A("act", lambda h: h.activation(out=sqj[:, 0:H, :], in_=src, func=AF.Square), skeys, ["sqj"])
    A("dve", lambda h: h.tensor_reduce(out=ssq[:, 0:H], in_=sqj[:, 0:H, :], axis=AX.X, op=ALU.add), ["sqj"], ["ssq"])
    A("dve", lambda h: h.tensor_scalar(out=tq[:, 0:H], in0=ssq[:, 0:H], scalar1=1.0 / 128.0, scalar2=EPS, op0=ALU.mult, op1=ALU.add), ["ssq"], ["tq"])
    A("pool", lambda h: h.tensor_tensor(out=rq[:, 0:H], in0=tq[:, 0:H], in1=mhalf[:, 0:H], op=ALU.pow), ["tq"], ["rq"])
    A("dve", lambda h: h.tensor_tensor(out=qn[:, 0:H, :], in0=src, in1=rq[:, 0:H].unsqueeze(2).to_broadcast([128, H, 128]), op=ALU.mult),
      list(skeys) + ["rq"], ["qn"])
    if tabv is None:
        A("pool", lambda h: h.tensor_tensor(out=dst, in0=qn[:, 0:H, :], in1=gbc[:].unsqueeze(1).to_broadcast([128, H, 128]), op=ALU.mult),
          ["qn"], dkeys)
        return
    A("pool", lambda h: h.tensor_tensor(out=qn[:, 0:H, :], in0=qn[:, 0:H, :], in1=gbc[:].unsqueeze(1).to_broadcast([128, H, 128]), op=ALU.mult),
      ["qn"], ["qn"])
    q5 = qn[:, 0:H, :].rearrange("p h (a t f) -> p h a t f", a=2, t=2, f=32)
    d5 = dst.rearrange("p h (a t f) -> p h a t f", a=2, t=2, f=32)
    x1, x2 = q5[:, :, :, 0, :], q5[:, :, :, 1, :]
    cb = tabv[:, 0:64].rearrange("p (a f) -> p a f", a=2).unsqueeze(1).to_broadcast([128, H, 2, 32])
    sbb = tabv[:, 64:128].rearrange("p (a f) -> p a f", a=2).unsqueeze(1).to_broadcast([128, H, 2, 32])
    A("dve", lambda h: h.tensor_tensor(out=t1[:, 0:H], in0=x1, in1=cb, op=ALU.mult), ["qn", "tab"], ["t1"])
    A("pool", lambda h: h.tensor_tensor(out=t2[:, 0:H], in0=x2, in1=sbb, op=ALU.mult), ["qn", "tab"], ["t2"])
    A("dve", lambda h: h.tensor_tensor(out=d5[:, :, :, 0, :], in0=t1[:, 0:H], in1=t2[:, 0:H], op=ALU.subtract), ["t1", "t2"], dkeys)
    A("pool", lambda h: h.tensor_tensor(out=t3[:, 0:H], in0=x2, in1=cb, op=ALU.mult), ["qn", "tab"], ["t3"])
    A("dve", lambda h: h.tensor_tensor(out=t4[:, 0:H], in0=x1, in1=sbb, op=ALU.mult), ["qn", "tab"], ["t4"])
    A("pool", lambda h: h.tensor_tensor(out=d5[:, :, :, 1, :], in0=t3[:, 0:H], in1=t4[:, 0:H], op=ALU.add), ["t3", "t4"], dkeys)


def hn_work(nc, stack, H, pfx=""):
    def sb(name, shape, dt):
        return stack.enter_context(nc.sbuf_tensor("t_" + pfx + name, list(shape), dt))
    return (sb("sqj", [128, H, 128], F32), sb("ssq", [128, 8], F32), sb("tq", [128, 8], F32), sb("rq", [128, 8], F32),
            sb("qn", [128, H, 128], F32), sb("t1", [128, H, 2, 32], F32), sb("t2", [128, H, 2, 32], F32),
            sb("t3", [128, H, 2, 32], F32), sb("t4", [128, H, 2, 32], F32))


def build_phase1(L):
    nc, ps = L["nc"], L["ps"]
    NO, NH, NCX, MO, MH, MT, NKT = L["NO"], L["NH"], L["NCX"], L["MO"], L["MH"], L["MT"], L["NKT"]
    KT, Vx, ident, halo_all, meta = L["KT"], L["Vx"], L["ident"], L["halo_all"], L["meta"]
    with ExitStack() as e1:
        def sb(name, shape, dt):
            return e1.enter_context(nc.sbuf_tensor("t_" + name, list(shape), dt))
        P = Prog(nc, L["es"], "p1")
        A = P.add
        ringt = sb("ring1", [128, L["NRING1"], 4096], BF16)
        ring = Ring(ringt, L["NRING1"])
        xres = sb("xres", [128, 4, D], F32)
        xn = sb("xn", [128, 4, D], BF16)
        junk = sb("junk", [128, D], BF16)
        hT = sb("hT", [128, 8, MT], BF16)
        actT = sb("actT", [128, NJ, MT], BF16)
        sa = sb("sa", [128, 2, MT], F32)
        tmp = sb("tmp", [128, 2, MT], F32)
        ss = sb("ss", [128, 4], F32)
        tt = sb("tt", [128, 4], F32)
        rstd = sb("rstd", [128, 4], F32)
        wkv = sb("wkv", [128, 8, 512], BF16)
        g2 = sb("g2", [128, 2, D], F32)
        tabt = sb("tabt", [128, 4, 128], F32)
        kb = sb("kb", [128, 2, 128], BF16)
        W = hn_work(nc, e1, 2, "k_")

        A("sp", lambda h: h.dma_start(out=g2[:], in_=L["gbc_s"][0:2].rearrange("a p d -> p a d")), (), ["gate"], dma=True)
        A("sp", lambda h: h.dma_start(out=wkv[:].rearrange("p k c -> p (k c)"), in_=L["S_KV"][0]), (), ["wkv"], dma=True, ext=L["cext"]("kv"))
        A("pool", lambda h: h.memset(Vx[:, :, :, 128:130], 1.0), (), ["Vones"])

        tiles = [("ctx", 0, NCX // 128)] + [("own", m, 4) for m in range(MO)] + [("oth", m, 4) for m in range(MH)]
        late = list(L["late_casts"])
        per_tile = (len(late) + max(1, len(tiles) - 2) - 1) // max(1, len(tiles) - 2)
        for ti, (kind, m, NT) in enumerate(tiles):
            for _ in range(per_tile if ti < len(tiles) - 1 else len(late)):
                if late:
                    P.add_raw("pool", late.pop(0))
            T = NT * 128
            kd = 1 if kind == "ctx" else 0
            if kind == "ctx":
                src, kt0, sub0 = L["ctx"], 0, None
            elif kind == "own":
                src, kt0, sub0 = L["x_own"][m * MT:(m + 1) * MT, :], (NCX + m * MT) // 128, m * 4
            else:
                src, kt0, sub0 = L["x_oth"][m * MT:(m + 1) * MT, :], (NCX + NO + m * MT) // 128, (NO + m * MT) // 128
            A("act", lambda h, src=src, NT=NT: h.dma_start(out=xres[:, 0:NT, :], in_=src.rearrange("(s p) d -> p s d", p=128)),
              (), [("xres", s) for s in range(NT)], dma=True)
            if sub0 is not None:
                A("sp", lambda h, sub0=sub0: h.dma_start(out=tabt[:], in_=L["rope_s"][sub0:sub0 + 4].rearrange("s p c -> p s c")),
                  (), ["tab"], dma=True)
            norm_to_hT(L, P, NT, xres, 0, kd, hT, "hT", xn, junk, ss, tt, rstd)
            ffn_core(L, P, NT, ring, L["S_1I"], L["S_1O"], L["cext"]("1i"), L["cext"]("1o"), hT, actT, sa, tmp, xres, g2[:, kd, :])
            norm_to_hT(L, P, NT, xres, 1, kd, hT, "hT", xn, junk, ss, tt, rstd)
            if kind == "own":
                A("pool", lambda h, m=m: h.dma_start(out=L["x1_s"][m * MT:(m + 1) * MT, :].rearrange("(s p) d -> p s d", p=128), in_=xres[:]),
                  [("xres", s) for s in range(4)], [("x1_s", m)], dma=True)
                A("pool", lambda h, m=m: h.dma_start(out=L["h2T_s"][m], in_=hT[:].rearrange("p k t -> p (k t)")),
                  [("hT", k) for k in range(8)], [("h2T_s", m)], dma=True)
            if kind != "ctx":
                hi = m if kind == "own" else MO + m
                A("pool", lambda h, hi=hi: h.tensor_copy(out=halo_all[:, hi, :, 0], in_=hT[:, :, 0]), [("hT", k) for k in range(8)], ["halo"])
                A("pool", lambda h, hi=hi: h.tensor_copy(out=halo_all[:, hi, :, 1], in_=hT[:, :, MT - 1]), [("hT", k) for k in range(8)], ["halo"])
            for s in range(NT):
                bank = 4 + (s % 2)
                kt = kt0 + s

                def mmkv(h, s=s, bank=bank):
                    for k in range(8):
                        i = h.matmul(ps[:, bank, :], lhsT=hT[:, k, s * 128:(s + 1) * 128], rhs=wkv[:, k, :], start=(k == 0), stop=(k == 7))
                    return i
                A("pe", mmkv, [("hT", k) for k in range(8)] + ["wkv"], [("ps", bank)])
                A("act", lambda h, bank=bank, kt=kt: h.activation(out=Vx[:, kt, :, 0:128], in_=ps[:, bank, 256:512].rearrange("p (g d) -> p g d", g=2),
                                                                  func=AF.Copy), [("ps", bank)], [("V", kt)])
                headnorm_rope(L, P, ps[:, bank, 0:256].rearrange("p (g d) -> p g d", g=2), [("ps", bank)], 2, L["kgb"],
                              None if kind == "ctx" else tabt[:, s, :], kb[:], ["kb"], W)

                def trk(h):
                    for g in range(2):
                        i = h.transpose(out=ps[:, 6, :].bitcast(BF16)[:, g * 128:(g + 1) * 128], in_=kb[:, g, :], identity=ident[:])
                    return i
                A("pe", trk, ["kb"], [("ps", 6)])
                A("dve", lambda h, kt=kt: h.tensor_copy(out=KT[:, :, kt * 128:(kt + 1) * 128],
                                                        in_=ps[:, 6, :].bitcast(BF16)[:, 0:256].rearrange("p (g t) -> p g t", g=2)),
                  [("ps", 6)], [("KT", kt)])
        with nc.Block() as block:
            P.emit(block)


def build_phase2(L):
    nc, ps = L["nc"], L["ps"]
    NO, NH, NCX, MO, MH, MT, NKT = L["NO"], L["NH"], L["NCX"], L["MO"], L["MH"], L["MT"], L["NKT"]
    KT, Vx, ident, halo_all, meta, convc, nbias = L["KT"], L["Vx"], L["ident"], L["halo_all"], L["meta"], L["convc"], L["nbias"]
    cext = L["cext"]
    with ExitStack() as e2:
        def sb(name, shape, dt):
            return e2.enter_context(nc.sbuf_tensor("t_" + name, list(shape), dt))
        P = Prog(nc, L["es"], "p2")
        A = P.add
        ringt = sb("ring2", [128, L["NRING2"], 4096], BF16)
        ring = Ring(ringt, L["NRING2"])
        h2T = sb("h2T", [128, 8, MT], BF16)
        hal = sb("hal", [128, 8, 2], BF16)
        bufA = sb("bufA", [128, 4, D], F32)
        ycT = bufA[:, 0:2, :].bitcast(BF16).rearrange("p a (k t) -> p (a k) t", t=MT) if False else None
        bA16 = bufA[:].rearrange("p a d -> p (a d)").bitcast(BF16)
        ycT = bA16[:, 0:4096].rearrange("p (k t) -> p k t", k=8)
        OT = bA16[:, 4096:8192].rearrange("p (k t) -> p k t", k=8)
        xres = bufA
        qT = sb("qT", [128, 8, MT], BF16)
        mT = qT
        qb = sb("qb", [128, 4, 8, 128], BF16)
        wq = sb("wq", [128, 2, 4096], BF16)
        bsb = sb("bsb", [128, 2, MT], F32)
        tabt = sb("tabt2", [128, 4, 128], F32)
        W = hn_work(nc, e2, 8, "q_")
        csb = sb("csb", [128, 2, MT], F32)
        u = sb("u", [128, 2, MT + 2], F32)
        hps = sb("hps", [128, 2, 4], F32)
        PT = sb("PT", [128, 3, 1024], BF16)
        rden = sb("rden", [128, 4, 1], F32)
        on = sb("on", [128, 4, 128], BF16)
        sg = sb("sg", [128, 4, MT], F32)
        tm = sb("tm", [128, 4, MT], F32)
        cv = tm
        tmp = tm
        g5 = sb("g5", [128, D], F32)

        A("sp", lambda h: h.dma_start(out=g5[:], in_=L["gbc_s"][2]), (), ["gate"], dma=True)
        A("sp", lambda h: h.dma_start(out=wq[:], in_=L["S_Q"].rearrange("c p n -> p c n")), (), ["wq"], dma=True, ext=cext("q"))
        for g_ in range(2):
            A("sp", lambda h, g_=g_: h.dma_start(out=KT[:, g_, :], in_=L["kt_s"][:, g_, :]), (), [("KTl", g_)], dma=True)
        nq = 4
        for q_ in range(nq):
            a_, b_ = (NKT * q_) // nq, (NKT * (q_ + 1)) // nq
            A("sp", lambda h, a_=a_, b_=b_: h.dma_start(out=Vx[:, a_:b_], in_=L["vx_s"][:, a_:b_]), (), [("Vxl", q_)], dma=True)
        KVL = [("KTl", 0), ("KTl", 1)] + [("Vxl", q_) for q_ in range(nq)]
        BUFA = [("bufA", i) for i in range(4)]

        for m in range(MO):
            A("act", lambda h, m=m: h.dma_start(out=h2T[:].rearrange("p k t -> p (k t)"), in_=L["h2T_s"][m]), (), [("h2T", k) for k in range(8)], dma=True)
            A("sp", lambda h, m=m: h.dma_start(out=tabt[:], in_=L["rope_s"][m * 4:m * 4 + 4].rearrange("s p c -> p s c")), (), ["tab"], dma=True)
            li = (m - 1) if m > 0 else (MO + MH - 1)
            ri = (m + 1) if m < MO - 1 else MO
            if m == 0:
                A("dve", lambda h, li=li: h.tensor_scalar(out=hal[:, :, 0], in0=halo_all[:, li, :, 1], scalar1=meta[:, 3:4], scalar2=None, op0=ALU.mult), (), [("hal", 0)])
            else:
                A("dve", lambda h, li=li: h.tensor_copy(out=hal[:, :, 0], in_=halo_all[:, li, :, 1]), (), [("hal", 0)])
            if m == MO - 1:
                A("dve", lambda h, ri=ri: h.tensor_scalar(out=hal[:, :, 1], in0=halo_all[:, ri, :, 0], scalar1=meta[:, 4:5], scalar2=None, op0=ALU.mult), (), [("hal", 1)])
            else:
                A("dve", lambda h, ri=ri: h.tensor_copy(out=hal[:, :, 1], in_=halo_all[:, ri, :, 0]), (), [("hal", 1)])
            H2K = [("h2T", k) for k in range(8)]

            qv = [wq[:, 0].rearrange("p (k c) -> p k c", k=8), wq[:, 1].rearrange("p (k c) -> p k c", k=8)]

            def emit_mmq(s):
                b0 = 2 * (s % 2)

                def mmq(h, s=s, b0=b0):
                    for k in range(8):
                        for hh in range(2):
                            i = h.matmul(ps[:, b0 + hh, :], lhsT=h2T[:, k, s * 128:(s + 1) * 128], rhs=qv[hh][:, k, :], start=(k == 0), stop=(k == 7))
                    return i
                A("pe", mmq, H2K + ["wq"], [("ps", b0), ("ps", b0 + 1)])
                headnorm_rope(L, P, ps[:, b0:b0 + 2, :].rearrange("p a (h d) -> p (a h) d", d=128), [("ps", b0), ("ps", b0 + 1)], 8, L["qgb"],
                              tabt[:, s, :], qb[:, s], [("qb", s)], W)

            def emit_trq(s):
                tb = (0, 2, 1, 3)[s]

                def trq(h, s=s, tb=tb):
                    for hd in range(8):
                        i = h.transpose(out=ps[:, tb, :].bitcast(BF16)[:, hd * 128:(hd + 1) * 128], in_=qb[:, s, hd, :], identity=ident[:])
                    return i
                A("pe", trq, [("qb", s)], [("ps", tb)])
                A("dve", lambda h, s=s, tb=tb: h.tensor_copy(out=qT[:, :, s * 128:(s + 1) * 128], in_=ps[:, tb, :].bitcast(BF16).rearrange("p (h t) -> p h t", h=8)),
                  [("ps", tb)], [("qT", s)])

            conv_w = {}

            def emit_conv(j):
                half, jj = j // 4, j % 4
                if jj == 0:
                    sc_, kc = ring.load(P, L["S_C"][half], ext=cext("c"))
                    sv_, kv = ring.load(P, L["S_V"][half], ext=cext("v"))
                    sb_, kb_ = ring.load(P, L["S_B"][half], ext=cext("b"))
                    conv_w["v"] = tuple(t.rearrange("p (k c) -> p k c", k=8) for t in (sc_, sv_, sb_))
                    conv_w["k"] = (kc, kv, kb_)
                scv, svv, sbv = conv_w["v"]
                kc, kv, kb_ = conv_w["k"]
                par = j % 2
                bC, bV, bB, bH = 4, 5, 6, 7

                def mmc(h, jj=jj, scv=scv, svv=svv):
                    for k in range(8):
                        w = scv[:, k, jj * 128:(jj + 1) * 128]
                        h.matmul(ps[:, bC, :], lhsT=w, rhs=h2T[:, k, :], start=(k == 0), stop=(k == 7))
                        h.matmul(ps[:, bH, 0:2], lhsT=w, rhs=hal[:, k, :], start=(k == 0), stop=(k == 7))
                    for k in range(8):
                        w = svv[:, k, jj * 128:(jj + 1) * 128]
                        h.matmul(ps[:, bV, :], lhsT=w, rhs=h2T[:, k, :], start=(k == 0), stop=(k == 7))
                        i = h.matmul(ps[:, bH, 2:4], lhsT=w, rhs=hal[:, k, :], start=False, stop=(k == 7), skip_group_check=True)
                    return i
                A("pe", mmc, H2K + [kc, kv, ("hal", 0), ("hal", 1)], [("ps", bC), ("ps", bV), ("ps", bH)])

                def mmb(h, jj=jj, sbv=sbv):
                    for k in range(8):
                        i = h.matmul(ps[:, bB, :], lhsT=sbv[:, k, jj * 128:(jj + 1) * 128], rhs=h2T[:, k, :], start=(k == 0), stop=(k == 7))
                    return i
                A("pe", mmb, H2K + [kb_], [("ps", bB)])
                A("act", lambda h, par=par: h.activation(out=csb[:, par, :], in_=ps[:, bC, :], func=AF.Copy), [("ps", bC)], [("csb", par)])
                A("act", lambda h, par=par: h.activation(out=hps[:, par, 0:2], in_=ps[:, bH, 0:2], func=AF.Copy), [("ps", bH)], [("hps", par)])
                A("act", lambda h, par=par: h.activation(out=bsb[:, par, :], in_=ps[:, bB, :], func=AF.Copy), [("ps", bB)], [("bsb", par)])
                A("dve", lambda h, par=par: h.tensor_tensor(out=u[:, par, 1:MT + 1], in0=csb[:, par, :], in1=ps[:, bV, :], op=ALU.mult),
                  [("csb", par), ("ps", bV)], [("u", par)])
                A("dve", lambda h, par=par: h.tensor_tensor(out=u[:, par, 0:MT + 2:MT + 1], in0=hps[:, par, 0:2], in1=ps[:, bH, 2:4], op=ALU.mult),
                  [("hps", par), ("ps", bH)], [("u", par)])
                A("act", lambda h, par=par, j=j: h.activation(out=cv[:, par, :], in_=u[:, par, 0:MT], func=AF.Copy, scale=convc[:, 0, j:j + 1]),
                  [("u", par)], [("tm", par)])
                A("dve", lambda h, par=par, j=j: h.scalar_tensor_tensor(out=cv[:, par, :], in0=u[:, par, 1:MT + 1], scalar=convc[:, 1, j:j + 1], in1=cv[:, par, :],
                                                                        op0=ALU.mult, op1=ALU.add), [("u", par), ("tm", par)], [("tm", par)])
                A("dve", lambda h, par=par, j=j: h.scalar_tensor_tensor(out=cv[:, par, :], in0=u[:, par, 2:MT + 2], scalar=convc[:, 2, j:j + 1], in1=cv[:, par, :],
                                                                        op0=ALU.mult, op1=ALU.add), [("u", par), ("tm", par)], [("tm", par)])
                A("pool", lambda h, par=par, j=j: h.tensor_tensor(out=ycT[:, j, :], in0=cv[:, par, :], in1=bsb[:, par, :], op=ALU.mult),
                  [("tm", par), ("bsb", par)], BUFA[0:2])

            emit_mmq(0)
            emit_mmq(1)
            emit_conv(0)
            emit_mmq(2)
            emit_conv(1)
            emit_mmq(3)
            emit_conv(2)
            emit_trq(0)
            emit_conv(3)
            emit_trq(1)
            emit_conv(4)
            emit_conv(5)
            emit_trq(2)
            emit_conv(6)
            emit_trq(3)
            emit_conv(7)

            npair = NKT // 2
            its = [(s, g) for s in range(4) for g in range(2)]
            jobs = [(it, pi) for it in range(len(its)) for pi in range(npair)]

            def qk(gp):
                it, pi = jobs[gp]
                s, g = its[it]
                buf = gp % 2
                qmov = qT[:, g * 4:(g + 1) * 4, s * 128:(s + 1) * 128]

                def f(h, pi=pi, buf=buf, g=g, qmov=qmov):
                    for kk in range(2):
                        kt = 2 * pi + kk
                        i = h.matmul(ps[:, 2 * buf + kk, :], lhsT=KT[:, g, kt * 128:(kt + 1) * 128], rhs=qmov, start=True, stop=True)
                    return i
                A("pe", f, [("qT", s)] + KVL, [("ps", 2 * buf), ("ps", 2 * buf + 1)])
                A("act", lambda h, gp=gp, buf=buf: h.activation(out=PT[:, gp % 3, :], in_=ps[:, 2 * buf:2 * buf + 2, :].rearrange("p a b -> p (a b)"),
                                                                func=AF.Exp, bias=nbias[:, 0:1], scale=1.0),
                  [("ps", 2 * buf), ("ps", 2 * buf + 1)], [("PT", gp % 3)])

            def pv(gp):
                it, pi = jobs[gp]
                s, g = its[it]

                def f(h, pi=pi, g=g, gp=gp):
                    for kk in range(2):
                        kt = 2 * pi + kk
                        for hd in range(4):
                            i = h.matmul(ps[:, 4 + hd, 0:129], lhsT=PT[:, gp % 3, kk * 512 + hd * 128:kk * 512 + (hd + 1) * 128],
                                         rhs=Vx[:, kt, g, 0:129], start=(kt == 0), stop=(kt == NKT - 1))
                    return i
                A("pe", f, [("PT", gp % 3)], [("ps", 4 + hd) for hd in range(4)])

            OB = [("ps", 4 + hd) for hd in range(4)]

            def evac(it):
                s, g = its[it]
                A("dve", lambda h: h.reciprocal(out=rden[:], in_=ps[:, 4:8, 128:129]), OB, ["rden"])
                A("dve", lambda h: h.tensor_tensor(out=on[:], in0=ps[:, 4:8, 0:128], in1=rden[:].to_broadcast([128, 4, 128]), op=ALU.mult),
                  OB + ["rden"], ["on"])

                def tro(h):
                    for hd in range(4):
                        i = h.transpose(out=ps[:, 4, :].bitcast(BF16)[:, 512 + hd * 128:512 + (hd + 1) * 128], in_=on[:, hd, :], identity=ident[:])
                    return i
                A("pe", tro, ["on"], [("ps", 4)])
                A("dve", lambda h, s=s, g=g: h.tensor_copy(out=OT[:, g * 4:(g + 1) * 4, s * 128:(s + 1) * 128],
                                                           in_=ps[:, 4, :].bitcast(BF16)[:, 512:1024].rearrange("p (h t) -> p h t", h=4)),
                  [("ps", 4)], BUFA[2:4])

            NJOB = len(jobs)
            qk(0)
            if NJOB > 1:
                qk(1)
            for gp in range(NJOB):
                if gp + 2 < NJOB:
                    qk(gp + 2)
                pv(gp)
                if jobs[gp][1] == npair - 1:
                    evac(jobs[gp][0])

            if L["cfg"].get("dbg") and m == 0:
                dbg = L["dbg_t"]
                A("pool", lambda h: h.dma_start(out=dbg["ycT"], in_=bA16[:, 0:4096]), BUFA, ["d1"], dma=True)
                A("pool", lambda h: h.dma_start(out=dbg["OT"], in_=bA16[:, 4096:8192]), BUFA, ["d2"], dma=True)
                A("pool", lambda h: h.dma_start(out=dbg["qT"], in_=qT[:].rearrange("p k t -> p (k t)")), [("qT", s_) for s_ in range(4)], ["d3"], dma=True)
                A("pool", lambda h: h.dma_start(out=dbg["KT"], in_=KT[:].rearrange("p g t -> p (g t)")), (), ["d4"], dma=True)
                A("pool", lambda h: h.dma_start(out=dbg["Vx"], in_=Vx[:].rearrange("p a g d -> p (a g d)")), (), ["d5"], dma=True)
            for jj in range(4):
                sg_, kg_ = ring.load(P, L["S_G"][jj], ext=cext("g"))
                sr_, kr_ = ring.load(P, L["S_R"][jj], ext=cext("r"))
                sgv = sg_.rearrange("p (k c) -> p k c", k=8)
                srv = sr_.rearrange("p (k c) -> p k c", k=8)
                for t in range(2):
                    j = 2 * jj + t
                    base = 4 * (j % 2)

                    def mmg(h, sgv=sgv, srv=srv, t=t, base=base):
                        for c2 in range(2):
                            for k in range(8):
                                h.matmul(ps[:, base + c2, :], lhsT=sgv[:, k, (2 * t + c2) * 128:(2 * t + c2 + 1) * 128], rhs=h2T[:, k, :], start=(k == 0), stop=(k == 7))
                        for c2 in range(2):
                            src = ycT if c2 == 0 else OT
                            for k in range(8):
                                i = h.matmul(ps[:, base + 2 + c2, :], lhsT=srv[:, k, (2 * t + c2) * 128:(2 * t + c2 + 1) * 128], rhs=src[:, k, :], start=(k == 0), stop=(k == 7))
                        return i
                    A("pe", mmg, H2K + [kg_, kr_] + BUFA, [("ps", base + b_) for b_ in range(4)])
                    for c2 in range(2):
                        sl = 2 * (j % 2) + c2
                        A("act", lambda h, sl=sl, base=base, c2=c2: h.activation(out=sg[:, sl, :], in_=ps[:, base + c2, :], func=AF.Sigmoid), [("ps", base + c2)], [("sg", sl)])
                        A("dve", lambda h, sl=sl, base=base, c2=c2: h.tensor_tensor(out=tm[:, sl, :], in0=sg[:, sl, :], in1=ps[:, base + 2 + c2, :], op=ALU.mult),
                          [("sg", sl), ("ps", base + 2 + c2)], [("tm", sl)])
                    s0 = 2 * (j % 2)
                    A("pool", lambda h, j=j, s0=s0: h.tensor_tensor(out=mT[:, j, :], in0=tm[:, s0, :], in1=tm[:, s0 + 1, :], op=ALU.add),
                      [("tm", s0), ("tm", s0 + 1)], [("qT", s_) for s_ in range(4)])

            A("act", lambda h, m=m: h.dma_start(out=xres[:], in_=L["x1_s"][m * MT:(m + 1) * MT, :].rearrange("(s p) d -> p s d", p=128)),
              (), BUFA, dma=True)
            o0, ko0 = ring.load(P, L["S_O"][0], ext=cext("o"))
            o1, ko1 = ring.load(P, L["S_O"][1], ext=cext("o"))
            ov = [o0.rearrange("p (k c) -> p k c", k=8), o1.rearrange("p (k c) -> p k c", k=8)]
            for s in range(4):
                b0 = 2 * (s % 2)

                def mmo(h, s=s, b0=b0, ov=ov):
                    for k in range(8):
                        for hh in range(2):
                            i = h.matmul(ps[:, b0 + hh, :], lhsT=mT[:, k, s * 128:(s + 1) * 128], rhs=ov[hh][:, k, :], start=(k == 0), stop=(k == 7))
                    return i
                A("pe", mmo, [("qT", s_) for s_ in range(4)] + [ko0, ko1], [("ps", b0), ("ps", b0 + 1)])
                for hh in range(2):
                    A("dve", lambda h, b0=b0, hh=hh: h.tensor_tensor(out=tmp[:, hh, :], in0=ps[:, b0 + hh, :], in1=g5[:, hh * 512:(hh + 1) * 512], op=ALU.mult),
                      [("ps", b0 + hh), "gate"], [("tm", hh)])
                    A("pool", lambda h, s=s, hh=hh: h.tensor_tensor(out=xres[:, s, hh * 512:(hh + 1) * 512], in0=tmp[:, hh, :],
                                                                   in1=xres[:, s, hh * 512:(hh + 1) * 512], op=ALU.add),
                      [("tm", hh)] + BUFA, BUFA)
            A("pool", lambda h, m=m: h.dma_start(out=L["xm_s"][m * MT:(m + 1) * MT, :].rearrange("(s p) d -> p s d", p=128), in_=xres[:]),
              BUFA, [("xm_s", m)], dma=True)
        with nc.Block() as block:
            P.emit(block)


def build_phase3(L):
    nc, ps = L["nc"], L["ps"]
    NO, MO, MT = L["NO"], L["MO"], L["MT"]
    with ExitStack() as e3:
        def sb(name, shape, dt):
            return e3.enter_context(nc.sbuf_tensor("t_" + name, list(shape), dt))
        P = Prog(nc, L["es"], "p3")
        A = P.add
        ringt = sb("ring3", [128, L["NRING3"], 4096], BF16)
        ring = Ring(ringt, L["NRING3"])
        xres = sb("xres3", [128, 4, D], F32)
        xn = sb("xn3", [128, 4, D], BF16)
        junk = sb("junk3", [128, D], BF16)
        hT = sb("hT3", [128, 8, MT], BF16)
        actT = sb("actT3", [128, NJ, MT], BF16)
        sa = sb("sa3", [128, 2, MT], F32)
        tmp = sb("tmp3", [128, 2, MT], F32)
        ss = sb("ss3", [128, 4], F32)
        tt = sb("tt3", [128, 4], F32)
        rstd = sb("rstd3", [128, 4], F32)
        g8 = sb("g8", [128, D], F32)
        fgb = sb("fgb", [128, D], F32)
        yo = sb("yo", [128, 2, D], F32)
        A("sp", lambda h: h.dma_start(out=g8[:], in_=L["gbc_s"][3]), (), ["gate"], dma=True)
        A("sp", lambda h: h.dma_start(out=fgb[:], in_=L["fg_in"].partition_broadcast(128)), (), ["fgb"], dma=True)
        for m in range(MO):
            A("act", lambda h, m=m: h.dma_start(out=xres[:], in_=L["xm_s"][m * MT:(m + 1) * MT, :].rearrange("(s p) d -> p s d", p=128)),
              (), [("xres", s) for s in range(4)], dma=True)
            norm_to_hT(L, P, 4, xres, 2, 0, hT, "hT", xn, junk, ss, tt, rstd)
            ffn_core(L, P, 4, ring, L["S_3I"], L["S_3O"], L["cext"]("3i"), L["cext"]("3o"), hT, actT, sa, tmp, xres, g8[:])
            for s in range(4):
                A("act", lambda h, s=s: h.activation(out=xn[:, s, :], in_=xres[:, s, :], func=AF.Square, accum_out=ss[:, s:s + 1]),
                  [("xres", s)], [("ss", s), ("xn", s)])
            A("dve", lambda h: h.tensor_scalar(out=tt[:], in0=ss[:], scalar1=1.0 / D, scalar2=EPS, op0=ALU.mult, op1=ALU.add),
              [("ss", s) for s in range(4)], ["tt"])
            A("pool", lambda h: h.tensor_tensor(out=rstd[:], in0=tt[:], in1=L["mhalf"][:, 0:4], op=ALU.pow), ["tt"], ["rstd"])
            for s in range(4):
                A("dve", lambda h, s=s: h.scalar_tensor_tensor(out=yo[:, s % 2, :], in0=xres[:, s, :], scalar=rstd[:, s:s + 1], in1=fgb[:],
                                                               op0=ALU.mult, op1=ALU.mult), [("xres", s), "rstd", "fgb"], [("yo", s % 2)])
                A("pool", lambda h, s=s, m=m: h.dma_start(out=L["y_out"][m * MT + s * 128:m * MT + (s + 1) * 128, :], in_=yo[:, s % 2, :]),
                  [("yo", s % 2)], [("y", m, s)], dma=True)
        with nc.Block() as block:
            P.emit(block)


def norm_to_hT2(L, P, NT, xres, xk, n_i, kind, dst, dkey, W, pk, bank):
    A = P.add
    gsc, modc, mhalf, ident, ps = L["gsc"], L["modc"], L["mhalf"], L["ident"], L["ps"]
    xn, junk, ss, tt, rstd = W
    sh_m = (0, 3, 6)[n_i]
    for s in range(NT):
        A("act", lambda h, s=s: h.activation(out=junk[:, s % 2, :], in_=xres[:, s, :], func=AF.Square, accum_out=ss[:, s:s + 1]),
          [(xk, s)], [(pk + "ss", s), (pk + "junk", s % 2)])
        if s % 2 == 1:
            yield
    A("dve", lambda h: h.tensor_scalar(out=tt[:, 0:NT], in0=ss[:, 0:NT], scalar1=1.0 / D, scalar2=EPS, op0=ALU.mult, op1=ALU.add),
      [(pk + "ss", s) for s in range(NT)], [pk + "tt"])
    A("pool", lambda h: h.tensor_tensor(out=rstd[:, 0:NT], in0=tt[:, 0:NT], in1=mhalf[:, 0:NT], op=ALU.pow), [pk + "tt"], [pk + "rstd"])
    yield
    for s in range(NT):
        A("dve", lambda h, s=s: h.tensor_scalar(out=xn[:, s % 2, :], in0=xres[:, s, :], scalar1=rstd[:, s:s + 1], scalar2=None, op0=ALU.mult),
          [(xk, s), pk + "rstd"], [(pk + "xn", s % 2)])
        yield

        def tr(h, s=s):
            for k in range(8):
                i = h.transpose(out=ps[:, bank, :].bitcast(BF16)[:, k * 128:(k + 1) * 128], in_=xn[:, s % 2, k * 128:(k + 1) * 128], identity=ident[:])
            return i
        A("pe", tr, [(pk + "xn", s % 2)], [("ps", bank)])
        yield
        for k in range(8):
            src = ps[:, bank, :].bitcast(BF16)[:, k * 128:(k + 1) * 128]
            o = dst[:, k, s * 128:(s + 1) * 128]
            A("act", lambda h, k=k, o=o, src=src: h.activation(out=o, in_=src, func=AF.Identity, bias=modc[:, sh_m, k, kind:kind + 1],
                                                               scale=gsc[:, n_i, k, kind:kind + 1]), [("ps", bank)], [(dkey, k)])
            if k == 3:
                yield


def norm_work(nc, stack, pfx):
    def sb(name, shape, dt):
        return stack.enter_context(nc.sbuf_tensor("t_" + pfx + name, list(shape), dt))
    return (sb("xn", [128, 2, D], BF16), sb("junk", [128, 2, D], BF16), sb("ss", [128, 4], F32), sb("tt", [128, 4], F32), sb("rstd", [128, 4], F32))


def drain(gens):
    for g in gens:
        for _ in g:
            pass


def ffn_B(L, P, T, ring, S_I, ext_i, hT, hkey, actT, sa, sched):
    A = P.add
    ps = L["ps"]
    gens = []
    for jj in range(11):
        slot, skey = ring.load(P, S_I[jj], ext=ext_i)
        sv = slot.rearrange("p (k c) -> p k c", k=8)
        for t in range(2):
            q = 2 * jj + t
            gens.extend(sched.get(q, []))
            for g in list(gens):
                try:
                    next(g)
                except StopIteration:
                    gens.remove(g)
            st = q % 2
            ba, bb = 2 * st, 2 * st + 1

            def mm(h, t=t, sv=sv, ba=ba, bb=bb):
                for k in range(8):
                    h.matmul(ps[:, ba, 0:T], lhsT=sv[:, k, t * 128:(t + 1) * 128], rhs=hT[:, k, 0:T], start=(k == 0), stop=(k == 7))
                for k in range(8):
                    i = h.matmul(ps[:, bb, 0:T], lhsT=sv[:, k, (2 + t) * 128:(3 + t) * 128], rhs=hT[:, k, 0:T], start=(k == 0), stop=(k == 7))
                return i
            A("pe", mm, [skey] + [(hkey, k) for k in range(8)], [("ps", ba), ("ps", bb)])
            A("act", lambda h, q=q, ba=ba: h.activation(out=sa[:, q % 2, 0:T], in_=ps[:, ba, 0:T], func=AF.Silu), [("ps", ba)], [("sa", q % 2)])
            A("dve", lambda h, q=q, bb=bb: h.tensor_tensor(out=actT[:, q, 0:T], in0=sa[:, q % 2, 0:T], in1=ps[:, bb, 0:T], op=ALU.mult),
              [("sa", q % 2), ("ps", bb)], [("actT", q)])
    drain(gens)


def ffn_C(L, P, NT, ring, S_O, ext_o, actT):
    A = P.add
    ps = L["ps"]
    for c in range(6):
        nj = 4 if c < 5 else 2
        slot, skey = ring.load(P, S_O[c][:, 0:nj * 1024], ncols=nj * 1024, ext=ext_o)
        sv = slot.rearrange("p (j d) -> p j d", j=4)
        for jj in range(nj):
            j = 4 * c + jj

            def mm(h, j=j, jj=jj, sv=sv):
                for s in range(NT):
                    for hh in range(2):
                        i = h.matmul(ps[:, 2 * s + hh, :], lhsT=actT[:, j, s * 128:(s + 1) * 128], rhs=sv[:, jj, hh * 512:(hh + 1) * 512],
                                     start=(j == 0), stop=(j == NJ - 1))
                return i
            A("pe", mm, [skey, ("actT", j)], [("ps", b) for b in range(2 * NT)])


def ffn_evac(L, P, NT, tmp, xres, xk, gate_bc):
    A = P.add
    ps = L["ps"]
    for s in range(NT):
        for hh in range(2):
            b = 2 * s + hh
            A("dve", lambda h, b=b, hh=hh: h.tensor_tensor(out=tmp[:, b % 2, :], in0=ps[:, b, :], in1=gate_bc[:, hh * 512:(hh + 1) * 512], op=ALU.mult),
              [("ps", b), "gate"], [("tmp", b % 2)])
            A("pool", lambda h, b=b, s=s, hh=hh: h.tensor_tensor(out=xres[:, s, hh * 512:(hh + 1) * 512], in0=tmp[:, b % 2, :],
                                                                 in1=xres[:, s, hh * 512:(hh + 1) * 512], op=ALU.add),
              [("tmp", b % 2), (xk, s)], [(xk, s)])


def build_phase1p(L):
    nc, ps = L["nc"], L["ps"]
    NO, NH, NCX, MO, MH, MT, NKT = L["NO"], L["NH"], L["NCX"], L["MO"], L["MH"], L["MT"], L["NKT"]
    ident, halo_all, meta = L["ident"], L["halo_all"], L["meta"]
    kt_s, vx_s = L["kt_s"], L["vx_s"]
    with ExitStack() as e1:
        def sb(name, shape, dt):
            return e1.enter_context(nc.sbuf_tensor("t_" + name, list(shape), dt))
        P = Prog(nc, L["es"], "p1")
        A = P.add
        NR = L["NRING1"]
        ringt = sb("ring1", [128, NR, 4096], BF16)
        ring = Ring(ringt, NR)
        xresb = [sb("xres_a", [128, 4, D], F32), sb("xres_b", [128, 4, D], F32), sb("xres_c", [128, 4, D], F32)]
        hTb = [sb("hT_a", [128, 8, MT], BF16), sb("hT_b", [128, 8, MT], BF16)]
        h2T = sb("h2T1", [128, 8, MT], BF16)
        actT = sb("actT", [128, NJ, MT], BF16)
        sa = sb("sa", [128, 2, MT], F32)
        tmp = sb("tmp", [128, 2, MT], F32)
        WH = norm_work(nc, e1, "nh_")
        WT = norm_work(nc, e1, "nt_")
        wkv = sb("wkv", [128, 8, 512], BF16)
        g2 = sb("g2", [128, 2, D], F32)
        tabt = sb("tabt", [128, 4, 128], F32)
        kb = sb("kb", [128, 2, 128], BF16)
        KTs = sb("KTs", [128, 2, 2, MT], BF16)
        Vs = sb("Vs", [128, 2, 4, 2, 130], BF16)
        W = hn_work(nc, e1, 2, "k_")

        A("sp", lambda h: h.dma_start(out=g2[:], in_=L["gbc_s"][0:2].rearrange("a p d -> p a d")), (), ["gate"], dma=True)
        A("sp", lambda h: h.dma_start(out=wkv[:].rearrange("p k c -> p (k c)"), in_=L["S_KV"][0]), (), ["wkv"], dma=True, ext=L["cext"]("kv"))
        A("pool", lambda h: h.memset(Vs[:, :, :, :, 128:130], 1.0), (), [("Vs", 0), ("Vs", 1)])

        tiles = [("own", m, 4) for m in range(MO)] + [("oth", m, 4) for m in range(MH)] + [("ctx", 0, NCX // 128)]
        NTI = len(tiles)
        late = list(L["late_casts"])
        per_tile = (len(late) + max(1, NTI - 2) - 1) // max(1, NTI - 2)

        def tile_src(i):
            kind, m, NT = tiles[i]
            if kind == "ctx":
                return L["ctx"], 0, None
            if kind == "own":
                return L["x_own"][m * MT:(m + 1) * MT, :], (NCX + m * MT) // 128, m * 4
            return L["x_oth"][m * MT:(m + 1) * MT, :], (NCX + NO + m * MT) // 128, (NO + m * MT) // 128

        def Hload(i):
            kind, m, NT = tiles[i]
            src, kt0, sub0 = tile_src(i)
            xr = xresb[i % 3]
            A("act", lambda h, src=src, NT=NT, xr=xr: h.dma_start(out=xr[:, 0:NT, :], in_=src.rearrange("(s p) d -> p s d", p=128)),
              (), [(("xres", i % 3), s) for s in range(NT)], dma=True)

        def Hnorm(i):
            kind, m, NT = tiles[i]
            kd = 1 if kind == "ctx" else 0
            yield from norm_to_hT2(L, P, NT, xresb[i % 3], ("xres", i % 3), 0, kd, hTb[i % 2], ("hT", i % 2), WH, "nh_", 4)

        def Trest(i):
            kind, m, NT = tiles[i]
            kd = 1 if kind == "ctx" else 0
            src, kt0, sub0 = tile_src(i)
            xr = xresb[i % 3]
            xk = ("xres", i % 3)
            T = NT * 128
            if sub0 is not None:
                A("sp", lambda h, sub0=sub0: h.dma_start(out=tabt[:], in_=L["rope_s"][sub0:sub0 + 4].rearrange("s p c -> p s c")),
                  (), ["tab"], dma=True)
            yield from norm_to_hT2(L, P, NT, xr, xk, 1, kd, h2T, "h2T", WT, "nt_", 5)
            H2 = [("h2T", k) for k in range(8)]
            if kind == "own":
                A("pool", lambda h, m=m, xr=xr: h.dma_start(out=L["x1_s"][m * MT:(m + 1) * MT, :].rearrange("(s p) d -> p s d", p=128), in_=xr[:]),
                  [(xk, s) for s in range(4)], [("x1_s", m)], dma=True)
                A("pool", lambda h, m=m: h.dma_start(out=L["h2T_s"][m], in_=h2T[:].rearrange("p k t -> p (k t)")), H2, [("h2T_s", m)], dma=True)
            if kind != "ctx":
                hi = m if kind == "own" else MO + m
                A("pool", lambda h, hi=hi: h.tensor_copy(out=halo_all[:, hi, :, 0], in_=h2T[:, :, 0]), H2, ["halo"])
                A("pool", lambda h, hi=hi: h.tensor_copy(out=halo_all[:, hi, :, 1], in_=h2T[:, :, MT - 1]), H2, ["halo"])
            sbuf = i % 2
            yield
            for s in range(NT):
                bank = 6
                tbank = 7

                def mmkv(h, s=s, bank=bank):
                    for k in range(8):
                        i_ = h.matmul(ps[:, bank, :], lhsT=h2T[:, k, s * 128:(s + 1) * 128], rhs=wkv[:, k, :], start=(k == 0), stop=(k == 7))
                    return i_
                A("pe", mmkv, H2 + ["wkv"], [("ps", bank)])
                yield
                A("act", lambda h, bank=bank, s=s, sbuf=sbuf: h.activation(out=Vs[:, sbuf, s, :, 0:128], in_=ps[:, bank, 256:512].rearrange("p (g d) -> p g d", g=2),
                                                                          func=AF.Copy), [("ps", bank)], [("Vs", sbuf)])
                headnorm_rope(L, P, ps[:, bank, 0:256].rearrange("p (g d) -> p g d", g=2), [("ps", bank)], 2, L["kgb"],
                              None if kind == "ctx" else tabt[:, s, :], kb[:], ["kb"], W)

                yield
                yield

                def trk(h, tbank=tbank):
                    for g in range(2):
                        i_ = h.transpose(out=ps[:, tbank, :].bitcast(BF16)[:, g * 128:(g + 1) * 128], in_=kb[:, g, :], identity=ident[:])
                    return i_
                A("pe", trk, ["kb"], [("ps", tbank)])
                A("dve", lambda h, s=s, sbuf=sbuf, tbank=tbank: h.tensor_copy(out=KTs[:, sbuf, :, s * 128:(s + 1) * 128],
                                                                              in_=ps[:, tbank, :].bitcast(BF16)[:, 0:256].rearrange("p (g t) -> p g t", g=2)),
                  [("ps", tbank)], [("KTs", sbuf)])
            A("pool", lambda h, sbuf=sbuf, kt0=kt0, T=T: h.dma_start(out=kt_s[:, :, kt0 * 128:kt0 * 128 + T], in_=KTs[:, sbuf, :, 0:T]),
              [("KTs", sbuf)], [("kt_s", i)], dma=True)
            A("pool", lambda h, sbuf=sbuf, kt0=kt0, NT=NT: h.dma_start(out=vx_s[:, kt0:kt0 + NT, :, :], in_=Vs[:, sbuf, 0:NT, :, :]),
              [("Vs", sbuf)], [("vx_s", i)], dma=True)

        Hload(0)
        drain([Hnorm(0)])
        for i, (kind, m, NT) in enumerate(tiles):
            for _ in range(per_tile if i < NTI - 1 else len(late)):
                if late:
                    P.add_raw("pool", late.pop(0))
            kd = 1 if kind == "ctx" else 0
            sched = {}
            if i >= 1:
                sched[1] = [Trest(i - 1)]
            if i + 1 < NTI:
                Hload(i + 1)
                sched[5] = [Hnorm(i + 1)]
            ffn_B(L, P, NT * 128, ring, L["S_1I"], L["cext"]("1i"), hTb[i % 2], ("hT", i % 2), actT, sa, sched)
            ffn_C(L, P, NT, ring, L["S_1O"], L["cext"]("1o"), actT)
            ffn_evac(L, P, NT, tmp, xresb[i % 3], ("xres", i % 3), g2[:, kd, :])
        drain([Trest(NTI - 1)])
        with nc.Block() as block:
            P.emit(block)


def build_phase3p(L):
    nc, ps = L["nc"], L["ps"]
    NO, MO, MT = L["NO"], L["MO"], L["MT"]
    with ExitStack() as e3:
        def sb(name, shape, dt):
            return e3.enter_context(nc.sbuf_tensor("t_" + name, list(shape), dt))
        P = Prog(nc, L["es"], "p3")
        A = P.add
        NR = L["NRING3"]
        ringt = sb("ring3", [128, NR, 4096], BF16)
        ring = Ring(ringt, NR)
        xresb = [sb("xres3a", [128, 4, D], F32), sb("xres3b", [128, 4, D], F32), sb("xres3c", [128, 4, D], F32)]
        hTb = [sb("hT3a", [128, 8, MT], BF16), sb("hT3b", [128, 8, MT], BF16)]
        actT = sb("actT3", [128, NJ, MT], BF16)
        sa = sb("sa3", [128, 2, MT], F32)
        tmp = sb("tmp3", [128, 2, MT], F32)
        WH = norm_work(nc, e3, "n3_")
        junk = sb("fjunk", [128, 2, D], BF16)
        ss = sb("fss", [128, 4], F32)
        tt = sb("ftt", [128, 4], F32)
        rstd = sb("frstd", [128, 4], F32)
        g8 = sb("g8", [128, D], F32)
        fgb = sb("fgb", [128, D], F32)
        yo = sb("yo", [128, 2, D], F32)
        A("sp", lambda h: h.dma_start(out=g8[:], in_=L["gbc_s"][3]), (), ["gate"], dma=True)
        A("sp", lambda h: h.dma_start(out=fgb[:], in_=L["fg_in"].partition_broadcast(128)), (), ["fgb"], dma=True)

        def Hload(m):
            xr = xresb[m % 3]
            A("act", lambda h, m=m, xr=xr: h.dma_start(out=xr[:], in_=L["xm_s"][m * MT:(m + 1) * MT, :].rearrange("(s p) d -> p s d", p=128)),
              (), [(("xres", m % 3), s) for s in range(4)], dma=True)

        def Hnorm(m):
            yield from norm_to_hT2(L, P, 4, xresb[m % 3], ("xres", m % 3), 2, 0, hTb[m % 2], ("hT", m % 2), WH, "n3_", 4)

        def Trest(m):
            xr = xresb[m % 3]
            xk = ("xres", m % 3)
            for s in range(4):
                A("act", lambda h, s=s, xr=xr: h.activation(out=junk[:, s % 2, :], in_=xr[:, s, :], func=AF.Square, accum_out=ss[:, s:s + 1]),
                  [(xk, s)], [("fss", s), ("fjunk", s % 2)])
                if s % 2 == 1:
                    yield
            A("dve", lambda h: h.tensor_scalar(out=tt[:], in0=ss[:], scalar1=1.0 / D, scalar2=EPS, op0=ALU.mult, op1=ALU.add),
              [("fss", s) for s in range(4)], ["ftt"])
            A("pool", lambda h: h.tensor_tensor(out=rstd[:], in0=tt[:], in1=L["mhalf"][:, 0:4], op=ALU.pow), ["ftt"], ["frstd"])
            for s in range(4):
                A("dve", lambda h, s=s, xr=xr: h.scalar_tensor_tensor(out=yo[:, s % 2, :], in0=xr[:, s, :], scalar=rstd[:, s:s + 1], in1=fgb[:],
                                                                      op0=ALU.mult, op1=ALU.mult), [(xk, s), "frstd", "fgb"], [("yo", s % 2)])
                A("pool", lambda h, s=s, m=m: h.dma_start(out=L["y_out"][m * MT + s * 128:m * MT + (s + 1) * 128, :], in_=yo[:, s % 2, :]),
                  [("yo", s % 2)], [("y", m, s)], dma=True)
                yield

        Hload(0)
        drain([Hnorm(0)])
        for m in range(MO):
            sched = {}
            if m >= 1:
                sched[1] = [Trest(m - 1)]
            if m + 1 < MO:
                Hload(m + 1)
                sched[5] = [Hnorm(m + 1)]
            ffn_B(L, P, MT, ring, L["S_3I"], L["cext"]("3i"), hTb[m % 2], ("hT", m % 2), actT, sa, sched)
            ffn_C(L, P, 4, ring, L["S_3O"], L["cext"]("3o"), actT)
            ffn_evac(L, P, 4, tmp, xresb[m % 3], ("xres", m % 3), g8[:])
        drain([Trest(MO - 1)])
        with nc.Block() as block:
            P.emit(block)

def host_inputs(b, hh, NO, x, c, ctx, c_ctx, w_mod, b_mod, norm1_g, norm2_g, norm3_g, ffn1_w_in, ffn1_w_out, w_in, conv_w,
                q_norm_g, k_norm_g, w_branch_conv, w_branch_attn, w_out, ffn2_w_in, ffn2_w_out, final_g):
    f = np.float32
    own = slice(hh * NO, (hh + 1) * NO)
    oth = slice((1 - hh) * NO, (2 - hh) * NO)
    cT = np.stack([c[b].reshape(8, 128).T, c_ctx.reshape(8, 128).T], axis=-1).reshape(128, 16)
    gT = np.stack([g[0].reshape(8, 128).T for g in (norm1_g, norm2_g, norm3_g)], axis=1).reshape(128, 24)
    convT = np.stack([conv_w[0, j].reshape(8, 128).T for j in range(3)], axis=1).reshape(128, 24)
    p = np.arange(128)
    meta = np.zeros((128, 8), f)
    meta[:, 0] = hh * NO // 64 + (p >> 6)
    meta[:, 1] = (1 - hh) * NO // 64 + (p >> 6)
    meta[:, 2] = p & 63
    meta[:, 3] = 0.0 if hh == 0 else 1.0
    meta[:, 4] = 1.0 if hh == 0 else 0.0
    return {
        "x_own": np.ascontiguousarray(x[b, own]), "x_oth": np.ascontiguousarray(x[b, oth]), "ctx": np.ascontiguousarray(ctx[b]),
        "cT": np.ascontiguousarray(cT, f), "bmodT": np.ascontiguousarray(b_mod[0].reshape(72, 128).T, f), "bmod": np.ascontiguousarray(b_mod[0], f),
        "gT": np.ascontiguousarray(gT, f), "convT": np.ascontiguousarray(convT, f), "final_g": np.ascontiguousarray(final_g, f),
        "q_norm_g": np.ascontiguousarray(q_norm_g[0], f), "k_norm_g": np.ascontiguousarray(k_norm_g[0], f), "meta": meta,
        "w_mod": w_mod[0], "ffn1_w_in": ffn1_w_in[0], "ffn1_w_out": ffn1_w_out[0], "w_in": w_in[0],
        "w_branch_conv": w_branch_conv[0], "w_branch_attn": w_branch_attn[0], "w_out": w_out[0],
        "ffn2_w_in": ffn2_w_in[0], "ffn2_w_out": ffn2_w_out[0],
    }


def kernel(**inputs):
    inputs = {k: np.asarray(v) for k, v in inputs.items()}
    x = inputs["x"]
    B, S, _ = x.shape
    NO = S // 2
    nc = build({"NO": NO, "NH": NO, "NCX": inputs["ctx"].shape[1]})
    in_maps = []
    for core in range(2 * B):
        in_maps.append(host_inputs(core // 2, core % 2, NO, **inputs))
    res = run_bass_kernel_spmd(nc, in_maps, core_ids=list(range(2 * B)))
    out = np.empty((B, S, D), np.float32)
    for core in range(2 * B):
        b, hh = core // 2, core % 2
        out[b, hh * NO:(hh + 1) * NO] = np.asarray(res.results[core]["y"], dtype=np.float32)
    return out
```
